# Optimizing a Trainium2 kernel written in Bass

```python
import math
import jax, jax.numpy as jnp
from jax import lax
import numpy as np

D_MODEL = 2048
BATCH = 16
SEQ = 2048
DEPTH = 1
DEC_BATCH = 32
DEC_SEQ = 16
PAST_LEN = 2048

CHUNK = 64
N_PAST_CHUNKS = 8
BAND_PAST = CHUNK * N_PAST_CHUNKS
ATT_WIDTH = D_MODEL // 2
HEAD_DIM_ATT = 128
N_HEADS_ATT = ATT_WIDTH // HEAD_DIM_ATT
MAX_REL = 128
GLA_V_WIDTH = D_MODEL - ATT_WIDTH
N_HEADS_GLA = 4
DV_GLA = GLA_V_WIDTH // N_HEADS_GLA
DK_GLA = DV_GLA // 2
GLA_QK_WIDTH = N_HEADS_GLA * DK_GLA
GATE_RANK = 16
GATE_TAU = 16.0
GLA_BLOCK = 16
IN_SIZES = (ATT_WIDTH, ATT_WIDTH, ATT_WIDTH, GLA_QK_WIDTH, GLA_QK_WIDTH, GLA_V_WIDTH, GLA_V_WIDTH, GATE_RANK)
IN_WIDTH = 3 * ATT_WIDTH + 2 * GLA_QK_WIDTH + 2 * GLA_V_WIDTH + GATE_RANK
MIX_WIDTH = ATT_WIDTH + GLA_V_WIDTH
D_FF = 5632
CONV_W = 3
EPS = 1e-6

kernel_name = 'hybrid_streaming_encoder_step'


def rms_norm(x, g):
    xf = x.astype(jnp.float32)
    y = xf * lax.rsqrt(jnp.mean(xf * xf, axis=-1, keepdims=True) + EPS)
    return (y * g.astype(jnp.float32)).astype(x.dtype)


def split_in_proj(p):
    offs = []
    acc = 0
    for s in IN_SIZES[:-1]:
        acc += s
        offs.append(acc)
    return jnp.split(p, offs, axis=-1)


def band_attention(q, k, v, q_pos, k_pos, rel_bias):
    rel = jnp.clip(q_pos[:, :, None] - k_pos[:, None, :], -MAX_REL, MAX_REL) + MAX_REL
    bias = jnp.transpose(rel_bias[:, rel], (1, 0, 2, 3)).astype(jnp.float32)
    s = jnp.einsum('bnqhd,bnkhd->bnhqk', q.astype(jnp.float32), k.astype(jnp.float32)) * (HEAD_DIM_ATT ** -0.5) + bias
    s = jnp.where((k_pos >= 0)[None, :, None, None, :], s, -1e30)
    p = jax.nn.softmax(s, axis=-1)
    o = jnp.einsum('bnhqk,bnkhd->bnqhd', p, v.astype(jnp.float32))
    return o.astype(q.dtype)


def prompt_attention(q, k, v, rel_bias):
    B, T, H, Dh = q.shape
    nc = T // CHUNK
    nb = N_PAST_CHUNKS + 1
    band_idx = jnp.arange(nc)[:, None] + jnp.arange(nb)[None, :]

    def gather_band(t):
        tc = jnp.pad(t.reshape(B, nc, CHUNK, H, Dh), ((0, 0), (N_PAST_CHUNKS, 0), (0, 0), (0, 0), (0, 0)))
        return tc[:, band_idx].reshape(B, nc, nb * CHUNK, H, Dh)

    q_pos = jnp.arange(T).reshape(nc, CHUNK)
    k_pos = ((band_idx - N_PAST_CHUNKS)[:, :, None] * CHUNK + jnp.arange(CHUNK)[None, None, :]).reshape(nc, nb * CHUNK)
    o = band_attention(q.reshape(B, nc, CHUNK, H, Dh), gather_band(k), gather_band(v), q_pos, k_pos, rel_bias)
    return o.reshape(B, T, H, Dh)


def sample_attention(q, k, v, cache_k, cache_v, rel_bias):
    B, T, H, Dh = q.shape
    L = cache_k.shape[1]
    kb = jnp.concatenate([cache_k.astype(k.dtype), k], axis=1)[:, None]
    vb = jnp.concatenate([cache_v.astype(v.dtype), v], axis=1)[:, None]
    q_pos = (PAST_LEN + jnp.arange(T))[None]
    k_pos = (PAST_LEN - L + jnp.arange(L + T))[None]
    o = band_attention(q[:, None], kb, vb, q_pos, k_pos, rel_bias)
    return o[:, 0]


def gla(q, k, v, log_a, state0):
    B, T, H, Dk = q.shape
    Dv = v.shape[-1]
    nblk = -(-T // GLA_BLOCK)
    pad = nblk * GLA_BLOCK - T

    def blocks(t):
        t = jnp.pad(t.astype(jnp.float32), ((0, 0), (0, pad), (0, 0), (0, 0)))
        return t.reshape(B, nblk, GLA_BLOCK, H, -1).transpose(1, 0, 3, 2, 4)

    qb, kb, vb = blocks(q), blocks(k), blocks(v)
    cum = jnp.cumsum(blocks(log_a), axis=3)
    causal = jnp.tril(jnp.ones((GLA_BLOCK, GLA_BLOCK), dtype=bool))

    def step(s, blk):
        qi, ki, vi, ci = blk
        c_last = ci[:, :, -1:, :]
        q_t = qi * jnp.exp(ci)
        k_t = ki * jnp.exp(-ci)
        a = jnp.where(causal, jnp.einsum('bhqd,bhkd->bhqk', q_t, k_t), 0.0)
        o = jnp.einsum('bhqk,bhkv->bhqv', a, vi) + jnp.einsum('bhqd,bhdv->bhqv', q_t, s)
        s = jnp.exp(c_last[:, :, 0, :, None]) * s + jnp.einsum('bhkd,bhkv->bhdv', ki * jnp.exp(c_last - ci), vi)
        return s, o

    s_final, o = lax.scan(step, state0.astype(jnp.float32), (qb, kb, vb, cum))
    o = o.transpose(1, 0, 3, 2, 4).reshape(B, nblk * GLA_BLOCK, H, Dv)[:, :T]
    return o, s_final


def conv_ffn(x, conv_prev, g_pre, w_up, w_conv, b_conv, w_down, g_post):
    T = x.shape[1]
    h = rms_norm(x, g_pre)
    gate, val = jnp.split(h @ w_up, 2, axis=-1)
    ext = jnp.concatenate([conv_prev.astype(gate.dtype), gate], axis=1)
    conv = b_conv + sum(ext[:, i:i + T] * w_conv[i] for i in range(CONV_W))
    y = (jax.nn.silu(conv) * val) @ w_down
    return rms_norm(y, g_post), ext[:, T:]


def encoder_layer(x, att_fn, gla_state0, conv_prev, g_mix_pre, w_in, w_gate_up, b_gate, g_gla, w_o,
                  g_mix_post, g_ffn_pre, w_up, w_conv, b_conv, w_down, g_ffn_post):
    B, T, _ = x.shape
    h = rms_norm(x, g_mix_pre)
    q_a, k_a, v_a, q_g, k_g, v_g, r_g, a_lo = split_in_proj(h @ w_in)
    heads = lambda t, n: t.reshape(B, T, n, -1)
    q_a, k_a, v_a = heads(q_a, N_HEADS_ATT), heads(k_a, N_HEADS_ATT), heads(v_a, N_HEADS_ATT)
    o_att = att_fn(q_a, k_a, v_a).reshape(B, T, ATT_WIDTH)
    log_a = jax.nn.log_sigmoid((a_lo @ w_gate_up + b_gate).astype(jnp.float32)) / GATE_TAU
    o_gla, s_gla = gla(heads(q_g, N_HEADS_GLA) * (DK_GLA ** -0.5), heads(k_g, N_HEADS_GLA),
                       heads(v_g, N_HEADS_GLA), heads(log_a, N_HEADS_GLA), gla_state0)
    o_gla = rms_norm(o_gla, g_gla).reshape(B, T, GLA_V_WIDTH).astype(x.dtype) * jax.nn.silu(r_g)
    mix = jnp.concatenate([o_att, o_gla], axis=-1) @ w_o
    x = x + rms_norm(mix, g_mix_post)
    f, conv_state = conv_ffn(x, conv_prev, g_ffn_pre, w_up, w_conv, b_conv, w_down, g_ffn_post)
    return x + f, k_a, v_a, s_gla, conv_state


def setup_inputs(seed: int = 0) -> dict:
    key = jax.random.key(seed)
    ks = jax.random.split(key, 24)
    nrm = lambda k, shape, scale: jax.random.normal(k, shape, jnp.float32) * scale
    gain = lambda k, n: 1.0 + 0.05 * jax.random.normal(k, (DEPTH, n), jnp.float32)
    att_cache = min(BAND_PAST, PAST_LEN)
    return {
        'x_prompt': nrm(ks[0], (BATCH, SEQ, D_MODEL), 1.0),
        'x_sample': nrm(ks[1], (DEC_BATCH, DEC_SEQ, D_MODEL), 1.0),
        'cache_k': nrm(ks[2], (DEPTH, DEC_BATCH, att_cache, N_HEADS_ATT, HEAD_DIM_ATT), 1.0),
        'cache_v': nrm(ks[3], (DEPTH, DEC_BATCH, att_cache, N_HEADS_ATT, HEAD_DIM_ATT), 1.0),
        'state_gla': nrm(ks[4], (DEPTH, DEC_BATCH, N_HEADS_GLA, DK_GLA, DV_GLA), 1.0),
        'state_conv': nrm(ks[5], (DEPTH, DEC_BATCH, CONV_W - 1, D_FF), 1.0),
        'g_mix_pre': gain(ks[6], D_MODEL),
        'w_in': nrm(ks[7], (DEPTH, D_MODEL, IN_WIDTH), D_MODEL ** -0.5),
        'w_gate_up': nrm(ks[8], (DEPTH, GATE_RANK, GLA_QK_WIDTH), GATE_RANK ** -0.5),
        'b_gate': nrm(ks[9], (DEPTH, GLA_QK_WIDTH), 0.1),
        'rel_bias': nrm(ks[10], (DEPTH, N_HEADS_ATT, 2 * MAX_REL + 1), 0.1),
        'g_gla': gain(ks[11], DV_GLA),
        'w_o': nrm(ks[12], (DEPTH, MIX_WIDTH, D_MODEL), MIX_WIDTH ** -0.5),
        'g_mix_post': gain(ks[13], D_MODEL),
        'g_ffn_pre': gain(ks[14], D_MODEL),
        'w_up': nrm(ks[15], (DEPTH, D_MODEL, 2 * D_FF), D_MODEL ** -0.5),
        'w_conv': nrm(ks[16], (DEPTH, CONV_W, D_FF), CONV_W ** -0.5),
        'b_conv': nrm(ks[17], (DEPTH, D_FF), 0.01),
        'w_down': nrm(ks[18], (DEPTH, D_FF, D_MODEL), D_FF ** -0.5),
        'g_ffn_post': gain(ks[19], D_MODEL),
    }


def reference(x_prompt, x_sample, cache_k, cache_v, state_gla, state_conv, g_mix_pre, w_in, w_gate_up,
              b_gate, rel_bias, g_gla, w_o, g_mix_post, g_ffn_pre, w_up, w_conv, b_conv, w_down, g_ffn_post):
    xp, xs = x_prompt, x_sample
    kp_l, vp_l, gp_l, cp_l, ks_l, vs_l, gs_l, cs_l = [], [], [], [], [], [], [], []
    for l in range(DEPTH):
        shared = (g_mix_pre[l], w_in[l], w_gate_up[l], b_gate[l], g_gla[l], w_o[l], g_mix_post[l],
                  g_ffn_pre[l], w_up[l], w_conv[l], b_conv[l], w_down[l], g_ffn_post[l])
        rb = rel_bias[l]
        bp = xp.shape[0]
        xp, kp, vp, gp, cp = encoder_layer(
            xp, lambda q, k, v, rb=rb: prompt_attention(q, k, v, rb),
            jnp.zeros((bp, N_HEADS_GLA, DK_GLA, DV_GLA), jnp.float32),
            jnp.zeros((bp, CONV_W - 1, D_FF), xp.dtype), *shared)
        keep = min(BAND_PAST, xp.shape[1])
        kp_l.append(kp[:, -keep:]); vp_l.append(vp[:, -keep:]); gp_l.append(gp); cp_l.append(cp)
        ck, cv = cache_k[l], cache_v[l]
        xs, kn, vn, gn, cn = encoder_layer(
            xs, lambda q, k, v, ck=ck, cv=cv, rb=rb: sample_attention(q, k, v, ck, cv, rb),
            state_gla[l], state_conv[l], *shared)
        ks_l.append(kn); vs_l.append(vn); gs_l.append(gn); cs_l.append(cn)
    return (xp, xs, jnp.stack(kp_l), jnp.stack(vp_l), jnp.stack(gp_l), jnp.stack(cp_l),
            jnp.stack(ks_l), jnp.stack(vs_l), jnp.stack(gs_l), jnp.stack(cs_l))
```

```python
import numpy as np
from contextlib import ExitStack
import concourse.bass as bass
import concourse.mybir as mybir
from concourse.bass_utils import run_bass_kernel_spmd

F32 = mybir.dt.float32
BF16 = mybir.dt.bfloat16
AF = mybir.ActivationFunctionType
ALU = mybir.AluOpType
AX = mybir.AxisListType

NCORES = 8
D = 2048
NKC = 16
SEQ = 2048
TT = 256
NTILE = SEQ // TT
DFF = 5632
NFC = 44
INW = 6160
EPS = 1e-6
MASKV = -30000.0
ATT_SCALE = 128.0 ** -0.5
GLA_QSCALE = 128.0 ** -0.5
NSLOT = 3
WITH_SAMPLE = True


class Buf:
    __slots__ = ("name", "wr", "rd", "excl")

    def __init__(self, name, excl=False):
        self.name = name
        self.wr = {}
        self.rd = {}
        self.excl = excl


class Eng:
    def __init__(self, name, h, sem):
        self.name = name
        self.h = h
        self.sem = sem
        self.cnt = 0
        self.seen = {}


class Chan:
    def __init__(self, sem):
        self.sem = sem
        self.cnt = 0


def inherit(new_bufs, old_bufs):
    allev = {}
    for b in old_bufs:
        for d in (b.wr, b.rd):
            for k, (sem, val) in d.items():
                if allev.get(k, (None, 0))[1] < val:
                    allev[k] = (sem, val)
    for b in new_bufs:
        b.wr = {}
        b.rd = dict(allev)


class KB:
    def __init__(self, nc, es):
        self.nc = nc
        self.es = es
        self.nsem = 0
        self.pe = Eng("pe", nc.tensor, self.sem())
        self.act = Eng("act", nc.scalar, self.sem())
        self.dve = Eng("dve", nc.vector, self.sem())
        self.pool = Eng("pool", nc.gpsimd, self.sem())
        self.sp = Eng("sp", nc.sync, None)
        self.out_chans = []
        self.ps = [es.enter_context(nc.psum_tensor(f"ps{i}", [128, 512], F32)) for i in range(8)]
        self.pb = [Buf(f"ps{i}", excl=True) for i in range(8)]
        self.rot = list(range(8))
        self.flip = 0

    def sem(self):
        self.nsem += 1
        return self.es.enter_context(self.nc.semaphore(f"sem{self.nsem}"))

    def chan(self, out=False):
        c = Chan(self.sem())
        if out:
            self.out_chans.append(c)
        return c

    def sb(self, name, shape, dt):
        return self.es.enter_context(self.nc.sbuf_tensor("sb_" + name, shape, dt))

    def bank(self):
        i = self.rot.pop(0)
        self.rot.append(i)
        return self.ps[i], self.pb[i]

    def pin_bank(self):
        i = self.rot.pop(0)
        return i, self.ps[i], self.pb[i]

    def unpin_bank(self, i):
        self.rot.append(i)

    def _waits(self, eng, rd, wr):
        need = {}

        def add(d, same_ok):
            for k, (sem, val) in d.items():
                if (sem is eng.sem) and not same_ok:
                    continue
                if need.get(k, (None, 0))[1] < val:
                    need[k] = (sem, val)

        strict = eng is not self.pe
        for b in rd:
            add(b.wr, True)
            if b.excl:
                add(b.rd, False)
        for b in wr:
            add(b.wr, strict)
            add(b.rd, strict)
        for k, (sem, val) in need.items():
            if eng.seen.get(k, 0) < val:
                eng.h.wait_ge(sem, val)
                eng.seen[k] = val

    def op(self, eng, fn, rd=(), wr=()):
        self._waits(eng, rd, wr)
        ins = fn()
        eng.cnt += 1
        ins.then_inc(eng.sem, 1)
        k = id(eng.sem)
        ev = (eng.sem, eng.cnt)
        for b in wr:
            b.wr = {k: ev}
            b.rd = {}
        for b in rd:
            b.rd[k] = ev

    def P(self, fn, rd=(), wr=()):
        self.op(self.pe, fn, rd, wr)

    def A(self, fn, rd=(), wr=()):
        self.op(self.act, fn, rd, wr)

    def V(self, fn, rd=(), wr=()):
        self.op(self.dve, fn, rd, wr)

    def G(self, fn, rd=(), wr=()):
        self.op(self.pool, fn, rd, wr)

    def AV(self, fa, fv, rd=(), wr=()):
        self.flip ^= 1
        if self.flip:
            self.op(self.act, fa, rd, wr)
        else:
            self.op(self.dve, fv, rd, wr)

    def dma(self, q, chan, out, in_, rd=(), wr=(), **kw):
        self._waits(q, rd, wr)
        ins = q.h.dma_start(out=out, in_=in_, **kw)
        chan.cnt += 16
        ins.then_inc(chan.sem, 16)
        k = id(chan.sem)
        ev = (chan.sem, chan.cnt)
        for b in wr:
            b.wr = {k: ev}
            b.rd = {}
        for b in rd:
            b.rd[k] = ev


def build_program(dbg_tiles=None, with_sample=WITH_SAMPLE, dbg_stop=None):
    nc = bass.Bass("TRN2", target_bir_lowering=False)

    def din(name, shape):
        return nc.dram_tensor(name, shape, F32, kind="ExternalInput").ap()

    def dout(name, shape):
        return nc.dram_tensor(name, shape, F32, kind="ExternalOutput").ap()

    xp = din("xp", [2, SEQ, D])
    xs = din("xs", [64, D])
    ck = din("ck", [4, 512, 1024])
    cv = din("cv", [4, 512, 1024])
    sg = din("sg", [4, 4, 128, 256])
    sc = din("sc", [8, DFF])
    w_in = din("w_in", [D, INW])
    w_o = din("w_o", [D, D])
    w_up = din("w_up", [D, 2 * DFF])
    w_down = din("w_down", [DFF, D])
    d_gpre1 = din("gpre1", [128, 16])
    d_gpre2 = din("gpre2", [128, 16])
    d_gpost1 = din("gpost1", [128, D])
    d_gpost2 = din("gpost2", [128, D])
    d_brep = din("brep", [128, 512])
    d_ggla = din("ggla", [128, 256])
    d_wg = din("wg", [16, 512])
    d_convw = din("convw", [128, NFC * 4])
    d_biasT = din("biasT", [128, 8 * 640])
    d_biasN = din("biasN", [64, 8 * 64])
    d_cbias = din("cbias", [128, 8])
    d_biasS = din("biasS", [128, 8 * 64])
    d_smask = din("smask", [64, 132])

    yp = dout("yp", [2, SEQ, D])
    ys = dout("ys", [64, D])
    kp = dout("kp", [2, 512, 1024])
    vp = dout("vp", [2, 512, 1024])
    gp = dout("gp", [2, 4, 128, 256])
    cp = dout("cp", [4, DFF])
    ksn = dout("ksn", [64, 1024])
    vsn = dout("vsn", [64, 1024])
    gs = dout("gs", [4, 4, 128, 256])
    cs = dout("cs", [8, DFF])

    win_b = nc.dram_tensor("win_b", [12, 128, 16, 512], BF16, kind="Internal").ap()
    wo_b = nc.dram_tensor("wo_b", [4, 128, 16, 512], BF16, kind="Internal").ap()
    wup_b = nc.dram_tensor("wup_b", [22, 128, 16, 512], BF16, kind="Internal").ap()
    wdn_b = nc.dram_tensor("wdn_b", [12, 128, 16, 512], BF16, kind="Internal").ap()

    es = ExitStack()
    with es:
        K = KB(nc, es)
        P, A, V, G, AV = K.P, K.A, K.V, K.G, K.AV
        sb = K.sb

        gpre1 = sb("gpre1", [128, 16], F32)
        gpre2 = sb("gpre2", [128, 16], F32)
        gpost = sb("gpost", [128, D], F32)
        brep = sb("brep", [128, 512], F32)
        ggla = sb("ggla", [128, 256], F32)
        convw = sb("convw", [128, NFC, 4], F32)
        biasT = sb("biasT", [128, 8, 640], F32)
        biasN = sb("biasN", [64, 8, 64], F32)
        smask = sb("smask", [64, 132], F32)
        cbias = sb("cbias", [128, 8], F32)
        biasS = sb("biasS", [128, 8, 4, 16], F32)
        smask_bf = sb("smask_bf", [64, 132], BF16)
        wg_bf = sb("wg_bf", [16, 512], BF16)
        wlo = sb("wlo", [128, 16, 16], BF16)
        identf = sb("identf", [128, 128], F32)
        Ubf = sb("Ubf", [128, 128], BF16)
        onesbf = sb("onesbf", [128, 128], BF16)
        oneT = sb("oneT", [128, 1], F32)
        epsT = sb("epsT", [128, 1], F32)
        mhalf = sb("mhalf", [128, 4], F32)
        B_const = Buf("const")
        c_const = K.chan()
        for (t, d) in [(gpre1, d_gpre1), (gpre2, d_gpre2), (brep, d_brep), (ggla, d_ggla), (smask, d_smask), (cbias, d_cbias)]:
            K.dma(K.sp, c_const, t[:, :], d[:, :])
        K.dma(K.sp, c_const, convw[:, :, :], d_convw.rearrange("p (j c) -> p j c", c=4))
        K.dma(K.sp, c_const, biasT[:, :, :], d_biasT.rearrange("p (h k) -> p h k", h=8))
        K.dma(K.sp, c_const, biasN[:, :, :], d_biasN.rearrange("p (h k) -> p h k", h=8))
        K.dma(K.sp, c_const, biasS[:, :, :, :], d_biasS.rearrange("p (h b t) -> p h b t", h=8, b=4))
        c_const_g = K.chan()
        K.dma(K.pool, c_const_g, wg_bf[:, :], d_wg[:, :])
        K.dma(K.pool, c_const_g, wlo[:, :, :], w_in[:, 6144:6160].rearrange("(kc p) n -> p kc n", p=128))
        B_const.wr = {id(c_const.sem): (c_const.sem, c_const.cnt), id(c_const_g.sem): (c_const_g.sem, c_const_g.cnt)}
        B_gpost = Buf("gpost")
        c_gpost = K.chan()

        B_gen = Buf("gen")
        G(lambda: nc.gpsimd.memset(identf[:, :], 1.0), wr=[B_gen])
        G(lambda: nc.gpsimd.affine_select(out=identf[:, :], in_=identf[:, :], pattern=[[1, 128]],
                                          compare_op=ALU.is_equal, fill=0.0, base=0, channel_multiplier=-1),
          rd=[B_gen], wr=[B_gen])
        G(lambda: nc.gpsimd.memset(Ubf[:, :], 1.0), wr=[B_gen])
        G(lambda: nc.gpsimd.affine_select(out=Ubf[:, :], in_=Ubf[:, :], pattern=[[1, 128]],
                                          compare_op=ALU.is_ge, fill=0.0, base=0, channel_multiplier=-1),
          rd=[B_gen], wr=[B_gen])
        G(lambda: nc.gpsimd.memset(onesbf[:, :], 1.0), wr=[B_gen])
        G(lambda: nc.gpsimd.memset(oneT[:, :], 1.0), wr=[B_gen])
        G(lambda: nc.gpsimd.memset(epsT[:, :], EPS), wr=[B_gen])
        G(lambda: nc.gpsimd.memset(mhalf[:, :], -0.5), wr=[B_gen])
        G(lambda: nc.gpsimd.tensor_copy(out=smask_bf[:, :], in_=smask[:, :]), rd=[B_const], wr=[B_gen])
        CONST = [B_const, B_gen]

        def convert(lst, chanW):
            for (src_ap, dst_ap) in lst:
                K.dma(K.pool, chanW, dst_ap, src_ap)
            b = Buf("wscr")
            b.wr = {id(chanW.sem): (chanW.sem, chanW.cnt)}
            return b

        def kcview(ap2d):
            return ap2d.rearrange("(kc p) n -> p kc n", p=128)

        B_win = [convert([(kcview(w_in[:, j * 512:(j + 1) * 512]), win_b[j]) for j in range(6 * g, 6 * g + 6)], K.chan())
                 for g in range(2)]
        B_wo = convert([(kcview(w_o[:, j * 512:(j + 1) * 512]), wo_b[j]) for j in range(4)], K.chan())
        B_wup = []
        for g, (j0, j1) in enumerate([(2 * k_, 2 * k_ + 2) for k_ in range(11)]):
            lst = []
            for j in range(j0, j1):
                lst.append((kcview(w_up[:, j * 256:(j + 1) * 256]), wup_b[j][:, :, 0:256]))
                lst.append((kcview(w_up[:, DFF + j * 256:DFF + (j + 1) * 256]), wup_b[j][:, :, 256:512]))
            B_wup.append((j1, convert(lst, K.chan())))
        B_wdn = []
        for g in range(4):
            lst = []
            for cb in range(g, g + 1):
                for kgi in range(3):
                    nk = 16 if kgi < 2 else 12
                    lst.append((kcview(w_down[kgi * 2048: kgi * 2048 + nk * 128, cb * 512:(cb + 1) * 512]),
                                wdn_b[cb * 3 + kgi][:, 0:nk, :]))
            B_wdn.append(convert(lst, K.chan()))

        def wup_buf(j):
            for (j1, b_) in B_wup:
                if j < j1:
                    return b_

        kTr = sb("kTr", [128, 8, 768], BF16)
        vr = sb("vr", [128, 6, 8, 129], BF16)
        B_kT = [[Buf(f"kT{h}_{r}") for r in range(3)] for h in range(8)]
        B_vr = [[Buf(f"vr{b}_{j}") for j in range(2)] for b in range(6)]
        V(lambda: nc.vector.memset(vr[:, :, :, 128:129], 1.0), wr=[b for bb in B_vr for b in bb])
        slabs = [sb(f"slab{i}", [128, 16, 512], BF16) for i in range(NSLOT)]
        B_slab = [Buf(f"slab{i}") for i in range(NSLOT)]
        c_slab = [K.chan() for i in range(NSLOT)]
        Sst = sb("Sst", [128, 4, 256], F32)
        Sbf = sb("Sbf", [128, 4, 256], BF16)
        B_S = [Buf(f"S{h}") for h in range(4)]
        B_Sbf = [Buf(f"Sbf{h}") for h in range(4)]
        c_S = K.chan(out=True)
        c_Sin = K.chan()
        carry = sb("carry", [128, NFC, 8], F32)
        B_carry = [Buf(f"carry{j}") for j in range(NFC)]
        qtTm = sb("qtTm", [128, 4, 4, 64], BF16)
        B_qtTm = Buf("qtTm")
        pTn = sb("pTn", [64, 4, 64], BF16)
        B_pTn = [Buf(f"pTn{i}") for i in range(4)]
        stats = sb("stats", [128, 64], F32)
        B_st = {}

        def st(name, c0, n):
            B_st[name] = Buf("st_" + name)
            return stats[:, c0:c0 + n]

        ss = st("ss", 0, 2)
        ms = st("ms", 2, 2)
        sd = st("sd", 4, 2)
        rstd = st("rstd", 6, 2)
        ssq = st("ssq", 8, 8)
        oss = st("oss", 16, 4)
        orstd = st("orstd", 20, 4)
        oms = st("oms", 24, 4)
        osd = st("osd", 28, 4)
        rinv = st("rinv", 32, 2)
        den = st("den", 34, 2)
        ecl = st("ecl", 36, 16)

        Xr = sb("Xr", [128, 4096], F32)
        Nr = sb("Nr", [128, 4096], F32)
        actT = sb("actT", [128, 16, TT], BF16)
        B_actT = [Buf(f"actT{kc}") for kc in range(16)]
        Wr = sb("Wr", [128, 9856], F32)

        xv = Xr[:, :].rearrange("p (s d) -> p s d", s=2)
        B_x = [Buf("x0"), Buf("x1")]
        qg = Xr[:, 0:1024].rearrange("p (s d) -> p s d", s=2)
        kg = Xr[:, 1024:2048].rearrange("p (s d) -> p s d", s=2)
        vg = Xr[:, 2048:3072].bitcast(BF16).rearrange("p (s d) -> p s d", s=2)
        qT = Xr[:, 3072:4096].bitcast(BF16).rearrange("p (h t) -> p h t", h=8)
        B_qg = [Buf("qg0"), Buf("qg1")]
        B_kg = [Buf("kg0"), Buf("kg1")]
        B_vg = [[Buf(f"vg{s}_{j}") for j in range(2)] for s in range(2)]
        B_qT = [Buf(f"qT{h}") for h in range(8)]
        XA = B_x
        XB = B_qg + B_kg + [b for bb in B_vg for b in bb] + B_qT
        k2m = [Xr[0:64, 512:1024].bitcast(BF16).rearrange("p (i d) -> p i d", i=2),
               Xr[0:64, 1536:2048].bitcast(BF16).rearrange("p (i d) -> p i d", i=2)]
        B_k2m = [Buf("k2m0"), Buf("k2m1")]
        pTs = Xr[:, 2560:3072].bitcast(BF16).rearrange("p (i b t) -> p i b t", i=4, b=4)
        B_pTs = [Buf(f"pTs{i}") for i in range(4)]
        nv = Nr[:, :].rearrange("p (s d) -> p s d", s=2)
        B_n = [Buf("n0"), Buf("n1")]
        kstb = [Nr[:, 0:512], Nr[:, 512:1024]]
        vstb = [Nr[:, 1024:1536], Nr[:, 1536:2048]]
        scb = [Nr[:, 2048:2688], Nr[:, 2688:3328]]
        pTb = [Nr[:, 3328:3648].bitcast(BF16), Nr[:, 3648:3968].bitcast(BF16)]
        B_kst = [Buf("kst0"), Buf("kst1")]
        B_vst = [Buf("vst0"), Buf("vst1")]
        B_sc = [Buf("sc0"), Buf("sc1")]
        B_pT = [Buf("pT0"), Buf("pT1")]
        c_kst = [K.chan(out=True), K.chan(out=True)]
        c_vst = [K.chan(out=True), K.chan(out=True)]
        NA = B_n
        NB = B_kst + B_vst + B_sc + B_pT
        NB_S = NB
        cvb = [Nr[0:8, 0:512], Nr[0:8, 512:1024]]
        B_cvb = [Buf("cvb0"), Buf("cvb1")]
        c_cvb = [K.chan(out=True), K.chan(out=True)]
        c_cvin = [K.chan(), K.chan()]
        NC_ = B_cvb
        mix_in = Wr[:, 0:4096].rearrange("p (s d) -> p s d", s=2)
        B_mi = [[Buf(f"mi{s}_{c}") for c in range(16)] for s in range(2)]
        o = 4096
        zt = Wr[:, o:o + 512]; o += 512
        sp_hi = Wr[:, o:o + 256].bitcast(BF16); o += 256
        sp_lo = Wr[:, o:o + 256].bitcast(BF16); o += 256
        E1 = Wr[:, o:o + 512]; o += 512
        Einv = Wr[:, o:o + 512]; o += 512
        ECL = Wr[:, o:o + 512]; o += 512
        k2_ = []; qtT_ = []; ktT_ = []
        for _par in range(2):
            k2_.append(Wr[:, o:o + 256].bitcast(BF16)); o += 256
            qtT_.append(Wr[:, o:o + 256].bitcast(BF16).rearrange("p (h t) -> p h t", h=4)); o += 256
            ktT_.append(Wr[:, o:o + 256].bitcast(BF16).rearrange("p (h t) -> p h t", h=4)); o += 256
        ATm = Wr[:, o:o + 256].bitcast(BF16).rearrange("p (b t) -> p b t", b=4); o += 256
        o_sb = Wr[:, o:o + 1024].rearrange("p (h v) -> p h v", h=4); o += 1024
        junk = Wr[:, o:o + 256].bitcast(BF16); o += 256
        aloT = Wr[:, o:o + 128].bitcast(BF16); o += 128
        assert o <= 9856, o
        B_zt, B_hi, B_lo, B_E1, B_Einv, B_ECL = [Buf(n) for n in ("zt", "hi", "lo", "E1", "Einv", "ECL")]
        B_ecl_ = [Buf("ecla"), Buf("eclb")]
        B_k2_ = [Buf("k2a"), Buf("k2b")]
        B_qtT_ = [Buf("qtTa"), Buf("qtTb")]
        B_ktT_ = [Buf("ktTa"), Buf("ktTb")]
        B_ATm = [Buf(f"ATm{h}") for h in range(4)]
        B_osb = [Buf(f"osb{h}") for h in range(4)]
        B_junk = Buf("junk")
        B_alo = Buf("aloT")
        WA = ([b for bb in B_mi for b in bb] + [B_zt, B_hi, B_lo, B_E1, B_Einv, B_ECL] + B_k2_ + B_qtT_ + B_ktT_
              + B_ATm + B_osb + [B_junk, B_alo])
        cst = Wr[:, 2048:4096].rearrange("p (b d) -> p b d", b=2)
        cstN = [Nr[:, 0:1024], Nr[:, 1024:2048]]
        CST = [cst[:, 0, :], cst[:, 1, :], cstN[0], cstN[1]]
        B_cst = [Buf(f"cst{i}") for i in range(4)]
        c_cst = [K.chan() for i in range(4)]
        Sbf4 = Wr[:, 2048:4096].bitcast(BF16).rearrange("p (i h v) -> p i h v", i=4, h=4)
        B_Sbf4 = [Buf(f"Sbf4_{i}") for i in range(4)]
        actb = Wr[:, 0:5632].bitcast(BF16).rearrange("p (j t) -> p j t", j=NFC)
        B_act = [Buf(f"act{j}") for j in range(NFC)]
        gsb = [Wr[:, 5632:5632 + 260], Wr[:, 5892:5892 + 260]]
        ccb = [Wr[:, 6152:6152 + 256], Wr[:, 6408:6408 + 256]]
        sgb = [Wr[:, 6664:6664 + 256], Wr[:, 6920:6920 + 256]]
        B_gs = [Buf("gs0"), Buf("gs1")]
        B_cc = [Buf("cc0"), Buf("cc1")]
        B_sg = [Buf("sg0"), Buf("sg1")]
        WB = B_act + B_gs + B_cc + B_sg

        c_x = [K.chan(), K.chan()]
        c_x2 = [K.chan(), K.chan()]
        c_y = [K.chan(out=True), K.chan(out=True)]

        plan = []
        state = {"cur": 0, "loaded": 0, "limit": 0}
        SLABS_PER_TILE = 50

        def tile_plan():
            l = []
            for j in range(12):
                l.append((win_b[j], 16, B_win[j // 6]))
            for j in range(4):
                l.append((wo_b[j], 16, B_wo))
            for sl in range(22):
                l.append((wup_b[sl], 16, wup_buf(sl)))
            for cb in range(4):
                for kgi in range(3):
                    l.append((wdn_b[cb * 3 + kgi], 16 if kgi < 2 else 12, B_wdn[cb]))
            return l

        NTILES_TOTAL = (2 * NTILE if dbg_tiles is None else len(dbg_tiles)) + (1 if with_sample else 0)
        for _ in range(NTILES_TOTAL):
            plan.extend(tile_plan())

        def pump(n):
            while state["loaded"] < min(n + NSLOT, len(plan), state["limit"]):
                m = state["loaded"]
                ap, nk, srcb = plan[m]
                K.dma(K.sp, c_slab[m % NSLOT], slabs[m % NSLOT][:, 0:nk, :], ap[:, 0:nk, :],
                      rd=[srcb], wr=[B_slab[m % NSLOT]])
                state["loaded"] += 1

        def next_slab():
            n = state["cur"]
            pump(n)
            state["cur"] += 1
            return slabs[n % NSLOT], B_slab[n % NSLOT]

        def next_slab_old():
            n = state["cur"]
            while state["loaded"] < min(n + NSLOT, len(plan)):
                m = state["loaded"]
                ap, nk, srcb = plan[m]
                K.dma(K.sp, c_slab[m % NSLOT], slabs[m % NSLOT][:, 0:nk, :], ap[:, 0:nk, :],
                      rd=[srcb], wr=[B_slab[m % NSLOT]])
                state["loaded"] += 1
            state["cur"] += 1
            return slabs[n % NSLOT], B_slab[n % NSLOT]

        pool_ok = [False]

        def pin_tables():
            pass

        B_ss = [Buf("ss0"), Buf("ss1")]
        B_ms = [Buf("ms0"), Buf("ms1")]
        B_sd = [Buf("sd0"), Buf("sd1")]
        B_rs = [Buf("rs0"), Buf("rs1")]
        B_ssq = [Buf("ssq0"), Buf("ssq1")]

        def rstd_s(s, np_, denom):
            V(lambda: nc.vector.tensor_scalar(out=ms[0:np_, s:s + 1], in0=ss[0:np_, s:s + 1], scalar1=1.0 / denom,
                                              scalar2=EPS, op0=ALU.mult, op1=ALU.add), rd=[B_ss[s]], wr=[B_ms[s]])
            if pool_ok[0]:
                G(lambda: nc.gpsimd.tensor_tensor(out=rstd[0:np_, s:s + 1], in0=ms[0:np_, s:s + 1],
                                                  in1=mhalf[0:np_, 0:1], op=ALU.pow),
                  rd=[B_ms[s]] + CONST, wr=[B_rs[s]])
            else:
                A(lambda: nc.scalar.activation(out=sd[0:np_, s:s + 1], in_=ms[0:np_, s:s + 1], func=AF.Sqrt),
                  rd=[B_ms[s]], wr=[B_sd[s]])
                V(lambda: nc.vector.reciprocal(out=rstd[0:np_, s:s + 1], in_=sd[0:np_, s:s + 1]),
                  rd=[B_sd[s]], wr=[B_rs[s]])

        def norm_transpose(nsub, np_, gcol, chunked=False):
            ntok = nsub * np_
            for s in range(nsub):
                if chunked:
                    A(lambda: nc.scalar.activation(out=actT[0:np_, 8 * s:8 * s + 8, :],
                                                   in_=xv[0:np_, s, :].rearrange("p (a b) -> p a b", a=8),
                                                   func=AF.Square, accum_out=ss[0:np_, s:s + 1]),
                      rd=[B_x[s]], wr=[B_ss[s]] + B_actT[8 * s:8 * s + 8])
                else:
                    A(lambda: nc.scalar.activation(out=nv[0:np_, s, :], in_=xv[0:np_, s, :], func=AF.Square,
                                                   accum_out=ss[0:np_, s:s + 1]),
                      rd=[B_x[s]], wr=[B_ss[s], B_n[s]])
            for s in range(nsub):
                rstd_s(s, np_, float(D))
            for s in range(nsub):
                V(lambda: nc.vector.tensor_scalar(out=nv[0:np_, s, :], in0=xv[0:np_, s, :], scalar1=rstd[0:np_, s:s + 1],
                                                  scalar2=None, op0=ALU.mult),
                  rd=[B_x[s], B_rs[s]], wr=[B_n[s]])
            for kc in range(NKC):
                ps, pb = K.bank()
                for s in range(nsub):
                    P(lambda: nc.tensor.transpose(out=ps[:, s * np_:(s + 1) * np_],
                                                  in_=nv[0:np_, s, kc * 128:(kc + 1) * 128],
                                                  identity=identf[0:np_, 0:np_]),
                      rd=[B_n[s]] + CONST, wr=[pb])
                AV(lambda: nc.scalar.mul(out=actT[:, kc, 0:ntok], in_=ps[:, 0:ntok], mul=gcol[:, kc:kc + 1]),
                   lambda: nc.vector.tensor_scalar(out=actT[:, kc, 0:ntok], in0=ps[:, 0:ntok],
                                                   scalar1=gcol[:, kc:kc + 1], scalar2=None, op0=ALU.mult),
                   rd=[pb] + CONST, wr=[B_actT[kc]])

        def fm_block(slab, sbuf, c0, ntok, ncol=128):
            ps, pb = K.bank()
            for kc in range(NKC):
                P(lambda: nc.tensor.matmul(ps[0:ncol, 0:ntok], lhsT=slab[:, kc, c0:c0 + ncol], rhs=actT[:, kc, 0:ntok],
                                           start=(kc == 0), stop=(kc == NKC - 1)),
                  rd=[sbuf, B_actT[kc]], wr=[pb])
            return ps, pb

        def tm_block(slab, sbuf, s, np_):
            ps, pb = K.bank()
            for kc in range(NKC):
                P(lambda: nc.tensor.matmul(ps[0:np_, 0:512], lhsT=actT[:, kc, s * np_:(s + 1) * np_],
                                           rhs=slab[:, kc, :], start=(kc == 0), stop=(kc == NKC - 1)),
                  rd=[sbuf, B_actT[kc]], wr=[pb])
            return ps, pb

        def copy_av(dst, src, rd, wr):
            AV(lambda: nc.scalar.copy(out=dst, in_=src),
               lambda: nc.vector.tensor_copy(out=dst, in_=src), rd=rd, wr=wr)

        def load_gpost(d_gp):
            K.dma(K.sp, c_gpost, gpost[:, :], d_gp[:, :], wr=[B_gpost])

        def evac_post(ps, pb, s, cb, np_, jk, jkb):
            A(lambda: nc.scalar.activation(out=jk[0:np_, 0:512], in_=ps[0:np_, :], func=AF.Square,
                                           accum_out=ssq[0:np_, s * 4 + cb:s * 4 + cb + 1]),
              rd=[pb], wr=[jkb, B_ssq[s]])
            V(lambda: nc.vector.tensor_tensor(out=nv[0:np_, s, cb * 512:(cb + 1) * 512], in0=ps[0:np_, :],
                                              in1=gpost[0:np_, cb * 512:(cb + 1) * 512], op=ALU.mult),
              rd=[pb, B_gpost], wr=[B_n[s]])

        def post_norm_residual(nsub, np_, into_n=False):
            for s in range(nsub):
                V(lambda: nc.vector.reduce_sum(out=ss[0:np_, s:s + 1], in_=ssq[0:np_, s * 4:(s + 1) * 4], axis=AX.X),
                  rd=[B_ssq[s]], wr=[B_ss[s]])
                rstd_s(s, np_, float(D))
            for s in range(nsub):
                dstv, dstb = (nv, B_n) if into_n else (xv, B_x)
                V(lambda: nc.vector.scalar_tensor_tensor(out=dstv[0:np_, s, :], in0=nv[0:np_, s, :],
                                                         scalar=rstd[0:np_, s:s + 1], in1=xv[0:np_, s, :],
                                                         op0=ALU.mult, op1=ALU.add),
                  rd=[B_n[s], B_rs[s], B_x[s]], wr=[dstb[s]])

        def attention_prompt(ti, extra=None):
            g0 = 2 * ti
            pending = []
            units_done = [0]

            def flush():
                (s, h, bsel, blks, pv) = pending.pop(0)
                pso, pbo = K.bank()
                for i, bk in enumerate(blks):
                    rp = bk % 6
                    P(lambda: nc.tensor.matmul(pso[:, 0:129], lhsT=pv[:, i * 128:(i + 1) * 128],
                                               rhs=vr[:, rp, h, 0:129], start=(i == 0), stop=(i == len(blks) - 1)),
                      rd=[B_pT[bsel], B_vr[rp][h // 4]], wr=[pbo])
                V(lambda: nc.vector.reciprocal(out=rinv[:, bsel:bsel + 1], in_=pso[:, 128:129]),
                  rd=[pbo], wr=[B_st["rinv"]])
                V(lambda: nc.vector.tensor_scalar(out=mix_in[:, s, h * 128:(h + 1) * 128], in0=pso[:, 0:128],
                                                  scalar1=rinv[:, bsel:bsel + 1], scalar2=None, op0=ALU.mult),
                  rd=[pbo, B_st["rinv"]], wr=[B_mi[s][h]])

            for s in range(2):
                g = g0 + s
                allb = [bk for bk in range(g - 4, g + 1) if bk >= 0]
                cst = [bk for bk in allb if 4 - (g - bk) in (1, 2)]
                var = [bk for bk in allb if 4 - (g - bk) not in (1, 2)]
                nvar, ncst = len(var), len(cst)
                blks = var + cst
                for h in range(8):
                    bsel = (s * 8 + h) % 2
                    psA, pbA = K.bank()
                    psB, pbB = K.bank()
                    for i, bk in enumerate(blks):
                        rp = bk % 6
                        if i < nvar:
                            dst, db = psB[:, i * 128:(i + 1) * 128], pbB
                        else:
                            dst, db = psA[:, (i - nvar) * 128:(i - nvar + 1) * 128], pbA
                        P(lambda: nc.tensor.matmul(dst, lhsT=kTr[:, h, rp * 128:(rp + 1) * 128],
                                                   rhs=qT[:, h, s * 128:(s + 1) * 128], start=True, stop=True),
                          rd=[B_kT[h][rp // 2], B_qT[h]], wr=[db])
                    scv = scb[bsel]
                    pv = pTb[bsel]
                    V(lambda: nc.vector.scalar_tensor_tensor(
                        out=scv[:, 0:nvar * 128], in0=psB[:, 0:nvar * 128], scalar=ATT_SCALE,
                        in1=biasT[:, h, (3 - nvar) * 128:384], op0=ALU.mult, op1=ALU.add),
                      rd=[pbB] + CONST, wr=[B_sc[bsel]])
                    if ncst:
                        A(lambda: nc.scalar.activation(out=pv[:, nvar * 128:(nvar + ncst) * 128],
                                                       in_=psA[:, 0:ncst * 128], func=AF.Exp,
                                                       bias=cbias[:, h:h + 1], scale=ATT_SCALE),
                          rd=[pbA] + CONST, wr=[B_pT[bsel]])
                    A(lambda: nc.scalar.activation(out=pv[:, 0:nvar * 128], in_=scv[:, 0:nvar * 128], func=AF.Exp),
                      rd=[B_sc[bsel]], wr=[B_pT[bsel]])
                    if pending:
                        flush()
                    pending.append((s, h, bsel, blks, pv))
                    units_done[0] += 1
                    if extra is not None:
                        extra(units_done[0])
            while pending:
                flush()

        def attention_sample():
            V(lambda: nc.vector.memset(pTs[:, :, :, :], 0.0), wr=B_pTs)
            V(lambda: nc.vector.memset(pTn[:, :, :], 0.0), wr=B_pTn)
            inherit(B_cst[2:4], B_kst + B_vst)
            ld = [0]
            pending = []

            def stage_load(src):
                cb_ = ld[0] % 4
                ld[0] += 1
                K.dma(K.sp, c_cst[cb_], CST[cb_], src, wr=[B_cst[cb_]])
                return cb_

            def flush():
                (i, h, bsel) = pending.pop(0)
                pso, pbo = K.bank()
                for bk in range(4):
                    P(lambda: nc.tensor.matmul(pso[0:64, 0:129], lhsT=pTs[:, i, bk, :], rhs=vr[:, bk, h, 0:129],
                                               start=(bk == 0), stop=False),
                      rd=[B_pTs[i], B_vr[bk][h // 4]], wr=[pbo])
                P(lambda: nc.tensor.matmul(pso[0:64, 0:129], lhsT=pTn[:, i, :], rhs=vr[0:64, 4, h, 0:129],
                                           start=False, stop=True),
                  rd=[B_pTn[i], B_vr[4][h // 4]], wr=[pbo])
                V(lambda: nc.vector.tensor_scalar(out=den[0:64, bsel:bsel + 1], in0=pso[0:64, 128:129],
                                                  scalar1=1e-30, scalar2=None, op0=ALU.max),
                  rd=[pbo], wr=[B_st["den"]])
                V(lambda: nc.vector.reciprocal(out=rinv[0:64, bsel:bsel + 1], in_=den[0:64, bsel:bsel + 1]),
                  rd=[B_st["den"]], wr=[B_st["rinv"]])
                if i == 0:
                    V(lambda: nc.vector.tensor_scalar(out=mix_in[0:64, 0, h * 128:(h + 1) * 128],
                                                      in0=pso[0:64, 0:128], scalar1=rinv[0:64, bsel:bsel + 1],
                                                      scalar2=None, op0=ALU.mult),
                      rd=[pbo, B_st["rinv"]], wr=[B_mi[0][h]])
                else:
                    V(lambda: nc.vector.scalar_tensor_tensor(out=mix_in[0:64, 0, h * 128:(h + 1) * 128],
                                                             in0=pso[0:64, 0:128], scalar=rinv[0:64, bsel:bsel + 1],
                                                             in1=mix_in[0:64, 0, h * 128:(h + 1) * 128],
                                                             op0=ALU.mult, op1=ALU.add),
                      rd=[pbo, B_st["rinv"], B_mi[0][h]], wr=[B_mi[0][h]])

            for i in range(4):
                for bk in range(4):
                    cb_ = stage_load(ck[i, bk * 128:(bk + 1) * 128, :])
                    for hq in range(2):
                        ps, pb = K.bank()
                        for hh in range(4):
                            h = hq * 4 + hh
                            P(lambda: nc.tensor.transpose(out=ps[:, hh * 128:(hh + 1) * 128],
                                                          in_=CST[cb_][:, h * 128:(h + 1) * 128],
                                                          identity=identf[:, :]),
                              rd=[B_cst[cb_]] + CONST, wr=[pb])
                        dst = kTr[:, hq * 4:hq * 4 + 4, bk * 128:(bk + 1) * 128]
                        src = ps[:, :].rearrange("p (h t) -> p h t", h=4)
                        copy_av(dst, src, [pb], [B_kT[hq * 4 + hh][bk // 2] for hh in range(4)])
                for bk in range(4):
                    cb_ = stage_load(cv[i, bk * 128:(bk + 1) * 128, :])
                    dst = vr[:, bk, :, 0:128]
                    src = CST[cb_].rearrange("p (h d) -> p h d", h=8)
                    copy_av(dst, src, [B_cst[cb_]], B_vr[bk])
                for h in range(8):
                    bsel = h % 2
                    psA, pbA = K.bank()
                    psB, pbB = K.bank()
                    for bk in range(4):
                        P(lambda: nc.tensor.matmul(psA[:, bk * 16:(bk + 1) * 16],
                                                   lhsT=kTr[:, h, bk * 128:(bk + 1) * 128],
                                                   rhs=qT[:, h, 16 * i:16 * i + 16], start=True, stop=True),
                          rd=[B_kT[h][bk // 2], B_qT[h]], wr=[pbA])
                    P(lambda: nc.tensor.matmul(psB[0:64, 0:16], lhsT=kTr[:, h, 512:576],
                                               rhs=qT[:, h, 16 * i:16 * i + 16], start=True, stop=True),
                      rd=[B_kT[h][2], B_qT[h]], wr=[pbB])
                    scv = scb[bsel]
                    V(lambda: nc.vector.scalar_tensor_tensor(
                        out=scv[:, 0:64].rearrange("p (b t) -> p b t", b=4),
                        in0=psA[:, 0:64].rearrange("p (b t) -> p b t", b=4), scalar=ATT_SCALE,
                        in1=biasS[:, h, :, :],
                        op0=ALU.mult, op1=ALU.add),
                      rd=[pbA] + CONST, wr=[B_sc[bsel]])
                    V(lambda: nc.vector.scalar_tensor_tensor(
                        out=scv[0:64, 64:80], in0=psB[0:64, 0:16], scalar=ATT_SCALE,
                        in1=biasN[:, h, 16 * i:16 * i + 16], op0=ALU.mult, op1=ALU.add),
                      rd=[pbB] + CONST, wr=[B_sc[bsel]])
                    if pending:
                        flush()
                    A(lambda: nc.scalar.activation(out=pTs[:, i, :, 16 * i:16 * i + 16],
                                                   in_=scv[:, 0:64].rearrange("p (b t) -> p b t", b=4), func=AF.Exp),
                      rd=[B_sc[bsel]], wr=[B_pTs[i]])
                    A(lambda: nc.scalar.activation(out=pTn[:, i, 16 * i:16 * i + 16], in_=scv[0:64, 64:80],
                                                   func=AF.Exp),
                      rd=[B_sc[bsel]], wr=[B_pTn[i]])
                    pending.append((i, h, bsel))
                while pending:
                    flush()
            inherit(B_kst + B_vst, B_cst[2:4])

        def gla_front(prompt, s, np_, par):
            k2, qtT, ktT = k2_[par], qtT_[par], ktT_[par]
            B_k2, B_qtT, B_ktT = B_k2_[par], B_qtT_[par], B_ktT_[par]
            eoff = par * 4 if prompt else 0
            nseq = 1 if prompt else 4
            if prompt:
                Umat = Ubf[:, :]
                Omat = onesbf[:, :]
                sel = onesbf[:, 0:1]
            else:
                Umat = smask_bf[0:64, 4:68]
                Omat = smask_bf[0:64, 68:132]
                sel = smask_bf[0:64, 0:4]
            ps, pb = K.bank()
            P(lambda: nc.tensor.matmul(ps[0:np_, 0:512], lhsT=aloT[0:16, s * np_:(s + 1) * np_], rhs=wg_bf[0:16, :],
                                       start=True, stop=True), rd=[B_alo] + CONST, wr=[pb])
            V(lambda: nc.vector.tensor_tensor(out=zt[0:np_, :], in0=ps[0:np_, 0:512], in1=brep[0:np_, :], op=ALU.add),
              rd=[pb] + CONST, wr=[B_zt])
            yield
            A(lambda: nc.scalar.activation(out=zt[0:np_, :], in_=zt[0:np_, :], func=AF.Exp, scale=-1.0),
              rd=[B_zt], wr=[B_zt])
            A(lambda: nc.scalar.activation(out=zt[0:np_, :], in_=zt[0:np_, :], func=AF.Ln, bias=oneT[0:np_, 0:1]),
              rd=[B_zt] + CONST, wr=[B_zt])
            yield
            V(lambda: nc.vector.tensor_copy(out=sp_hi[0:np_, :], in_=zt[0:np_, :]), rd=[B_zt], wr=[B_hi])
            V(lambda: nc.vector.tensor_tensor(out=sp_lo[0:np_, :], in0=zt[0:np_, :], in1=sp_hi[0:np_, :],
                                              op=ALU.subtract), rd=[B_zt, B_hi], wr=[B_lo])
            yield
            psc, pbc = K.bank()
            P(lambda: nc.tensor.matmul(psc[0:np_, 0:512], lhsT=Umat, rhs=sp_hi[0:np_, :], start=True, stop=False),
              rd=[B_hi] + CONST, wr=[pbc])
            P(lambda: nc.tensor.matmul(psc[0:np_, 0:512], lhsT=Umat, rhs=sp_lo[0:np_, :], start=False, stop=True),
              rd=[B_lo] + CONST, wr=[pbc])
            pst, pbt = K.bank()
            P(lambda: nc.tensor.matmul(pst[0:np_, 0:512], lhsT=Omat, rhs=sp_hi[0:np_, :], start=True, stop=False),
              rd=[B_hi] + CONST, wr=[pbt])
            P(lambda: nc.tensor.matmul(pst[0:np_, 0:512], lhsT=Omat, rhs=sp_lo[0:np_, :], start=False, stop=True),
              rd=[B_lo] + CONST, wr=[pbt])
            pse, pbe = K.bank()
            for h in range(4):
                P(lambda: nc.tensor.matmul(pse[:, h * nseq:(h + 1) * nseq], lhsT=sp_hi[0:np_, h * 128:(h + 1) * 128],
                                           rhs=sel, start=True, stop=False), rd=[B_hi] + CONST, wr=[pbe])
                P(lambda: nc.tensor.matmul(pse[:, h * nseq:(h + 1) * nseq], lhsT=sp_lo[0:np_, h * 128:(h + 1) * 128],
                                           rhs=sel, start=False, stop=True), rd=[B_lo] + CONST, wr=[pbe])
            yield
            A(lambda: nc.scalar.activation(out=E1[0:np_, :], in_=psc[0:np_, 0:512], func=AF.Exp, scale=-1.0 / 16.0),
              rd=[pbc], wr=[B_E1])
            A(lambda: nc.scalar.activation(out=Einv[0:np_, :], in_=psc[0:np_, 0:512], func=AF.Exp, scale=1.0 / 16.0),
              rd=[pbc], wr=[B_Einv])
            A(lambda: nc.scalar.activation(out=ECL[0:np_, :], in_=pst[0:np_, 0:512], func=AF.Exp, scale=-1.0 / 16.0),
              rd=[pbt], wr=[B_ECL])
            A(lambda: nc.scalar.activation(out=ecl[:, eoff:eoff + 4 * nseq], in_=pse[:, 0:4 * nseq], func=AF.Exp,
                                           scale=-1.0 / 16.0), rd=[pbe], wr=[B_ecl_[par]])
            yield
            V(lambda: nc.vector.tensor_tensor(out=E1[0:np_, :], in0=E1[0:np_, :], in1=qg[0:np_, s, :], op=ALU.mult),
              rd=[B_E1, B_qg[s]], wr=[B_E1])
            V(lambda: nc.vector.tensor_tensor(out=Einv[0:np_, :], in0=Einv[0:np_, :], in1=kg[0:np_, s, :], op=ALU.mult),
              rd=[B_Einv, B_kg[s]], wr=[B_Einv])
            V(lambda: nc.vector.tensor_tensor(out=k2[0:np_, :], in0=Einv[0:np_, :], in1=ECL[0:np_, :], op=ALU.mult),
              rd=[B_Einv, B_ECL], wr=[B_k2])
            yield
            psq, pbq = K.bank()
            psk, pbk = K.bank()
            for h in range(4):
                P(lambda: nc.tensor.transpose(out=psq[:, h * np_:(h + 1) * np_], in_=E1[0:np_, h * 128:(h + 1) * 128],
                                              identity=identf[0:np_, 0:np_]), rd=[B_E1] + CONST, wr=[pbq])
                P(lambda: nc.tensor.transpose(out=psk[:, h * np_:(h + 1) * np_], in_=Einv[0:np_, h * 128:(h + 1) * 128],
                                              identity=identf[0:np_, 0:np_]), rd=[B_Einv] + CONST, wr=[pbk])
            A(lambda: nc.scalar.copy(out=qtT[:, :, 0:np_], in_=psq[:, 0:4 * np_].rearrange("p (h t) -> p h t", h=4)),
              rd=[pbq], wr=[B_qtT])
            V(lambda: nc.vector.tensor_copy(out=ktT[:, :, 0:np_],
                                            in_=psk[:, 0:4 * np_].rearrange("p (h t) -> p h t", h=4)),
              rd=[pbk], wr=[B_ktT])
            yield
            if not prompt:
                V(lambda: nc.vector.memset(qtTm[:, :, :, :], 0.0), wr=[B_qtTm])
                for i in range(4):
                    V(lambda: nc.vector.tensor_copy(out=qtTm[:, i, :, 16 * i:16 * i + 16],
                                                    in_=qtT[:, :, 16 * i:16 * i + 16]),
                      rd=[B_qtT], wr=[B_qtTm])
                    V(lambda: nc.vector.tensor_scalar(out=k2m[i // 2][:, i % 2, :], in0=k2[0:64, :],
                                                      scalar1=smask[0:64, i:i + 1], scalar2=None, op0=ALU.mult),
                      rd=[B_k2] + CONST, wr=[B_k2m[i // 2]])
            yield

        def gla_back(prompt, s, np_, par, hook=None):
            nseq = 1 if prompt else 4
            Umat = Ubf[:, :] if prompt else smask_bf[0:64, 4:68]
            k2, qtT, ktT = k2_[par], qtT_[par], ktT_[par]
            B_k2, B_qtT, B_ktT = B_k2_[par], B_qtT_[par], B_ktT_[par]
            eoff = par * 4 if prompt else 0
            bankA = []
            for h in range(4):
                psa, pba = K.bank()
                bankA.append((psa, pba))
                P(lambda: nc.tensor.matmul(psa[0:np_, 0:np_], lhsT=ktT[:, h, 0:np_], rhs=qtT[:, h, 0:np_],
                                           start=True, stop=True), rd=[B_ktT, B_qtT], wr=[pba])
            for h in range(4):
                psa, pba = bankA[h]
                V(lambda: nc.vector.tensor_tensor(out=ATm[0:np_, h, 0:np_], in0=psa[0:np_, 0:np_], in1=Umat,
                                                  op=ALU.mult), rd=[pba] + CONST, wr=[B_ATm[h]])
            if hook is not None:
                hook()
            bankO = []
            for h in range(4):
                pso, pbo = K.bank()
                bankO.append((pso, pbo))
                P(lambda: nc.tensor.matmul(pso[0:np_, 0:256], lhsT=ATm[0:np_, h, 0:np_],
                                           rhs=vg[0:np_, s, h * 256:(h + 1) * 256], start=True, stop=False),
                  rd=[B_ATm[h], B_vg[s][h // 2]], wr=[pbo])
                if prompt:
                    P(lambda: nc.tensor.matmul(pso[0:np_, 0:256], lhsT=qtT[:, h, 0:np_], rhs=Sbf[:, h, :],
                                               start=False, stop=True), rd=[B_qtT, B_Sbf[h]], wr=[pbo])
                else:
                    for i in range(4):
                        P(lambda: nc.tensor.matmul(pso[0:np_, 0:256], lhsT=qtTm[:, i, h, :], rhs=Sbf4[:, i, h, :],
                                                   start=False, stop=(i == 3)), rd=[B_qtTm, B_Sbf4[i]], wr=[pbo])
            for h in range(4):
                pso, pbo = bankO[h]
                V(lambda: nc.vector.tensor_copy(out=o_sb[0:np_, h, :], in_=pso[0:np_, 0:256]), rd=[pbo], wr=[B_osb[h]])
                A(lambda: nc.scalar.activation(out=junk[0:np_, 0:256], in_=o_sb[0:np_, h, :], func=AF.Square,
                                               accum_out=oss[0:np_, h:h + 1]), rd=[B_osb[h]], wr=[B_junk, B_st["oss"]])
            if hook is not None:
                hook()
            if prompt:
                bankU = []
                for h in range(4):
                    psu, pbu = K.bank()
                    bankU.append((psu, pbu))
                    P(lambda: nc.tensor.matmul(psu[:, 0:256], lhsT=k2[0:np_, h * 128:(h + 1) * 128],
                                               rhs=vg[0:np_, s, h * 256:(h + 1) * 256], start=True, stop=True),
                      rd=[B_k2, B_vg[s][h // 2]], wr=[pbu])
                for h in range(4):
                    psu, pbu = bankU[h]
                    V(lambda: nc.vector.scalar_tensor_tensor(out=Sst[:, h, :], in0=Sst[:, h, :],
                                                             scalar=ecl[:, eoff + h:eoff + h + 1],
                                                             in1=psu[:, 0:256], op0=ALU.mult, op1=ALU.add),
                      rd=[B_S[h], B_ecl_[par], pbu], wr=[B_S[h]])
                    A(lambda: nc.scalar.copy(out=Sbf[:, h, :], in_=Sst[:, h, :]), rd=[B_S[h]], wr=[B_Sbf[h]])
            if hook is not None:
                hook()
                hook()
            V(lambda: nc.vector.tensor_scalar(out=oms[0:np_, :], in0=oss[0:np_, :], scalar1=1.0 / 256.0, scalar2=EPS,
                                              op0=ALU.mult, op1=ALU.add), rd=[B_st["oss"]], wr=[B_st["oms"]])
            if pool_ok[0]:
                G(lambda: nc.gpsimd.tensor_tensor(out=orstd[0:np_, :], in0=oms[0:np_, :], in1=mhalf[0:np_, 0:4],
                                                  op=ALU.pow), rd=[B_st["oms"]] + CONST, wr=[B_st["orstd"]])
            else:
                A(lambda: nc.scalar.activation(out=osd[0:np_, :], in_=oms[0:np_, :], func=AF.Sqrt),
                  rd=[B_st["oms"]], wr=[B_st["osd"]])
                V(lambda: nc.vector.reciprocal(out=orstd[0:np_, :], in_=osd[0:np_, :]), rd=[B_st["osd"]],
                  wr=[B_st["orstd"]])
            for h in range(4):
                mb = [B_mi[s][8 + 2 * h], B_mi[s][9 + 2 * h]]
                msl = mix_in[0:np_, s, 1024 + h * 256:1024 + (h + 1) * 256]
                tq = junk[0:np_, :].bitcast(F32)
                A(lambda: nc.scalar.activation(out=tq, in_=msl, func=AF.Tanh, scale=0.5), rd=mb, wr=[B_junk])
                V(lambda: nc.vector.scalar_tensor_tensor(out=msl, in0=tq, scalar=1.0, in1=msl,
                                                         op0=ALU.add, op1=ALU.mult),
                  rd=[B_junk] + mb, wr=mb)
                V(lambda: nc.vector.scalar_tensor_tensor(out=o_sb[0:np_, h, :], in0=o_sb[0:np_, h, :],
                                                         scalar=orstd[0:np_, h:h + 1], in1=ggla[0:np_, :],
                                                         op0=ALU.mult, op1=ALU.mult),
                  rd=[B_osb[h], B_st["orstd"]] + CONST, wr=[B_osb[h]])
                V(lambda: nc.vector.scalar_tensor_tensor(out=msl, in0=msl, scalar=0.5, in1=o_sb[0:np_, h, :],
                                                         op0=ALU.mult, op1=ALU.mult),
                  rd=[B_osb[h]] + mb, wr=mb)

        def gla_sample_state_update():
            for i in range(4):
                K.dma(K.sp, c_Sin, Sst[:, :, :], sg[i].rearrange("h d v -> d h v"), wr=B_S)
                for h in range(4):
                    psu, pbu = K.bank()
                    P(lambda: nc.tensor.matmul(psu[:, 0:256], lhsT=k2m[i // 2][:, i % 2, h * 128:(h + 1) * 128],
                                               rhs=vg[0:64, 0, h * 256:(h + 1) * 256], start=True, stop=True),
                      rd=[B_k2m[i // 2], B_vg[0][h // 2]], wr=[pbu])
                    V(lambda: nc.vector.scalar_tensor_tensor(out=Sst[:, h, :], in0=Sst[:, h, :],
                                                             scalar=ecl[:, h * 4 + i:h * 4 + i + 1],
                                                             in1=psu[:, 0:256], op0=ALU.mult, op1=ALU.add),
                      rd=[B_S[h], B_ecl_[0], pbu], wr=[B_S[h]])
                K.dma(K.act, c_S, gs[i].rearrange("h d v -> d h v"), Sst[:, :, :], rd=B_S)

        def run_tile(kind, b, ti):
            prompt = (kind == "p")
            nsub, np_ = (2, 128) if prompt else (1, 64)
            ntok = nsub * np_
            rslot = ti % 3
            kcol0 = rslot * 256
            emit_kv = (prompt and ti >= NTILE - 2) or (not prompt)

            def dbg_dump(ph):
                for s in range(nsub):
                    dst = yp[b, ti * TT + s * 128: ti * TT + (s + 1) * 128, :] if prompt else ys[:, :]
                    if ph < 4:
                        K.dma(K.act, c_y[s], dst, mix_in[0:np_, s, :], rd=WA)
                    else:
                        K.dma(K.act, c_y[s], dst, xv[0:np_, s, :], rd=[B_x[s]])

            for s in range(nsub):
                src = xp[b, ti * TT + s * 128: ti * TT + (s + 1) * 128, :] if prompt else xs[:, :]
                K.dma(K.sp, c_x[s], xv[0:np_, s, :], src, wr=[B_x[s]])
            state["limit"] += SLABS_PER_TILE
            pump(state["cur"])
            norm_transpose(nsub, np_, gpre1, chunked=True)
            inherit(XB + B_k2m + B_pTs, XA)
            inherit(NB, NA)
            inherit(WA + B_cst + B_Sbf4, WB)

            if dbg_stop == 0:
                return dbg_dump(0)
            kv_i = 0
            for j in range(4):
                slab, sbuf = next_slab()
                for hb in range(4):
                    h = (j % 2) * 4 + hb
                    ps, pb = fm_block(slab, sbuf, hb * 128, ntok)
                    if j < 2:
                        dst, dbuf = qT[:, h, 0:ntok], B_qT[h]
                    elif prompt:
                        dst, dbuf = kTr[:, h, kcol0:kcol0 + ntok], B_kT[h][rslot]
                    else:
                        dst, dbuf = kTr[:, h, 512:512 + ntok], B_kT[h][2]
                    copy_av(dst, ps[:, 0:ntok], [pb], [dbuf])
                if j >= 2 and emit_kv:
                    for s in range(nsub):
                        ps, pb = tm_block(slab, sbuf, s, np_)
                        bs = kv_i % 2
                        kv_i += 1
                        copy_av(kstb[bs][0:np_, :], ps[0:np_, :], [pb], [B_kst[bs]])
                        c0 = (j - 2) * 512
                        if prompt:
                            r0 = (ti - (NTILE - 2)) * TT + s * 128
                            dst = kp[b, r0:r0 + 128, c0:c0 + 512]
                        else:
                            dst = ksn[:, c0:c0 + 512]
                        K.dma(K.act, c_kst[bs], dst, kstb[bs][0:np_, :], rd=[B_kst[bs]])
            for j in range(2):
                slab, sbuf = next_slab()
                for s in range(nsub):
                    ps, pb = tm_block(slab, sbuf, s, np_)
                    blk = (2 * ti + s) % 6 if prompt else 4
                    dst = vr[0:np_, blk, 4 * j:4 * j + 4, 0:128]
                    src = ps[0:np_, :].rearrange("p (h d) -> p h d", h=4)
                    copy_av(dst, src, [pb], [B_vr[blk][j]])
                    if emit_kv:
                        bs = kv_i % 2
                        kv_i += 1
                        copy_av(vstb[bs][0:np_, :], ps[0:np_, :], [pb], [B_vst[bs]])
                        c0 = j * 512
                        if prompt:
                            r0 = (ti - (NTILE - 2)) * TT + s * 128
                            dst2 = vp[b, r0:r0 + 128, c0:c0 + 512]
                        else:
                            dst2 = vsn[:, c0:c0 + 512]
                        K.dma(K.act, c_vst[bs], dst2, vstb[bs][0:np_, :], rd=[B_vst[bs]])
            for j in range(2):
                slab, sbuf = next_slab()
                for s in range(nsub):
                    ps, pb = tm_block(slab, sbuf, s, np_)
                    if j == 0:
                        A(lambda: nc.scalar.mul(out=qg[0:np_, s, :], in_=ps[0:np_, :], mul=GLA_QSCALE),
                          rd=[pb], wr=[B_qg[s]])
                    else:
                        V(lambda: nc.vector.tensor_copy(out=kg[0:np_, s, :], in_=ps[0:np_, :]), rd=[pb], wr=[B_kg[s]])
            for j in range(2):
                slab, sbuf = next_slab()
                for s in range(nsub):
                    ps, pb = tm_block(slab, sbuf, s, np_)
                    copy_av(vg[0:np_, s, j * 512:(j + 1) * 512], ps[0:np_, :], [pb], [B_vg[s][j]])
            ps, pb = K.bank()
            for kc in range(NKC):
                P(lambda: nc.tensor.matmul(ps[0:16, 0:ntok], lhsT=wlo[:, kc, :], rhs=actT[:, kc, 0:ntok],
                                           start=(kc == 0), stop=(kc == NKC - 1)),
                  rd=[B_actT[kc]] + CONST, wr=[pb])
            V(lambda: nc.vector.tensor_copy(out=aloT[0:16, 0:ntok], in_=ps[0:16, 0:ntok]), rd=[pb], wr=[B_alo])
            load_gpost(d_gpost1)

            if dbg_stop == 1:
                return dbg_dump(1)
            rg = {}
            RGS = [(0, 0), (0, 1), (1, 0), (1, 1)]

            def rg_piece(blk, piece, npieces):
                j, s = RGS[blk]
                if piece == 0:
                    if s == 0:
                        rg["slab"] = next_slab()
                    rg["bank"] = K.pin_bank()
                slab, sbuf = rg["slab"]
                bi, ps, pb = rg["bank"]
                per = NKC // npieces
                for kc in range(piece * per, (piece + 1) * per):
                    P(lambda: nc.tensor.matmul(ps[0:np_, 0:512], lhsT=actT[:, kc, s * np_:(s + 1) * np_],
                                               rhs=slab[:, kc, :], start=(kc == 0), stop=(kc == NKC - 1)),
                      rd=[sbuf, B_actT[kc]], wr=[pb])
                if piece == npieces - 1:
                    copy_av(mix_in[0:np_, s, 1024 + j * 512:1024 + (j + 1) * 512], ps[0:np_, :], [pb],
                            B_mi[s][8 + 4 * j:12 + 4 * j])
                    K.unpin_bank(bi)

            def mix_transposes(kcs):
                for kc in kcs:
                    ps, pb = K.bank()
                    for s in range(nsub):
                        P(lambda: nc.tensor.transpose(out=ps[:, s * np_:(s + 1) * np_],
                                                      in_=mix_in[0:np_, s, kc * 128:(kc + 1) * 128],
                                                      identity=identf[0:np_, 0:np_]),
                          rd=[B_mi[s][kc]] + CONST, wr=[pb])
                    copy_av(actT[:, kc, 0:ntok], ps[:, 0:ntok], [pb], [B_actT[kc]])

            if prompt:
                if ti == 0:
                    V(lambda: nc.vector.memset(Sst[:, :, :], 0.0), wr=B_S)
                    V(lambda: nc.vector.memset(Sbf[:, :, :], 0.0), wr=B_Sbf)
                g0 = gla_front(True, 0, np_, 0)

                def extra(u):
                    rg_piece((u - 1) // 4, (u - 1) % 4, 4)
                    next(g0, None)

                attention_prompt(ti, extra=extra)
                if dbg_stop == 2:
                    return dbg_dump(2)
                for _ in g0:
                    pass
                mix_transposes(range(0, 8))
                g1 = gla_front(True, 1, np_, 1)

                def hook():
                    next(g1, None)
                    next(g1, None)

                gla_back(True, 0, np_, 0, hook=hook)
                for _ in g1:
                    pass
                gla_back(True, 1, np_, 1)
                if ti == NTILE - 1:
                    K.dma(K.act, c_S, gp[b].rearrange("h d v -> d h v"), Sst[:, :, :], rd=B_S)
            else:
                rg_piece(0, 0, 1)
                rg_piece(2, 0, 1)
                attention_sample()
                inherit(B_Sbf4, B_cst)
                for i in range(4):
                    K.dma(K.sp, c_Sin, Sst[:, :, :], sg[i].rearrange("h d v -> d h v"), wr=B_S)
                    V(lambda: nc.vector.tensor_copy(out=Sbf4[:, i, :, :], in_=Sst[:, :, :]), rd=B_S, wr=[B_Sbf4[i]])
                for _ in gla_front(False, 0, np_, 0):
                    pass
                gla_back(False, 0, np_, 0)
                gla_sample_state_update()
            if dbg_stop == 3.5:
                return dbg_dump(3.5)
            mix_transposes(range(8, NKC) if prompt else range(NKC))
            inherit(XA, XB + B_k2m + B_pTs)
            inherit(NA, NB)
            for s in range(nsub):
                src = xp[b, ti * TT + s * 128: ti * TT + (s + 1) * 128, :] if prompt else xs[:, :]
                K.dma(K.act, c_x2[s], xv[0:np_, s, :], src, wr=[B_x[s]])
            for cb in range(4):
                slab, sbuf = next_slab()
                for s in range(nsub):
                    ps, pb = tm_block(slab, sbuf, s, np_)
                    evac_post(ps, pb, s, cb, np_, junk, B_junk)
            post_norm_residual(nsub, np_)

            if dbg_stop == 4:
                return dbg_dump(4)
            norm_transpose(nsub, np_, gpre2)
            inherit(WB, WA + B_cst + B_Sbf4)
            inherit(NC_, NA)

            if dbg_stop == 5:
                return dbg_dump(5)
            load_gpost(d_gpost2)
            nseq, L = (1, 256) if prompt else (4, 16)
            if prompt and ti == 0:
                V(lambda: nc.vector.memset(carry[:, :, :], 0.0), wr=B_carry)
            if not prompt:
                for c in range(11):
                    bs = c % 2
                    K.dma(K.sp, c_cvin[bs], cvb[bs], sc[:, c * 512:(c + 1) * 512], wr=[B_cvb[bs]])
                    ps, pb = K.bank()
                    for q in range(4):
                        P(lambda: nc.tensor.transpose(out=ps[:, q * 8:(q + 1) * 8], in_=cvb[bs][:, q * 128:(q + 1) * 128],
                                                      identity=identf[0:8, 0:8]), rd=[B_cvb[bs]] + CONST, wr=[pb])
                    V(lambda: nc.vector.tensor_copy(out=carry[:, 4 * c:4 * c + 4, :],
                                                    in_=ps[:, 0:32].rearrange("p (q r) -> p q r", q=4)),
                      rd=[pb], wr=B_carry[4 * c:4 * c + 4])
            for sl in range(22):
                gslab, gsbuf = next_slab()
                for fb in range(2):
                    j = sl * 2 + fb
                    bs = j % 2
                    psg, pbg = fm_block(gslab, gsbuf, fb * 128, ntok)
                    psv, pbv = fm_block(gslab, gsbuf, 256 + fb * 128, ntok)
                    gsv = gsb[bs][:, 0:nseq * (L + 2)].rearrange("p (i t) -> p i t", i=nseq)
                    ccv = ccb[bs][:, 0:ntok].rearrange("p (i t) -> p i t", i=nseq)
                    sgv = sgb[bs][:, 0:ntok].rearrange("p (i t) -> p i t", i=nseq)
                    V(lambda: nc.vector.tensor_copy(out=gsv[:, :, 0:2],
                                                    in_=carry[:, j, 0:2 * nseq].rearrange("p (i r) -> p i r", i=nseq)),
                      rd=[B_carry[j]], wr=[B_gs[bs]])
                    A(lambda: nc.scalar.copy(out=gsv[:, :, 2:L + 2],
                                             in_=psg[:, 0:ntok].rearrange("p (i t) -> p i t", i=nseq)),
                      rd=[pbg], wr=[B_gs[bs]])
                    A(lambda: nc.scalar.activation(out=ccv, in_=psg[:, 0:ntok].rearrange("p (i t) -> p i t", i=nseq),
                                                   func=AF.Identity, scale=convw[:, j, 2:3], bias=convw[:, j, 3:4]),
                      rd=[pbg] + CONST, wr=[B_cc[bs]])
                    V(lambda: nc.vector.tensor_copy(out=carry[:, j, 0:2 * nseq].rearrange("p (i r) -> p i r", i=nseq),
                                                    in_=gsv[:, :, L:L + 2]),
                      rd=[B_gs[bs]], wr=[B_carry[j]])
                    V(lambda: nc.vector.scalar_tensor_tensor(out=ccv, in0=gsv[:, :, 1:L + 1], scalar=convw[:, j, 1:2],
                                                             in1=ccv, op0=ALU.mult, op1=ALU.add),
                      rd=[B_gs[bs], B_cc[bs]] + CONST, wr=[B_cc[bs]])
                    V(lambda: nc.vector.scalar_tensor_tensor(out=ccv, in0=gsv[:, :, 0:L], scalar=convw[:, j, 0:1],
                                                             in1=ccv, op0=ALU.mult, op1=ALU.add),
                      rd=[B_gs[bs], B_cc[bs]] + CONST, wr=[B_cc[bs]])
                    A(lambda: nc.scalar.activation(out=sgv, in_=ccv, func=AF.Silu), rd=[B_cc[bs]], wr=[B_sg[bs]])
                    V(lambda: nc.vector.tensor_tensor(out=actb[:, j, 0:ntok], in0=psv[:, 0:ntok], in1=sgb[bs][:, 0:ntok],
                                                      op=ALU.mult), rd=[pbv, B_sg[bs]], wr=[B_act[j]])
            pin_tables()
            if (prompt and ti == NTILE - 1) or not prompt:
                nr = 2 * nseq
                for c in range(11):
                    bs = c % 2
                    ps, pb = K.bank()
                    for q in range(4):
                        P(lambda: nc.tensor.transpose(out=ps[0:nr, q * 128:(q + 1) * 128], in_=carry[:, 4 * c + q, 0:nr],
                                                      identity=identf[:, :]), rd=[B_carry[4 * c + q]] + CONST, wr=[pb])
                    V(lambda: nc.vector.tensor_copy(out=cvb[bs][0:nr, :], in_=ps[0:nr, 0:512]), rd=[pb], wr=[B_cvb[bs]])
                    dst = cp[2 * b:2 * b + 2, c * 512:(c + 1) * 512] if prompt else cs[:, c * 512:(c + 1) * 512]
                    K.dma(K.act, c_cvb[bs], dst, cvb[bs][0:nr, :], rd=[B_cvb[bs]])
            inherit(NA, NC_)

            if dbg_stop == 6:
                return dbg_dump(6)
            for cb in range(4):
                banks = [K.bank() for s in range(nsub)]
                for kgi in range(3):
                    nk = 16 if kgi < 2 else 12
                    slab, sbuf = next_slab()
                    for s in range(nsub):
                        ps, pb = banks[s]
                        for kc in range(nk):
                            fc = kgi * 16 + kc
                            P(lambda: nc.tensor.matmul(ps[0:np_, 0:512], lhsT=actb[:, fc, s * np_:(s + 1) * np_],
                                                       rhs=slab[:, kc, :], start=(fc == 0), stop=(fc == NFC - 1)),
                              rd=[sbuf, B_act[fc]], wr=[pb])
                for s in range(nsub):
                    ps, pb = banks[s]
                    evac_post(ps, pb, s, cb, np_, junk_b, B_junkb)
            post_norm_residual(nsub, np_, into_n=True)
            for s in range(nsub):
                dst = yp[b, ti * TT + s * 128: ti * TT + (s + 1) * 128, :] if prompt else ys[:, :]
                K.dma(K.act, c_y[s], dst, nv[0:np_, s, :], rd=[B_n[s]])

        junk_b = sb("junk_b", [128, 512], BF16)
        B_junkb = Buf("junkb")

        pin_tables()
        for b in range(2):
            for ti in range(NTILE):
                if dbg_tiles is not None and (b, ti) not in dbg_tiles:
                    continue
                run_tile("p", b, ti)
                pool_ok[0] = True
        if with_sample:
            run_tile("s", 0, 0)

        for c in K.out_chans:
            if c.cnt:
                nc.sync.wait_ge(c.sem, c.cnt)
    return nc


def _host_tables(rel_bias):
    rb = np.asarray(rel_bias, np.float32)[0]
    kl = np.arange(128)[:, None, None]
    blk = np.arange(5)[None, :, None]
    q = np.arange(128)[None, None, :]
    d = 512 - 128 * blk + q - kl
    idx = np.clip(d, -128, 128) + 128
    tab = rb[:, idx]
    tab = np.ascontiguousarray(np.transpose(tab, (1, 0, 2, 3))).copy()
    maskA = (q >= 64) & (kl < 64)
    maskB = (q < 64) & (kl >= 64)
    tab[:, :, 0, :][np.broadcast_to(maskA[:, 0, :][:, None, :], (128, 8, 128))] = MASKV
    tab[:, :, 4, :][np.broadcast_to(maskB[:, 0, :][:, None, :], (128, 8, 128))] = MASKV
    biasS = np.ascontiguousarray(tab[:, :, 0:4, 0:16]).reshape(128, 8 * 64)
    tab = np.ascontiguousarray(tab[:, :, [0, 3, 4, 1, 2], :])
    biasT = tab.reshape(128, 8 * 640)
    kk = np.arange(64)[:, None]
    qq = np.arange(64)[None, :]
    dn = (qq % 16) - (kk % 16)
    tn = rb[:, dn + 128]
    tn = np.ascontiguousarray(np.transpose(tn, (1, 0, 2))).copy()
    cross = (kk // 16) != (qq // 16)
    tn[np.broadcast_to(cross[:, None, :], (64, 8, 64))] = MASKV
    biasN = tn.reshape(64, 8 * 64)
    t = np.arange(64)
    seqsel = (t[:, None] // 16 == np.arange(4)[None, :]).astype(np.float32)
    same = (t[:, None] // 16 == t[None, :] // 16)
    Ub = (same & (t[:, None] <= t[None, :])).astype(np.float32)
    onesb = same.astype(np.float32)
    smask = np.concatenate([seqsel, Ub, onesb], axis=1).astype(np.float32)
    return biasT.astype(np.float32), biasN.astype(np.float32), smask, biasS.astype(np.float32)


_NC_CACHE = {}


def kernel(x_prompt, x_sample, cache_k, cache_v, state_gla, state_conv, g_mix_pre, w_in, w_gate_up,
           b_gate, rel_bias, g_gla, w_o, g_mix_post, g_ffn_pre, w_up, w_conv, b_conv, w_down, g_ffn_post):
    f = lambda a: np.ascontiguousarray(np.asarray(a, dtype=np.float32))
    x_prompt, x_sample = f(x_prompt), f(x_sample)
    cache_k, cache_v = f(cache_k)[0], f(cache_v)[0]
    state_gla, state_conv = f(state_gla)[0], f(state_conv)[0]
    biasT, biasN, smask, biasS = _host_tables(rel_bias)
    colmajor = lambda g: np.ascontiguousarray(f(g)[0].reshape(16, 128).T)
    rep = lambda v, n=128: np.ascontiguousarray(np.broadcast_to(f(v).reshape(1, -1), (n, f(v).size)))
    wc = f(w_conv)[0]
    bc = f(b_conv)[0]
    convw = np.stack([wc[0], wc[1], wc[2], bc], axis=-1)
    convw = np.ascontiguousarray(convw.reshape(NFC, 128, 4).transpose(1, 0, 2).reshape(128, NFC * 4))
    shared = {
        "w_in": f(w_in)[0], "w_o": f(w_o)[0], "w_up": f(w_up)[0], "w_down": f(w_down)[0],
        "gpre1": colmajor(g_mix_pre), "gpre2": colmajor(g_ffn_pre),
        "gpost1": rep(g_mix_post), "gpost2": rep(g_ffn_post),
        "brep": rep(b_gate), "ggla": rep(g_gla), "wg": f(w_gate_up)[0],
        "convw": convw, "biasT": biasT, "biasN": biasN, "smask": smask, "biasS": biasS,
        "cbias": np.ascontiguousarray(np.broadcast_to(f(rel_bias)[0][:, 256].reshape(1, 8), (128, 8))),
    }
    in_maps = []
    for c in range(NCORES):
        m = dict(shared)
        m["xp"] = x_prompt[2 * c:2 * c + 2]
        m["xs"] = np.ascontiguousarray(x_sample[4 * c:4 * c + 4].reshape(64, D))
        m["ck"] = np.ascontiguousarray(cache_k[4 * c:4 * c + 4].reshape(4, 512, 1024))
        m["cv"] = np.ascontiguousarray(cache_v[4 * c:4 * c + 4].reshape(4, 512, 1024))
        m["sg"] = state_gla[4 * c:4 * c + 4]
        m["sc"] = np.ascontiguousarray(state_conv[4 * c:4 * c + 4].reshape(8, DFF))
        in_maps.append(m)
    if "nc" not in _NC_CACHE:
        _NC_CACHE["nc"] = build_program()
    nc = _NC_CACHE["nc"]
    res = run_bass_kernel_spmd(nc, in_maps, core_ids=list(range(NCORES)))
    R = res.results
    cat = lambda k: np.concatenate([np.asarray(r[k]) for r in R], axis=0)
    y_prompt = cat("yp").reshape(16, SEQ, D)
    y_sample = cat("ys").reshape(32, 16, D)
    k_prompt = cat("kp").reshape(1, 16, 512, 8, 128)
    v_prompt = cat("vp").reshape(1, 16, 512, 8, 128)
    gla_prompt = cat("gp").reshape(1, 16, 4, 128, 256)
    conv_prompt = cat("cp").reshape(1, 16, 2, DFF)
    k_sample = cat("ksn").reshape(1, 32, 16, 8, 128)
    v_sample = cat("vsn").reshape(1, 32, 16, 8, 128)
    gla_sample = cat("gs").reshape(1, 32, 4, 128, 256)
    conv_sample = cat("cs").reshape(1, 32, 2, DFF)
    return (y_prompt.astype(np.float32), y_sample.astype(np.float32), k_prompt, v_prompt, gla_prompt,
            conv_prompt, k_sample, v_sample, gla_sample, conv_sample)
```

```python
import numpy as np
from contextlib import ExitStack
import concourse.bass as bass
import concourse.mybir as mybir
from concourse.bass_utils import run_bass_kernel_spmd

F32 = mybir.dt.float32
BF16 = mybir.dt.bfloat16
AF = mybir.ActivationFunctionType
ALU = mybir.AluOpType
AX = mybir.AxisListType

NCORES = 8
D = 2048
NKC = 16
SEQ = 2048
TT = 256
NTILE = SEQ // TT
DFF = 5632
NFC = 44
INW = 6160
EPS = 1e-6
MASKV = -30000.0
ATT_SCALE = 128.0 ** -0.5
GLA_QSCALE = 128.0 ** -0.5
NSLOT = 3
WITH_SAMPLE = True


class Buf:
    __slots__ = ("name", "wr", "rd", "excl")

    def __init__(self, name, excl=False):
        self.name = name
        self.wr = {}
        self.rd = {}
        self.excl = excl


class Eng:
    def __init__(self, name, h, sem):
        self.name = name
        self.h = h
        self.sem = sem
        self.cnt = 0
        self.seen = {}


class Chan:
    def __init__(self, sem):
        self.sem = sem
        self.cnt = 0


def inherit(new_bufs, old_bufs):
    allev = {}
    for b in old_bufs:
        for d in (b.wr, b.rd):
            for k, (sem, val) in d.items():
                if allev.get(k, (None, 0))[1] < val:
                    allev[k] = (sem, val)
    for b in new_bufs:
        b.wr = {}
        b.rd = dict(allev)


class KB:
    def __init__(self, nc, es):
        self.nc = nc
        self.es = es
        self.nsem = 0
        self.pe = Eng("pe", nc.tensor, self.sem())
        self.act = Eng("act", nc.scalar, self.sem())
        self.dve = Eng("dve", nc.vector, self.sem())
        self.pool = Eng("pool", nc.gpsimd, self.sem())
        self.sp = Eng("sp", nc.sync, None)
        self.out_chans = []
        self.ps = [es.enter_context(nc.psum_tensor(f"ps{i}", [128, 512], F32)) for i in range(8)]
        self.pb = [Buf(f"ps{i}", excl=True) for i in range(8)]
        self.rot = list(range(8))
        self.flip = 0

    def sem(self):
        self.nsem += 1
        return self.es.enter_context(self.nc.semaphore(f"sem{self.nsem}"))

    def chan(self, out=False):
        c = Chan(self.sem())
        if out:
            self.out_chans.append(c)
        return c

    def sb(self, name, shape, dt):
        return self.es.enter_context(self.nc.sbuf_tensor("sb_" + name, shape, dt))

    def bank(self):
        i = self.rot.pop(0)
        self.rot.append(i)
        return self.ps[i], self.pb[i]

    def pin_bank(self):
        i = self.rot.pop(0)
        return i, self.ps[i], self.pb[i]

    def unpin_bank(self, i):
        self.rot.append(i)

    def _waits(self, eng, rd, wr):
        need = {}

        def add(d, same_ok):
            for k, (sem, val) in d.items():
                if (sem is eng.sem) and not same_ok:
                    continue
                if need.get(k, (None, 0))[1] < val:
                    need[k] = (sem, val)

        strict = eng is not self.pe
        for b in rd:
            add(b.wr, True)
            if b.excl:
                add(b.rd, False)
        for b in wr:
            add(b.wr, strict)
            add(b.rd, strict)
        for k, (sem, val) in need.items():
            if eng.seen.get(k, 0) < val:
                eng.h.wait_ge(sem, val)
                eng.seen[k] = val

    def op(self, eng, fn, rd=(), wr=()):
        self._waits(eng, rd, wr)
        ins = fn()
        eng.cnt += 1
        ins.then_inc(eng.sem, 1)
        k = id(eng.sem)
        ev = (eng.sem, eng.cnt)
        for b in wr:
            b.wr = {k: ev}
            b.rd = {}
        for b in rd:
            b.rd[k] = ev

    def P(self, fn, rd=(), wr=()):
        self.op(self.pe, fn, rd, wr)

    def A(self, fn, rd=(), wr=()):
        self.op(self.act, fn, rd, wr)

    def V(self, fn, rd=(), wr=()):
        self.op(self.dve, fn, rd, wr)

    def G(self, fn, rd=(), wr=()):
        self.op(self.pool, fn, rd, wr)

    def AV(self, fa, fv, rd=(), wr=()):
        self.flip ^= 1
        if self.flip:
            self.op(self.act, fa, rd, wr)
        else:
            self.op(self.dve, fv, rd, wr)

    def dma(self, q, chan, out, in_, rd=(), wr=(), **kw):
        self._waits(q, rd, wr)
        ins = q.h.dma_start(out=out, in_=in_, **kw)
        chan.cnt += 16
        ins.then_inc(chan.sem, 16)
        k = id(chan.sem)
        ev = (chan.sem, chan.cnt)
        for b in wr:
            b.wr = {k: ev}
            b.rd = {}
        for b in rd:
            b.rd[k] = ev


def build_program(dbg_tiles=None, with_sample=WITH_SAMPLE, dbg_stop=None):
    nc = bass.Bass("TRN2", target_bir_lowering=False)

    def din(name, shape):
        return nc.dram_tensor(name, shape, F32, kind="ExternalInput").ap()

    def dout(name, shape):
        return nc.dram_tensor(name, shape, F32, kind="ExternalOutput").ap()

    xp = din("xp", [2, SEQ, D])
    xs = din("xs", [64, D])
    ck = din("ck", [4, 512, 1024])
    cv = din("cv", [4, 512, 1024])
    sg = din("sg", [4, 4, 128, 256])
    sc = din("sc", [8, DFF])
    w_in = din("w_in", [D, INW])
    w_o = din("w_o", [D, D])
    w_up = din("w_up", [D, 2 * DFF])
    w_down = din("w_down", [DFF, D])
    d_gpre1 = din("gpre1", [128, 16])
    d_gpre2 = din("gpre2", [128, 16])
    d_gpost1 = din("gpost1", [128, D])
    d_gpost2 = din("gpost2", [128, D])
    d_brep = din("brep", [128, 512])
    d_ggla = din("ggla", [128, 256])
    d_wg = din("wg", [16, 512])
    d_convw = din("convw", [128, NFC * 4])
    d_biasT = din("biasT", [128, 8 * 640])
    d_biasN = din("biasN", [64, 8 * 64])
    d_cbias = din("cbias", [128, 8])
    d_biasS = din("biasS", [128, 8 * 64])
    d_smask = din("smask", [64, 132])

    yp = dout("yp", [2, SEQ, D])
    ys = dout("ys", [64, D])
    kp = dout("kp", [2, 512, 1024])
    vp = dout("vp", [2, 512, 1024])
    gp = dout("gp", [2, 4, 128, 256])
    cp = dout("cp", [4, DFF])
    ksn = dout("ksn", [64, 1024])
    vsn = dout("vsn", [64, 1024])
    gs = dout("gs", [4, 4, 128, 256])
    cs = dout("cs", [8, DFF])

    win_b = nc.dram_tensor("win_b", [12, 128, 16, 512], BF16, kind="Internal").ap()
    wo_b = nc.dram_tensor("wo_b", [4, 128, 16, 512], BF16, kind="Internal").ap()
    wup_b = nc.dram_tensor("wup_b", [22, 128, 16, 512], BF16, kind="Internal").ap()
    wdn_b = nc.dram_tensor("wdn_b", [12, 128, 16, 512], BF16, kind="Internal").ap()

    es = ExitStack()
    with es:
        K = KB(nc, es)
        P, A, V, G, AV = K.P, K.A, K.V, K.G, K.AV
        sb = K.sb

        gpre1 = sb("gpre1", [128, 16], F32)
        gpre2 = sb("gpre2", [128, 16], F32)
        gpost = sb("gpost", [128, D], F32)
        brep = sb("brep", [128, 512], F32)
        ggla = sb("ggla", [128, 256], F32)
        convw = sb("convw", [128, NFC, 4], F32)
        biasT = sb("biasT", [128, 8, 640], F32)
        biasN = sb("biasN", [64, 8, 64], F32)
        smask = sb("smask", [64, 132], F32)
        cbias = sb("cbias", [128, 8], F32)
        biasS = sb("biasS", [128, 8, 4, 16], F32)
        smask_bf = sb("smask_bf", [64, 132], BF16)
        wg_bf = sb("wg_bf", [16, 512], BF16)
        wlo = sb("wlo", [128, 16, 16], BF16)
        identf = sb("identf", [128, 128], F32)
        Ubf = sb("Ubf", [128, 128], BF16)
        onesbf = sb("onesbf", [128, 128], BF16)
        oneT = sb("oneT", [128, 1], F32)
        epsT = sb("epsT", [128, 1], F32)
        mhalf = sb("mhalf", [128, 4], F32)
        B_const = Buf("const")
        c_const = K.chan()
        for (t, d) in [(gpre1, d_gpre1), (gpre2, d_gpre2), (brep, d_brep), (ggla, d_ggla), (smask, d_smask), (cbias, d_cbias)]:
            K.dma(K.sp, c_const, t[:, :], d[:, :])
        K.dma(K.sp, c_const, convw[:, :, :], d_convw.rearrange("p (j c) -> p j c", c=4))
        K.dma(K.sp, c_const, biasT[:, :, :], d_biasT.rearrange("p (h k) -> p h k", h=8))
        K.dma(K.sp, c_const, biasN[:, :, :], d_biasN.rearrange("p (h k) -> p h k", h=8))
        K.dma(K.sp, c_const, biasS[:, :, :, :], d_biasS.rearrange("p (h b t) -> p h b t", h=8, b=4))
        c_const_g = K.chan()
        K.dma(K.pool, c_const_g, wg_bf[:, :], d_wg[:, :])
        K.dma(K.pool, c_const_g, wlo[:, :, :], w_in[:, 6144:6160].rearrange("(kc p) n -> p kc n", p=128))
        B_const.wr = {id(c_const.sem): (c_const.sem, c_const.cnt), id(c_const_g.sem): (c_const_g.sem, c_const_g.cnt)}
        B_gpost = Buf("gpost")
        c_gpost = K.chan()

        B_gen = Buf("gen")
        G(lambda: nc.gpsimd.memset(identf[:, :], 1.0), wr=[B_gen])
        G(lambda: nc.gpsimd.affine_select(out=identf[:, :], in_=identf[:, :], pattern=[[1, 128]],
                                          compare_op=ALU.is_equal, fill=0.0, base=0, channel_multiplier=-1),
          rd=[B_gen], wr=[B_gen])
        G(lambda: nc.gpsimd.memset(Ubf[:, :], 1.0), wr=[B_gen])
        G(lambda: nc.gpsimd.affine_select(out=Ubf[:, :], in_=Ubf[:, :], pattern=[[1, 128]],
                                          compare_op=ALU.is_ge, fill=0.0, base=0, channel_multiplier=-1),
          rd=[B_gen], wr=[B_gen])
        G(lambda: nc.gpsimd.memset(onesbf[:, :], 1.0), wr=[B_gen])
        G(lambda: nc.gpsimd.memset(oneT[:, :], 1.0), wr=[B_gen])
        G(lambda: nc.gpsimd.memset(epsT[:, :], EPS), wr=[B_gen])
        G(lambda: nc.gpsimd.memset(mhalf[:, :], -0.5), wr=[B_gen])
        G(lambda: nc.gpsimd.tensor_copy(out=smask_bf[:, :], in_=smask[:, :]), rd=[B_const], wr=[B_gen])
        CONST = [B_const, B_gen]

        def convert(lst, chanW):
            for (src_ap, dst_ap) in lst:
                K.dma(K.pool, chanW, dst_ap, src_ap)
            b = Buf("wscr")
            b.wr = {id(chanW.sem): (chanW.sem, chanW.cnt)}
            return b

        def kcview(ap2d):
            return ap2d.rearrange("(kc p) n -> p kc n", p=128)

        B_win = [convert([(kcview(w_in[:, j * 512:(j + 1) * 512]), win_b[j]) for j in range(2 * g, 2 * g + 2)], K.chan())
                 for g in range(6)]
        B_wo = convert([(kcview(w_o[:, j * 512:(j + 1) * 512]), wo_b[j]) for j in range(4)], K.chan())
        B_wup = []
        for g, (j0, j1) in enumerate([(2 * k_, 2 * k_ + 2) for k_ in range(11)]):
            lst = []
            for j in range(j0, j1):
                lst.append((kcview(w_up[:, j * 256:(j + 1) * 256]), wup_b[j][:, :, 0:256]))
                lst.append((kcview(w_up[:, DFF + j * 256:DFF + (j + 1) * 256]), wup_b[j][:, :, 256:512]))
            B_wup.append((j1, convert(lst, K.chan())))
        B_wdn = []
        for g in range(4):
            lst = []
            for cb in range(g, g + 1):
                for kgi in range(3):
                    nk = 16 if kgi < 2 else 12
                    lst.append((kcview(w_down[kgi * 2048: kgi * 2048 + nk * 128, cb * 512:(cb + 1) * 512]),
                                wdn_b[cb * 3 + kgi][:, 0:nk, :]))
            B_wdn.append(convert(lst, K.chan()))

        def wup_buf(j):
            for (j1, b_) in B_wup:
                if j < j1:
                    return b_

        kTr = sb("kTr", [128, 8, 768], BF16)
        vr = sb("vr", [128, 6, 8, 129], BF16)
        B_kT = [[Buf(f"kT{h}_{r}") for r in range(3)] for h in range(8)]
        B_vr = [[Buf(f"vr{b}_{j}") for j in range(2)] for b in range(6)]
        V(lambda: nc.vector.memset(vr[:, :, :, 128:129], 1.0), wr=[b for bb in B_vr for b in bb])
        slabs = [sb(f"slab{i}", [128, 16, 512], BF16) for i in range(NSLOT)]
        B_slab = [Buf(f"slab{i}") for i in range(NSLOT)]
        c_slab = [K.chan() for i in range(NSLOT)]
        Sst = sb("Sst", [128, 4, 256], F32)
        Sbf = sb("Sbf", [128, 4, 256], BF16)
        B_S = [Buf(f"S{h}") for h in range(4)]
        B_Sbf = [Buf(f"Sbf{h}") for h in range(4)]
        c_S = K.chan(out=True)
        c_Sin = K.chan()
        carry = sb("carry", [128, NFC, 8], F32)
        B_carry = [Buf(f"carry{j}") for j in range(NFC)]
        qtTm = sb("qtTm", [128, 4, 4, 64], BF16)
        B_qtTm = Buf("qtTm")
        pTn = sb("pTn", [64, 4, 64], BF16)
        B_pTn = [Buf(f"pTn{i}") for i in range(4)]
        stats = sb("stats", [128, 64], F32)
        B_st = {}

        def st(name, c0, n):
            B_st[name] = Buf("st_" + name)
            return stats[:, c0:c0 + n]

        ss = st("ss", 0, 2)
        ms = st("ms", 2, 2)
        sd = st("sd", 4, 2)
        rstd = st("rstd", 6, 2)
        ssq = st("ssq", 8, 8)
        oss = st("oss", 16, 4)
        orstd = st("orstd", 20, 4)
        oms = st("oms", 24, 4)
        osd = st("osd", 28, 4)
        rinv = st("rinv", 32, 2)
        den = st("den", 34, 2)
        ecl = st("ecl", 36, 16)

        Xr = sb("Xr", [128, 4096], F32)
        Nr = sb("Nr", [128, 4096], F32)
        actT = sb("actT", [128, 16, TT], BF16)
        B_actT = [Buf(f"actT{kc}") for kc in range(16)]
        Wr = sb("Wr", [128, 9856], F32)

        xv = Xr[:, :].rearrange("p (s d) -> p s d", s=2)
        B_x = [Buf("x0"), Buf("x1")]
        qg = Xr[:, 0:1024].rearrange("p (s d) -> p s d", s=2)
        kg = Xr[:, 1024:2048].rearrange("p (s d) -> p s d", s=2)
        vg = Xr[:, 2048:3072].bitcast(BF16).rearrange("p (s d) -> p s d", s=2)
        qT = Xr[:, 3072:4096].bitcast(BF16).rearrange("p (h t) -> p h t", h=8)
        B_qg = [Buf("qg0"), Buf("qg1")]
        B_kg = [Buf("kg0"), Buf("kg1")]
        B_vg = [[Buf(f"vg{s}_{j}") for j in range(2)] for s in range(2)]
        B_qT = [Buf(f"qT{h}") for h in range(8)]
        XA = B_x
        XB = B_qg + B_kg + [b for bb in B_vg for b in bb] + B_qT
        k2m = [Xr[0:64, 512:1024].bitcast(BF16).rearrange("p (i d) -> p i d", i=2),
               Xr[0:64, 1536:2048].bitcast(BF16).rearrange("p (i d) -> p i d", i=2)]
        B_k2m = [Buf("k2m0"), Buf("k2m1")]
        pTs = Xr[:, 2560:3072].bitcast(BF16).rearrange("p (i b t) -> p i b t", i=4, b=4)
        B_pTs = [Buf(f"pTs{i}") for i in range(4)]
        nv = Nr[:, :].rearrange("p (s d) -> p s d", s=2)
        B_n = [Buf("n0"), Buf("n1")]
        kstb = [Nr[:, 0:512], Nr[:, 512:1024]]
        vstb = [Nr[:, 1024:1536], Nr[:, 1536:2048]]
        scb = [Nr[:, 2048:2688], Nr[:, 2688:3328]]
        pTb = [Nr[:, 3328:3648].bitcast(BF16), Nr[:, 3648:3968].bitcast(BF16)]
        B_kst = [Buf("kst0"), Buf("kst1")]
        B_vst = [Buf("vst0"), Buf("vst1")]
        B_sc = [Buf("sc0"), Buf("sc1")]
        B_pT = [Buf("pT0"), Buf("pT1")]
        c_kst = [K.chan(out=True), K.chan(out=True)]
        c_vst = [K.chan(out=True), K.chan(out=True)]
        NA = B_n
        NB = B_kst + B_vst + B_sc + B_pT
        NB_S = NB
        cvb = [Nr[0:8, 0:512], Nr[0:8, 512:1024]]
        B_cvb = [Buf("cvb0"), Buf("cvb1")]
        c_cvb = [K.chan(out=True), K.chan(out=True)]
        c_cvin = [K.chan(), K.chan()]
        NC_ = B_cvb
        mix_in = Wr[:, 0:4096].rearrange("p (s d) -> p s d", s=2)
        B_mi = [[Buf(f"mi{s}_{c}") for c in range(16)] for s in range(2)]
        o = 4096
        zt = Wr[:, o:o + 512]; o += 512
        sp_hi = Wr[:, o:o + 256].bitcast(BF16); o += 256
        sp_lo = Wr[:, o:o + 256].bitcast(BF16); o += 256
        E1 = Wr[:, o:o + 512]; o += 512
        Einv = Wr[:, o:o + 512]; o += 512
        ECL = Wr[:, o:o + 512]; o += 512
        k2_ = []; qtT_ = []; ktT_ = []
        for _par in range(2):
            k2_.append(Wr[:, o:o + 256].bitcast(BF16)); o += 256
            qtT_.append(Wr[:, o:o + 256].bitcast(BF16).rearrange("p (h t) -> p h t", h=4)); o += 256
            ktT_.append(Wr[:, o:o + 256].bitcast(BF16).rearrange("p (h t) -> p h t", h=4)); o += 256
        ATm = Wr[:, o:o + 256].bitcast(BF16).rearrange("p (b t) -> p b t", b=4); o += 256
        o_sb = Wr[:, o:o + 1024].rearrange("p (h v) -> p h v", h=4); o += 1024
        junk = Wr[:, o:o + 256].bitcast(BF16); o += 256
        aloT = Wr[:, o:o + 128].bitcast(BF16); o += 128
        assert o <= 9856, o
        B_zt, B_hi, B_lo, B_E1, B_Einv, B_ECL = [Buf(n) for n in ("zt", "hi", "lo", "E1", "Einv", "ECL")]
        B_ecl_ = [Buf("ecla"), Buf("eclb")]
        B_k2_ = [Buf("k2a"), Buf("k2b")]
        B_qtT_ = [Buf("qtTa"), Buf("qtTb")]
        B_ktT_ = [Buf("ktTa"), Buf("ktTb")]
        B_ATm = [Buf(f"ATm{h}") for h in range(4)]
        B_osb = [Buf(f"osb{h}") for h in range(4)]
        B_junk = Buf("junk")
        B_alo = Buf("aloT")
        WA = ([b for bb in B_mi for b in bb] + [B_zt, B_hi, B_lo, B_E1, B_Einv, B_ECL] + B_k2_ + B_qtT_ + B_ktT_
              + B_ATm + B_osb + [B_junk, B_alo])
        cst = Wr[:, 2048:4096].rearrange("p (b d) -> p b d", b=2)
        cstN = [Nr[:, 0:1024], Nr[:, 1024:2048]]
        CST = [cst[:, 0, :], cst[:, 1, :], cstN[0], cstN[1]]
        B_cst = [Buf(f"cst{i}") for i in range(4)]
        c_cst = [K.chan() for i in range(4)]
        Sbf4 = Wr[:, 2048:4096].bitcast(BF16).rearrange("p (i h v) -> p i h v", i=4, h=4)
        B_Sbf4 = [Buf(f"Sbf4_{i}") for i in range(4)]
        actb = Wr[:, 0:5632].bitcast(BF16).rearrange("p (j t) -> p j t", j=NFC)
        B_act = [Buf(f"act{j}") for j in range(NFC)]
        gsb = [Wr[:, 5632:5632 + 260], Wr[:, 5892:5892 + 260]]
        ccb = [Wr[:, 6152:6152 + 256], Wr[:, 6408:6408 + 256]]
        sgb = [Wr[:, 6664:6664 + 256], Wr[:, 6920:6920 + 256]]
        B_gs = [Buf("gs0"), Buf("gs1")]
        B_cc = [Buf("cc0"), Buf("cc1")]
        B_sg = [Buf("sg0"), Buf("sg1")]
        WB = B_act + B_gs + B_cc + B_sg

        c_x = [K.chan(), K.chan()]
        c_x2 = [K.chan(), K.chan()]
        c_y = [K.chan(out=True), K.chan(out=True)]

        plan = []
        state = {"cur": 0, "loaded": 0, "limit": 0}
        SLABS_PER_TILE = 50

        def tile_plan():
            l = []
            for j in range(12):
                l.append((win_b[j], 16, B_win[j // 2]))
            for j in range(4):
                l.append((wo_b[j], 16, B_wo))
            for sl in range(22):
                l.append((wup_b[sl], 16, wup_buf(sl)))
            for cb in range(4):
                for kgi in range(3):
                    l.append((wdn_b[cb * 3 + kgi], 16 if kgi < 2 else 12, B_wdn[cb]))
            return l

        NTILES_TOTAL = (2 * NTILE if dbg_tiles is None else len(dbg_tiles)) + (1 if with_sample else 0)
        for _ in range(NTILES_TOTAL):
            plan.extend(tile_plan())

        def pump(n):
            while state["loaded"] < min(n + NSLOT, len(plan), state["limit"]):
                m = state["loaded"]
                ap, nk, srcb = plan[m]
                K.dma(K.sp, c_slab[m % NSLOT], slabs[m % NSLOT][:, 0:nk, :], ap[:, 0:nk, :],
                      rd=[srcb], wr=[B_slab[m % NSLOT]])
                state["loaded"] += 1

        def next_slab():
            n = state["cur"]
            pump(n)
            state["cur"] += 1
            return slabs[n % NSLOT], B_slab[n % NSLOT]

        def next_slab_old():
            n = state["cur"]
            while state["loaded"] < min(n + NSLOT, len(plan)):
                m = state["loaded"]
                ap, nk, srcb = plan[m]
                K.dma(K.sp, c_slab[m % NSLOT], slabs[m % NSLOT][:, 0:nk, :], ap[:, 0:nk, :],
                      rd=[srcb], wr=[B_slab[m % NSLOT]])
                state["loaded"] += 1
            state["cur"] += 1
            return slabs[n % NSLOT], B_slab[n % NSLOT]

        pool_ok = [False]

        def pin_tables():
            pass

        B_ss = [Buf("ss0"), Buf("ss1")]
        B_ms = [Buf("ms0"), Buf("ms1")]
        B_sd = [Buf("sd0"), Buf("sd1")]
        B_rs = [Buf("rs0"), Buf("rs1")]
        B_ssq = [Buf("ssq0"), Buf("ssq1")]

        def rstd_s(s, np_, denom):
            V(lambda: nc.vector.tensor_scalar(out=ms[0:np_, s:s + 1], in0=ss[0:np_, s:s + 1], scalar1=1.0 / denom,
                                              scalar2=EPS, op0=ALU.mult, op1=ALU.add), rd=[B_ss[s]], wr=[B_ms[s]])
            if pool_ok[0]:
                G(lambda: nc.gpsimd.tensor_tensor(out=rstd[0:np_, s:s + 1], in0=ms[0:np_, s:s + 1],
                                                  in1=mhalf[0:np_, 0:1], op=ALU.pow),
                  rd=[B_ms[s]] + CONST, wr=[B_rs[s]])
            else:
                A(lambda: nc.scalar.activation(out=sd[0:np_, s:s + 1], in_=ms[0:np_, s:s + 1], func=AF.Sqrt),
                  rd=[B_ms[s]], wr=[B_sd[s]])
                V(lambda: nc.vector.reciprocal(out=rstd[0:np_, s:s + 1], in_=sd[0:np_, s:s + 1]),
                  rd=[B_sd[s]], wr=[B_rs[s]])

        def norm_transpose(nsub, np_, gcol, chunked=False):
            ntok = nsub * np_
            for s in range(nsub):
                if chunked:
                    A(lambda: nc.scalar.activation(out=actT[0:np_, 8 * s:8 * s + 8, :],
                                                   in_=xv[0:np_, s, :].rearrange("p (a b) -> p a b", a=8),
                                                   func=AF.Square, accum_out=ss[0:np_, s:s + 1]),
                      rd=[B_x[s]], wr=[B_ss[s]] + B_actT[8 * s:8 * s + 8])
                else:
                    A(lambda: nc.scalar.activation(out=nv[0:np_, s, :], in_=xv[0:np_, s, :], func=AF.Square,
                                                   accum_out=ss[0:np_, s:s + 1]),
                      rd=[B_x[s]], wr=[B_ss[s], B_n[s]])
            for s in range(nsub):
                rstd_s(s, np_, float(D))
            for s in range(nsub):
                V(lambda: nc.vector.tensor_scalar(out=nv[0:np_, s, :], in0=xv[0:np_, s, :], scalar1=rstd[0:np_, s:s + 1],
                                                  scalar2=None, op0=ALU.mult),
                  rd=[B_x[s], B_rs[s]], wr=[B_n[s]])
            for kc in range(NKC):
                ps, pb = K.bank()
                for s in range(nsub):
                    P(lambda: nc.tensor.transpose(out=ps[:, s * np_:(s + 1) * np_],
                                                  in_=nv[0:np_, s, kc * 128:(kc + 1) * 128],
                                                  identity=identf[0:np_, 0:np_]),
                      rd=[B_n[s]] + CONST, wr=[pb])
                AV(lambda: nc.scalar.mul(out=actT[:, kc, 0:ntok], in_=ps[:, 0:ntok], mul=gcol[:, kc:kc + 1]),
                   lambda: nc.vector.tensor_scalar(out=actT[:, kc, 0:ntok], in0=ps[:, 0:ntok],
                                                   scalar1=gcol[:, kc:kc + 1], scalar2=None, op0=ALU.mult),
                   rd=[pb] + CONST, wr=[B_actT[kc]])

        def fm_block(slab, sbuf, c0, ntok, ncol=128):
            ps, pb = K.bank()
            for kc in range(NKC):
                P(lambda: nc.tensor.matmul(ps[0:ncol, 0:ntok], lhsT=slab[:, kc, c0:c0 + ncol], rhs=actT[:, kc, 0:ntok],
                                           start=(kc == 0), stop=(kc == NKC - 1)),
                  rd=[sbuf, B_actT[kc]], wr=[pb])
            return ps, pb

        def tm_block(slab, sbuf, s, np_):
            ps, pb = K.bank()
            for kc in range(NKC):
                P(lambda: nc.tensor.matmul(ps[0:np_, 0:512], lhsT=actT[:, kc, s * np_:(s + 1) * np_],
                                           rhs=slab[:, kc, :], start=(kc == 0), stop=(kc == NKC - 1)),
                  rd=[sbuf, B_actT[kc]], wr=[pb])
            return ps, pb

        def copy_av(dst, src, rd, wr):
            AV(lambda: nc.scalar.copy(out=dst, in_=src),
               lambda: nc.vector.tensor_copy(out=dst, in_=src), rd=rd, wr=wr)

        def load_gpost(d_gp):
            K.dma(K.sp, c_gpost, gpost[:, :], d_gp[:, :], wr=[B_gpost])

        def evac_post(ps, pb, s, cb, np_, jk, jkb):
            A(lambda: nc.scalar.activation(out=jk[0:np_, 0:512], in_=ps[0:np_, :], func=AF.Square,
                                           accum_out=ssq[0:np_, s * 4 + cb:s * 4 + cb + 1]),
              rd=[pb], wr=[jkb, B_ssq[s]])
            V(lambda: nc.vector.tensor_tensor(out=nv[0:np_, s, cb * 512:(cb + 1) * 512], in0=ps[0:np_, :],
                                              in1=gpost[0:np_, cb * 512:(cb + 1) * 512], op=ALU.mult),
              rd=[pb, B_gpost], wr=[B_n[s]])

        def post_norm_residual(nsub, np_, into_n=False):
            for s in range(nsub):
                V(lambda: nc.vector.reduce_sum(out=ss[0:np_, s:s + 1], in_=ssq[0:np_, s * 4:(s + 1) * 4], axis=AX.X),
                  rd=[B_ssq[s]], wr=[B_ss[s]])
                rstd_s(s, np_, float(D))
            for s in range(nsub):
                dstv, dstb = (nv, B_n) if into_n else (xv, B_x)
                V(lambda: nc.vector.scalar_tensor_tensor(out=dstv[0:np_, s, :], in0=nv[0:np_, s, :],
                                                         scalar=rstd[0:np_, s:s + 1], in1=xv[0:np_, s, :],
                                                         op0=ALU.mult, op1=ALU.add),
                  rd=[B_n[s], B_rs[s], B_x[s]], wr=[dstb[s]])

        def attention_prompt(ti, extra=None):
            g0 = 2 * ti
            pending = []
            units_done = [0]

            def flush():
                (s, h, bsel, blks, pv) = pending.pop(0)
                pso, pbo = K.bank()
                for i, bk in enumerate(blks):
                    rp = bk % 6
                    P(lambda: nc.tensor.matmul(pso[:, 0:129], lhsT=pv[:, i * 128:(i + 1) * 128],
                                               rhs=vr[:, rp, h, 0:129], start=(i == 0), stop=(i == len(blks) - 1)),
                      rd=[B_pT[bsel], B_vr[rp][h // 4]], wr=[pbo])
                V(lambda: nc.vector.reciprocal(out=rinv[:, bsel:bsel + 1], in_=pso[:, 128:129]),
                  rd=[pbo], wr=[B_st["rinv"]])
                V(lambda: nc.vector.tensor_scalar(out=mix_in[:, s, h * 128:(h + 1) * 128], in0=pso[:, 0:128],
                                                  scalar1=rinv[:, bsel:bsel + 1], scalar2=None, op0=ALU.mult),
                  rd=[pbo, B_st["rinv"]], wr=[B_mi[s][h]])

            for s in range(2):
                g = g0 + s
                allb = [bk for bk in range(g - 4, g + 1) if bk >= 0]
                cst = [bk for bk in allb if 4 - (g - bk) in (1, 2)]
                var = [bk for bk in allb if 4 - (g - bk) not in (1, 2)]
                nvar, ncst = len(var), len(cst)
                blks = var + cst
                for h in range(8):
                    bsel = (s * 8 + h) % 2
                    psA, pbA = K.bank()
                    psB, pbB = K.bank()
                    for i, bk in enumerate(blks):
                        rp = bk % 6
                        if i < nvar:
                            dst, db = psB[:, i * 128:(i + 1) * 128], pbB
                        else:
                            dst, db = psA[:, (i - nvar) * 128:(i - nvar + 1) * 128], pbA
                        P(lambda: nc.tensor.matmul(dst, lhsT=kTr[:, h, rp * 128:(rp + 1) * 128],
                                                   rhs=qT[:, h, s * 128:(s + 1) * 128], start=True, stop=True),
                          rd=[B_kT[h][rp // 2], B_qT[h]], wr=[db])
                    scv = scb[bsel]
                    pv = pTb[bsel]
                    V(lambda: nc.vector.scalar_tensor_tensor(
                        out=scv[:, 0:nvar * 128], in0=psB[:, 0:nvar * 128], scalar=ATT_SCALE,
                        in1=biasT[:, h, (3 - nvar) * 128:384], op0=ALU.mult, op1=ALU.add),
                      rd=[pbB] + CONST, wr=[B_sc[bsel]])
                    if ncst:
                        A(lambda: nc.scalar.activation(out=pv[:, nvar * 128:(nvar + ncst) * 128],
                                                       in_=psA[:, 0:ncst * 128], func=AF.Exp,
                                                       bias=cbias[:, h:h + 1], scale=ATT_SCALE),
                          rd=[pbA] + CONST, wr=[B_pT[bsel]])
                    A(lambda: nc.scalar.activation(out=pv[:, 0:nvar * 128], in_=scv[:, 0:nvar * 128], func=AF.Exp),
                      rd=[B_sc[bsel]], wr=[B_pT[bsel]])
                    if pending:
                        flush()
                    pending.append((s, h, bsel, blks, pv))
                    units_done[0] += 1
                    if extra is not None:
                        extra(units_done[0])
            while pending:
                flush()

        def attention_sample():
            V(lambda: nc.vector.memset(pTs[:, :, :, :], 0.0), wr=B_pTs)
            V(lambda: nc.vector.memset(pTn[:, :, :], 0.0), wr=B_pTn)
            inherit(B_cst[2:4], B_kst + B_vst)
            ld = [0]
            pending = []

            def stage_load(src):
                cb_ = ld[0] % 4
                ld[0] += 1
                K.dma(K.sp, c_cst[cb_], CST[cb_], src, wr=[B_cst[cb_]])
                return cb_

            def flush():
                (i, h, bsel) = pending.pop(0)
                pso, pbo = K.bank()
                for bk in range(4):
                    P(lambda: nc.tensor.matmul(pso[0:64, 0:129], lhsT=pTs[:, i, bk, :], rhs=vr[:, bk, h, 0:129],
                                               start=(bk == 0), stop=False),
                      rd=[B_pTs[i], B_vr[bk][h // 4]], wr=[pbo])
                P(lambda: nc.tensor.matmul(pso[0:64, 0:129], lhsT=pTn[:, i, :], rhs=vr[0:64, 4, h, 0:129],
                                           start=False, stop=True),
                  rd=[B_pTn[i], B_vr[4][h // 4]], wr=[pbo])
                V(lambda: nc.vector.tensor_scalar(out=den[0:64, bsel:bsel + 1], in0=pso[0:64, 128:129],
                                                  scalar1=1e-30, scalar2=None, op0=ALU.max),
                  rd=[pbo], wr=[B_st["den"]])
                V(lambda: nc.vector.reciprocal(out=rinv[0:64, bsel:bsel + 1], in_=den[0:64, bsel:bsel + 1]),
                  rd=[B_st["den"]], wr=[B_st["rinv"]])
                if i == 0:
                    V(lambda: nc.vector.tensor_scalar(out=mix_in[0:64, 0, h * 128:(h + 1) * 128],
                                                      in0=pso[0:64, 0:128], scalar1=rinv[0:64, bsel:bsel + 1],
                                                      scalar2=None, op0=ALU.mult),
                      rd=[pbo, B_st["rinv"]], wr=[B_mi[0][h]])
                else:
                    V(lambda: nc.vector.scalar_tensor_tensor(out=mix_in[0:64, 0, h * 128:(h + 1) * 128],
                                                             in0=pso[0:64, 0:128], scalar=rinv[0:64, bsel:bsel + 1],
                                                             in1=mix_in[0:64, 0, h * 128:(h + 1) * 128],
                                                             op0=ALU.mult, op1=ALU.add),
                      rd=[pbo, B_st["rinv"], B_mi[0][h]], wr=[B_mi[0][h]])

            for i in range(4):
                for bk in range(4):
                    cb_ = stage_load(ck[i, bk * 128:(bk + 1) * 128, :])
                    for hq in range(2):
                        ps, pb = K.bank()
                        for hh in range(4):
                            h = hq * 4 + hh
                            P(lambda: nc.tensor.transpose(out=ps[:, hh * 128:(hh + 1) * 128],
                                                          in_=CST[cb_][:, h * 128:(h + 1) * 128],
                                                          identity=identf[:, :]),
                              rd=[B_cst[cb_]] + CONST, wr=[pb])
                        dst = kTr[:, hq * 4:hq * 4 + 4, bk * 128:(bk + 1) * 128]
                        src = ps[:, :].rearrange("p (h t) -> p h t", h=4)
                        copy_av(dst, src, [pb], [B_kT[hq * 4 + hh][bk // 2] for hh in range(4)])
                for bk in range(4):
                    cb_ = stage_load(cv[i, bk * 128:(bk + 1) * 128, :])
                    dst = vr[:, bk, :, 0:128]
                    src = CST[cb_].rearrange("p (h d) -> p h d", h=8)
                    copy_av(dst, src, [B_cst[cb_]], B_vr[bk])
                for h in range(8):
                    bsel = h % 2
                    psA, pbA = K.bank()
                    psB, pbB = K.bank()
                    for bk in range(4):
                        P(lambda: nc.tensor.matmul(psA[:, bk * 16:(bk + 1) * 16],
                                                   lhsT=kTr[:, h, bk * 128:(bk + 1) * 128],
                                                   rhs=qT[:, h, 16 * i:16 * i + 16], start=True, stop=True),
                          rd=[B_kT[h][bk // 2], B_qT[h]], wr=[pbA])
                    P(lambda: nc.tensor.matmul(psB[0:64, 0:16], lhsT=kTr[:, h, 512:576],
                                               rhs=qT[:, h, 16 * i:16 * i + 16], start=True, stop=True),
                      rd=[B_kT[h][2], B_qT[h]], wr=[pbB])
                    scv = scb[bsel]
                    V(lambda: nc.vector.scalar_tensor_tensor(
                        out=scv[:, 0:64].rearrange("p (b t) -> p b t", b=4),
                        in0=psA[:, 0:64].rearrange("p (b t) -> p b t", b=4), scalar=ATT_SCALE,
                        in1=biasS[:, h, :, :],
                        op0=ALU.mult, op1=ALU.add),
                      rd=[pbA] + CONST, wr=[B_sc[bsel]])
                    V(lambda: nc.vector.scalar_tensor_tensor(
                        out=scv[0:64, 64:80], in0=psB[0:64, 0:16], scalar=ATT_SCALE,
                        in1=biasN[:, h, 16 * i:16 * i + 16], op0=ALU.mult, op1=ALU.add),
                      rd=[pbB] + CONST, wr=[B_sc[bsel]])
                    if pending:
                        flush()
                    A(lambda: nc.scalar.activation(out=pTs[:, i, :, 16 * i:16 * i + 16],
                                                   in_=scv[:, 0:64].rearrange("p (b t) -> p b t", b=4), func=AF.Exp),
                      rd=[B_sc[bsel]], wr=[B_pTs[i]])
                    A(lambda: nc.scalar.activation(out=pTn[:, i, 16 * i:16 * i + 16], in_=scv[0:64, 64:80],
                                                   func=AF.Exp),
                      rd=[B_sc[bsel]], wr=[B_pTn[i]])
                    pending.append((i, h, bsel))
                while pending:
                    flush()
            inherit(B_kst + B_vst, B_cst[2:4])

        def gla_front(prompt, s, np_, par):
            k2, qtT, ktT = k2_[par], qtT_[par], ktT_[par]
            B_k2, B_qtT, B_ktT = B_k2_[par], B_qtT_[par], B_ktT_[par]
            eoff = par * 4 if prompt else 0
            nseq = 1 if prompt else 4
            if prompt:
                Umat = Ubf[:, :]
                Omat = onesbf[:, :]
                sel = onesbf[:, 0:1]
            else:
                Umat = smask_bf[0:64, 4:68]
                Omat = smask_bf[0:64, 68:132]
                sel = smask_bf[0:64, 0:4]
            ps, pb = K.bank()
            P(lambda: nc.tensor.matmul(ps[0:np_, 0:512], lhsT=aloT[0:16, s * np_:(s + 1) * np_], rhs=wg_bf[0:16, :],
                                       start=True, stop=True), rd=[B_alo] + CONST, wr=[pb])
            V(lambda: nc.vector.tensor_tensor(out=zt[0:np_, :], in0=ps[0:np_, 0:512], in1=brep[0:np_, :], op=ALU.add),
              rd=[pb] + CONST, wr=[B_zt])
            yield
            A(lambda: nc.scalar.activation(out=zt[0:np_, :], in_=zt[0:np_, :], func=AF.Exp, scale=-1.0),
              rd=[B_zt], wr=[B_zt])
            A(lambda: nc.scalar.activation(out=zt[0:np_, :], in_=zt[0:np_, :], func=AF.Ln, bias=oneT[0:np_, 0:1]),
              rd=[B_zt] + CONST, wr=[B_zt])
            yield
            V(lambda: nc.vector.tensor_copy(out=sp_hi[0:np_, :], in_=zt[0:np_, :]), rd=[B_zt], wr=[B_hi])
            V(lambda: nc.vector.tensor_tensor(out=sp_lo[0:np_, :], in0=zt[0:np_, :], in1=sp_hi[0:np_, :],
                                              op=ALU.subtract), rd=[B_zt, B_hi], wr=[B_lo])
            yield
            psc, pbc = K.bank()
            P(lambda: nc.tensor.matmul(psc[0:np_, 0:512], lhsT=Umat, rhs=sp_hi[0:np_, :], start=True, stop=False),
              rd=[B_hi] + CONST, wr=[pbc])
            P(lambda: nc.tensor.matmul(psc[0:np_, 0:512], lhsT=Umat, rhs=sp_lo[0:np_, :], start=False, stop=True),
              rd=[B_lo] + CONST, wr=[pbc])
            pst, pbt = K.bank()
            P(lambda: nc.tensor.matmul(pst[0:np_, 0:512], lhsT=Omat, rhs=sp_hi[0:np_, :], start=True, stop=False),
              rd=[B_hi] + CONST, wr=[pbt])
            P(lambda: nc.tensor.matmul(pst[0:np_, 0:512], lhsT=Omat, rhs=sp_lo[0:np_, :], start=False, stop=True),
              rd=[B_lo] + CONST, wr=[pbt])
            pse, pbe = K.bank()
            for h in range(4):
                P(lambda: nc.tensor.matmul(pse[:, h * nseq:(h + 1) * nseq], lhsT=sp_hi[0:np_, h * 128:(h + 1) * 128],
                                           rhs=sel, start=True, stop=False), rd=[B_hi] + CONST, wr=[pbe])
                P(lambda: nc.tensor.matmul(pse[:, h * nseq:(h + 1) * nseq], lhsT=sp_lo[0:np_, h * 128:(h + 1) * 128],
                                           rhs=sel, start=False, stop=True), rd=[B_lo] + CONST, wr=[pbe])
            yield
            A(lambda: nc.scalar.activation(out=E1[0:np_, :], in_=psc[0:np_, 0:512], func=AF.Exp, scale=-1.0 / 16.0),
              rd=[pbc], wr=[B_E1])
            A(lambda: nc.scalar.activation(out=Einv[0:np_, :], in_=psc[0:np_, 0:512], func=AF.Exp, scale=1.0 / 16.0),
              rd=[pbc], wr=[B_Einv])
            A(lambda: nc.scalar.activation(out=ECL[0:np_, :], in_=pst[0:np_, 0:512], func=AF.Exp, scale=-1.0 / 16.0),
              rd=[pbt], wr=[B_ECL])
            A(lambda: nc.scalar.activation(out=ecl[:, eoff:eoff + 4 * nseq], in_=pse[:, 0:4 * nseq], func=AF.Exp,
                                           scale=-1.0 / 16.0), rd=[pbe], wr=[B_ecl_[par]])
            yield
            V(lambda: nc.vector.tensor_tensor(out=E1[0:np_, :], in0=E1[0:np_, :], in1=qg[0:np_, s, :], op=ALU.mult),
              rd=[B_E1, B_qg[s]], wr=[B_E1])
            V(lambda: nc.vector.tensor_tensor(out=Einv[0:np_, :], in0=Einv[0:np_, :], in1=kg[0:np_, s, :], op=ALU.mult),
              rd=[B_Einv, B_kg[s]], wr=[B_Einv])
            V(lambda: nc.vector.tensor_tensor(out=k2[0:np_, :], in0=Einv[0:np_, :], in1=ECL[0:np_, :], op=ALU.mult),
              rd=[B_Einv, B_ECL], wr=[B_k2])
            yield
            psq, pbq = K.bank()
            psk, pbk = K.bank()
            for h in range(4):
                P(lambda: nc.tensor.transpose(out=psq[:, h * np_:(h + 1) * np_], in_=E1[0:np_, h * 128:(h + 1) * 128],
                                              identity=identf[0:np_, 0:np_]), rd=[B_E1] + CONST, wr=[pbq])
                P(lambda: nc.tensor.transpose(out=psk[:, h * np_:(h + 1) * np_], in_=Einv[0:np_, h * 128:(h + 1) * 128],
                                              identity=identf[0:np_, 0:np_]), rd=[B_Einv] + CONST, wr=[pbk])
            A(lambda: nc.scalar.copy(out=qtT[:, :, 0:np_], in_=psq[:, 0:4 * np_].rearrange("p (h t) -> p h t", h=4)),
              rd=[pbq], wr=[B_qtT])
            V(lambda: nc.vector.tensor_copy(out=ktT[:, :, 0:np_],
                                            in_=psk[:, 0:4 * np_].rearrange("p (h t) -> p h t", h=4)),
              rd=[pbk], wr=[B_ktT])
            yield
            if not prompt:
                V(lambda: nc.vector.memset(qtTm[:, :, :, :], 0.0), wr=[B_qtTm])
                for i in range(4):
                    V(lambda: nc.vector.tensor_copy(out=qtTm[:, i, :, 16 * i:16 * i + 16],
                                                    in_=qtT[:, :, 16 * i:16 * i + 16]),
                      rd=[B_qtT], wr=[B_qtTm])
                    V(lambda: nc.vector.tensor_scalar(out=k2m[i // 2][:, i % 2, :], in0=k2[0:64, :],
                                                      scalar1=smask[0:64, i:i + 1], scalar2=None, op0=ALU.mult),
                      rd=[B_k2] + CONST, wr=[B_k2m[i // 2]])
            yield

        def gla_back(prompt, s, np_, par, hook=None):
            nseq = 1 if prompt else 4
            Umat = Ubf[:, :] if prompt else smask_bf[0:64, 4:68]
            k2, qtT, ktT = k2_[par], qtT_[par], ktT_[par]
            B_k2, B_qtT, B_ktT = B_k2_[par], B_qtT_[par], B_ktT_[par]
            eoff = par * 4 if prompt else 0
            bankA = []
            for h in range(4):
                psa, pba = K.bank()
                bankA.append((psa, pba))
                P(lambda: nc.tensor.matmul(psa[0:np_, 0:np_], lhsT=ktT[:, h, 0:np_], rhs=qtT[:, h, 0:np_],
                                           start=True, stop=True), rd=[B_ktT, B_qtT], wr=[pba])
            for h in range(4):
                psa, pba = bankA[h]
                V(lambda: nc.vector.tensor_tensor(out=ATm[0:np_, h, 0:np_], in0=psa[0:np_, 0:np_], in1=Umat,
                                                  op=ALU.mult), rd=[pba] + CONST, wr=[B_ATm[h]])
            if hook is not None:
                hook()
            bankO = []
            for h in range(4):
                pso, pbo = K.bank()
                bankO.append((pso, pbo))
                P(lambda: nc.tensor.matmul(pso[0:np_, 0:256], lhsT=ATm[0:np_, h, 0:np_],
                                           rhs=vg[0:np_, s, h * 256:(h + 1) * 256], start=True, stop=False),
                  rd=[B_ATm[h], B_vg[s][h // 2]], wr=[pbo])
                if prompt:
                    P(lambda: nc.tensor.matmul(pso[0:np_, 0:256], lhsT=qtT[:, h, 0:np_], rhs=Sbf[:, h, :],
                                               start=False, stop=True), rd=[B_qtT, B_Sbf[h]], wr=[pbo])
                else:
                    for i in range(4):
                        P(lambda: nc.tensor.matmul(pso[0:np_, 0:256], lhsT=qtTm[:, i, h, :], rhs=Sbf4[:, i, h, :],
                                                   start=False, stop=(i == 3)), rd=[B_qtTm, B_Sbf4[i]], wr=[pbo])
            for h in range(4):
                pso, pbo = bankO[h]
                V(lambda: nc.vector.tensor_copy(out=o_sb[0:np_, h, :], in_=pso[0:np_, 0:256]), rd=[pbo], wr=[B_osb[h]])
                A(lambda: nc.scalar.activation(out=junk[0:np_, 0:256], in_=o_sb[0:np_, h, :], func=AF.Square,
                                               accum_out=oss[0:np_, h:h + 1]), rd=[B_osb[h]], wr=[B_junk, B_st["oss"]])
            if hook is not None:
                hook()
            if prompt:
                bankU = []
                for h in range(4):
                    psu, pbu = K.bank()
                    bankU.append((psu, pbu))
                    P(lambda: nc.tensor.matmul(psu[:, 0:256], lhsT=k2[0:np_, h * 128:(h + 1) * 128],
                                               rhs=vg[0:np_, s, h * 256:(h + 1) * 256], start=True, stop=True),
                      rd=[B_k2, B_vg[s][h // 2]], wr=[pbu])
                for h in range(4):
                    psu, pbu = bankU[h]
                    V(lambda: nc.vector.scalar_tensor_tensor(out=Sst[:, h, :], in0=Sst[:, h, :],
                                                             scalar=ecl[:, eoff + h:eoff + h + 1],
                                                             in1=psu[:, 0:256], op0=ALU.mult, op1=ALU.add),
                      rd=[B_S[h], B_ecl_[par], pbu], wr=[B_S[h]])
                    A(lambda: nc.scalar.copy(out=Sbf[:, h, :], in_=Sst[:, h, :]), rd=[B_S[h]], wr=[B_Sbf[h]])
            if hook is not None:
                hook()
                hook()
            V(lambda: nc.vector.tensor_scalar(out=oms[0:np_, :], in0=oss[0:np_, :], scalar1=1.0 / 256.0, scalar2=EPS,
                                              op0=ALU.mult, op1=ALU.add), rd=[B_st["oss"]], wr=[B_st["oms"]])
            if pool_ok[0]:
                G(lambda: nc.gpsimd.tensor_tensor(out=orstd[0:np_, :], in0=oms[0:np_, :], in1=mhalf[0:np_, 0:4],
                                                  op=ALU.pow), rd=[B_st["oms"]] + CONST, wr=[B_st["orstd"]])
            else:
                A(lambda: nc.scalar.activation(out=osd[0:np_, :], in_=oms[0:np_, :], func=AF.Sqrt),
                  rd=[B_st["oms"]], wr=[B_st["osd"]])
                V(lambda: nc.vector.reciprocal(out=orstd[0:np_, :], in_=osd[0:np_, :]), rd=[B_st["osd"]],
                  wr=[B_st["orstd"]])
            for h in range(4):
                mb = [B_mi[s][8 + 2 * h], B_mi[s][9 + 2 * h]]
                msl = mix_in[0:np_, s, 1024 + h * 256:1024 + (h + 1) * 256]
                tq = junk[0:np_, :].bitcast(F32)
                A(lambda: nc.scalar.activation(out=tq, in_=msl, func=AF.Tanh, scale=0.5), rd=mb, wr=[B_junk])
                V(lambda: nc.vector.scalar_tensor_tensor(out=msl, in0=tq, scalar=1.0, in1=msl,
                                                         op0=ALU.add, op1=ALU.mult),
                  rd=[B_junk] + mb, wr=mb)
                V(lambda: nc.vector.scalar_tensor_tensor(out=o_sb[0:np_, h, :], in0=o_sb[0:np_, h, :],
                                                         scalar=orstd[0:np_, h:h + 1], in1=ggla[0:np_, :],
                                                         op0=ALU.mult, op1=ALU.mult),
                  rd=[B_osb[h], B_st["orstd"]] + CONST, wr=[B_osb[h]])
                V(lambda: nc.vector.scalar_tensor_tensor(out=msl, in0=msl, scalar=0.5, in1=o_sb[0:np_, h, :],
                                                         op0=ALU.mult, op1=ALU.mult),
                  rd=[B_osb[h]] + mb, wr=mb)

        def gla_sample_state_update():
            for i in range(4):
                K.dma(K.sp, c_Sin, Sst[:, :, :], sg[i].rearrange("h d v -> d h v"), wr=B_S)
                for h in range(4):
                    psu, pbu = K.bank()
                    P(lambda: nc.tensor.matmul(psu[:, 0:256], lhsT=k2m[i // 2][:, i % 2, h * 128:(h + 1) * 128],
                                               rhs=vg[0:64, 0, h * 256:(h + 1) * 256], start=True, stop=True),
                      rd=[B_k2m[i // 2], B_vg[0][h // 2]], wr=[pbu])
                    V(lambda: nc.vector.scalar_tensor_tensor(out=Sst[:, h, :], in0=Sst[:, h, :],
                                                             scalar=ecl[:, h * 4 + i:h * 4 + i + 1],
                                                             in1=psu[:, 0:256], op0=ALU.mult, op1=ALU.add),
                      rd=[B_S[h], B_ecl_[0], pbu], wr=[B_S[h]])
                K.dma(K.act, c_S, gs[i].rearrange("h d v -> d h v"), Sst[:, :, :], rd=B_S)

        def run_tile(kind, b, ti):
            prompt = (kind == "p")
            nsub, np_ = (2, 128) if prompt else (1, 64)
            ntok = nsub * np_
            rslot = ti % 3
            kcol0 = rslot * 256
            emit_kv = (prompt and ti >= NTILE - 2) or (not prompt)

            def dbg_dump(ph):
                for s in range(nsub):
                    dst = yp[b, ti * TT + s * 128: ti * TT + (s + 1) * 128, :] if prompt else ys[:, :]
                    if ph < 4:
                        K.dma(K.act, c_y[s], dst, mix_in[0:np_, s, :], rd=WA)
                    else:
                        K.dma(K.act, c_y[s], dst, xv[0:np_, s, :], rd=[B_x[s]])

            for s in range(nsub):
                src = xp[b, ti * TT + s * 128: ti * TT + (s + 1) * 128, :] if prompt else xs[:, :]
                K.dma(K.sp, c_x[s], xv[0:np_, s, :], src, wr=[B_x[s]])
            state["limit"] += SLABS_PER_TILE
            pump(state["cur"])
            norm_transpose(nsub, np_, gpre1, chunked=True)
            inherit(XB + B_k2m + B_pTs, XA)
            inherit(NB, NA)
            inherit(WA + B_cst + B_Sbf4, WB)

            if dbg_stop == 0:
                return dbg_dump(0)
            kv_i = 0
            for j in range(4):
                slab, sbuf = next_slab()
                for hb in range(4):
                    h = (j % 2) * 4 + hb
                    ps, pb = fm_block(slab, sbuf, hb * 128, ntok)
                    if j < 2:
                        dst, dbuf = qT[:, h, 0:ntok], B_qT[h]
                    elif prompt:
                        dst, dbuf = kTr[:, h, kcol0:kcol0 + ntok], B_kT[h][rslot]
                    else:
                        dst, dbuf = kTr[:, h, 512:512 + ntok], B_kT[h][2]
                    copy_av(dst, ps[:, 0:ntok], [pb], [dbuf])
                if j >= 2 and emit_kv:
                    for s in range(nsub):
                        ps, pb = tm_block(slab, sbuf, s, np_)
                        bs = kv_i % 2
                        kv_i += 1
                        copy_av(kstb[bs][0:np_, :], ps[0:np_, :], [pb], [B_kst[bs]])
                        c0 = (j - 2) * 512
                        if prompt:
                            r0 = (ti - (NTILE - 2)) * TT + s * 128
                            dst = kp[b, r0:r0 + 128, c0:c0 + 512]
                        else:
                            dst = ksn[:, c0:c0 + 512]
                        K.dma(K.act, c_kst[bs], dst, kstb[bs][0:np_, :], rd=[B_kst[bs]])
            for j in range(2):
                slab, sbuf = next_slab()
                for s in range(nsub):
                    ps, pb = tm_block(slab, sbuf, s, np_)
                    blk = (2 * ti + s) % 6 if prompt else 4
                    dst = vr[0:np_, blk, 4 * j:4 * j + 4, 0:128]
                    src = ps[0:np_, :].rearrange("p (h d) -> p h d", h=4)
                    copy_av(dst, src, [pb], [B_vr[blk][j]])
                    if emit_kv:
                        bs = kv_i % 2
                        kv_i += 1
                        copy_av(vstb[bs][0:np_, :], ps[0:np_, :], [pb], [B_vst[bs]])
                        c0 = j * 512
                        if prompt:
                            r0 = (ti - (NTILE - 2)) * TT + s * 128
                            dst2 = vp[b, r0:r0 + 128, c0:c0 + 512]
                        else:
                            dst2 = vsn[:, c0:c0 + 512]
                        K.dma(K.act, c_vst[bs], dst2, vstb[bs][0:np_, :], rd=[B_vst[bs]])
            for j in range(2):
                slab, sbuf = next_slab()
                for s in range(nsub):
                    ps, pb = tm_block(slab, sbuf, s, np_)
                    if j == 0:
                        A(lambda: nc.scalar.mul(out=qg[0:np_, s, :], in_=ps[0:np_, :], mul=GLA_QSCALE),
                          rd=[pb], wr=[B_qg[s]])
                    else:
                        V(lambda: nc.vector.tensor_copy(out=kg[0:np_, s, :], in_=ps[0:np_, :]), rd=[pb], wr=[B_kg[s]])
            for j in range(2):
                slab, sbuf = next_slab()
                for s in range(nsub):
                    ps, pb = tm_block(slab, sbuf, s, np_)
                    copy_av(vg[0:np_, s, j * 512:(j + 1) * 512], ps[0:np_, :], [pb], [B_vg[s][j]])
            ps, pb = K.bank()
            for kc in range(NKC):
                P(lambda: nc.tensor.matmul(ps[0:16, 0:ntok], lhsT=wlo[:, kc, :], rhs=actT[:, kc, 0:ntok],
                                           start=(kc == 0), stop=(kc == NKC - 1)),
                  rd=[B_actT[kc]] + CONST, wr=[pb])
            V(lambda: nc.vector.tensor_copy(out=aloT[0:16, 0:ntok], in_=ps[0:16, 0:ntok]), rd=[pb], wr=[B_alo])
            load_gpost(d_gpost1)

            if dbg_stop == 1:
                return dbg_dump(1)
            rg = {}
            RGS = [(0, 0), (0, 1), (1, 0), (1, 1)]

            def rg_piece(blk, piece, npieces):
                j, s = RGS[blk]
                if piece == 0:
                    if s == 0:
                        rg["slab"] = next_slab()
                    rg["bank"] = K.pin_bank()
                slab, sbuf = rg["slab"]
                bi, ps, pb = rg["bank"]
                per = NKC // npieces
                for kc in range(piece * per, (piece + 1) * per):
                    P(lambda: nc.tensor.matmul(ps[0:np_, 0:512], lhsT=actT[:, kc, s * np_:(s + 1) * np_],
                                               rhs=slab[:, kc, :], start=(kc == 0), stop=(kc == NKC - 1)),
                      rd=[sbuf, B_actT[kc]], wr=[pb])
                if piece == npieces - 1:
                    copy_av(mix_in[0:np_, s, 1024 + j * 512:1024 + (j + 1) * 512], ps[0:np_, :], [pb],
                            B_mi[s][8 + 4 * j:12 + 4 * j])
                    K.unpin_bank(bi)

            def mix_transposes(kcs):
                for kc in kcs:
                    ps, pb = K.bank()
                    for s in range(nsub):
                        P(lambda: nc.tensor.transpose(out=ps[:, s * np_:(s + 1) * np_],
                                                      in_=mix_in[0:np_, s, kc * 128:(kc + 1) * 128],
                                                      identity=identf[0:np_, 0:np_]),
                          rd=[B_mi[s][kc]] + CONST, wr=[pb])
                    copy_av(actT[:, kc, 0:ntok], ps[:, 0:ntok], [pb], [B_actT[kc]])

            if prompt:
                if ti == 0:
                    V(lambda: nc.vector.memset(Sst[:, :, :], 0.0), wr=B_S)
                    V(lambda: nc.vector.memset(Sbf[:, :, :], 0.0), wr=B_Sbf)
                g0 = gla_front(True, 0, np_, 0)

                def extra(u):
                    rg_piece((u - 1) // 4, (u - 1) % 4, 4)
                    next(g0, None)

                attention_prompt(ti, extra=extra)
                if dbg_stop == 2:
                    return dbg_dump(2)
                for _ in g0:
                    pass
                mix_transposes(range(0, 8))
                g1 = gla_front(True, 1, np_, 1)

                def hook():
                    next(g1, None)
                    next(g1, None)

                gla_back(True, 0, np_, 0, hook=hook)
                for _ in g1:
                    pass
                gla_back(True, 1, np_, 1)
                if ti == NTILE - 1:
                    K.dma(K.act, c_S, gp[b].rearrange("h d v -> d h v"), Sst[:, :, :], rd=B_S)
            else:
                rg_piece(0, 0, 1)
                rg_piece(2, 0, 1)
                attention_sample()
                inherit(B_Sbf4, B_cst)
                for i in range(4):
                    K.dma(K.sp, c_Sin, Sst[:, :, :], sg[i].rearrange("h d v -> d h v"), wr=B_S)
                    V(lambda: nc.vector.tensor_copy(out=Sbf4[:, i, :, :], in_=Sst[:, :, :]), rd=B_S, wr=[B_Sbf4[i]])
                for _ in gla_front(False, 0, np_, 0):
                    pass
                gla_back(False, 0, np_, 0)
                gla_sample_state_update()
            if dbg_stop == 3.5:
                return dbg_dump(3.5)
            mix_transposes(range(8, NKC) if prompt else range(NKC))
            inherit(XA, XB + B_k2m + B_pTs)
            inherit(NA, NB)
            for s in range(nsub):
                src = xp[b, ti * TT + s * 128: ti * TT + (s + 1) * 128, :] if prompt else xs[:, :]
                K.dma(K.act, c_x2[s], xv[0:np_, s, :], src, wr=[B_x[s]])
            for cb in range(4):
                slab, sbuf = next_slab()
                for s in range(nsub):
                    ps, pb = tm_block(slab, sbuf, s, np_)
                    evac_post(ps, pb, s, cb, np_, junk, B_junk)
            post_norm_residual(nsub, np_)

            if dbg_stop == 4:
                return dbg_dump(4)
            norm_transpose(nsub, np_, gpre2)
            inherit(WB, WA + B_cst + B_Sbf4)
            inherit(NC_, NA)

            if dbg_stop == 5:
                return dbg_dump(5)
            load_gpost(d_gpost2)
            nseq, L = (1, 256) if prompt else (4, 16)
            if prompt and ti == 0:
                V(lambda: nc.vector.memset(carry[:, :, :], 0.0), wr=B_carry)
            if not prompt:
                for c in range(11):
                    bs = c % 2
                    K.dma(K.sp, c_cvin[bs], cvb[bs], sc[:, c * 512:(c + 1) * 512], wr=[B_cvb[bs]])
                    ps, pb = K.bank()
                    for q in range(4):
                        P(lambda: nc.tensor.transpose(out=ps[:, q * 8:(q + 1) * 8], in_=cvb[bs][:, q * 128:(q + 1) * 128],
                                                      identity=identf[0:8, 0:8]), rd=[B_cvb[bs]] + CONST, wr=[pb])
                    V(lambda: nc.vector.tensor_copy(out=carry[:, 4 * c:4 * c + 4, :],
                                                    in_=ps[:, 0:32].rearrange("p (q r) -> p q r", q=4)),
                      rd=[pb], wr=B_carry[4 * c:4 * c + 4])
            for sl in range(22):
                gslab, gsbuf = next_slab()
                for fb in range(2):
                    j = sl * 2 + fb
                    bs = j % 2
                    psg, pbg = fm_block(gslab, gsbuf, fb * 128, ntok)
                    psv, pbv = fm_block(gslab, gsbuf, 256 + fb * 128, ntok)
                    gsv = gsb[bs][:, 0:nseq * (L + 2)].rearrange("p (i t) -> p i t", i=nseq)
                    ccv = ccb[bs][:, 0:ntok].rearrange("p (i t) -> p i t", i=nseq)
                    sgv = sgb[bs][:, 0:ntok].rearrange("p (i t) -> p i t", i=nseq)
                    V(lambda: nc.vector.tensor_copy(out=gsv[:, :, 0:2],
                                                    in_=carry[:, j, 0:2 * nseq].rearrange("p (i r) -> p i r", i=nseq)),
                      rd=[B_carry[j]], wr=[B_gs[bs]])
                    A(lambda: nc.scalar.copy(out=gsv[:, :, 2:L + 2],
                                             in_=psg[:, 0:ntok].rearrange("p (i t) -> p i t", i=nseq)),
                      rd=[pbg], wr=[B_gs[bs]])
                    A(lambda: nc.scalar.activation(out=ccv, in_=psg[:, 0:ntok].rearrange("p (i t) -> p i t", i=nseq),
                                                   func=AF.Identity, scale=convw[:, j, 2:3], bias=convw[:, j, 3:4]),
                      rd=[pbg] + CONST, wr=[B_cc[bs]])
                    V(lambda: nc.vector.tensor_copy(out=carry[:, j, 0:2 * nseq].rearrange("p (i r) -> p i r", i=nseq),
                                                    in_=gsv[:, :, L:L + 2]),
                      rd=[B_gs[bs]], wr=[B_carry[j]])
                    V(lambda: nc.vector.scalar_tensor_tensor(out=ccv, in0=gsv[:, :, 1:L + 1], scalar=convw[:, j, 1:2],
                                                             in1=ccv, op0=ALU.mult, op1=ALU.add),
                      rd=[B_gs[bs], B_cc[bs]] + CONST, wr=[B_cc[bs]])
                    V(lambda: nc.vector.scalar_tensor_tensor(out=ccv, in0=gsv[:, :, 0:L], scalar=convw[:, j, 0:1],
                                                             in1=ccv, op0=ALU.mult, op1=ALU.add),
                      rd=[B_gs[bs], B_cc[bs]] + CONST, wr=[B_cc[bs]])
                    A(lambda: nc.scalar.activation(out=sgv, in_=ccv, func=AF.Silu), rd=[B_cc[bs]], wr=[B_sg[bs]])
                    V(lambda: nc.vector.tensor_tensor(out=actb[:, j, 0:ntok], in0=psv[:, 0:ntok], in1=sgb[bs][:, 0:ntok],
                                                      op=ALU.mult), rd=[pbv, B_sg[bs]], wr=[B_act[j]])
            pin_tables()
            if (prompt and ti == NTILE - 1) or not prompt:
                nr = 2 * nseq
                for c in range(11):
                    bs = c % 2
                    ps, pb = K.bank()
                    for q in range(4):
                        P(lambda: nc.tensor.transpose(out=ps[0:nr, q * 128:(q + 1) * 128], in_=carry[:, 4 * c + q, 0:nr],
                                                      identity=identf[:, :]), rd=[B_carry[4 * c + q]] + CONST, wr=[pb])
                    V(lambda: nc.vector.tensor_copy(out=cvb[bs][0:nr, :], in_=ps[0:nr, 0:512]), rd=[pb], wr=[B_cvb[bs]])
                    dst = cp[2 * b:2 * b + 2, c * 512:(c + 1) * 512] if prompt else cs[:, c * 512:(c + 1) * 512]
                    K.dma(K.act, c_cvb[bs], dst, cvb[bs][0:nr, :], rd=[B_cvb[bs]])
            inherit(NA, NC_)

            if dbg_stop == 6:
                return dbg_dump(6)
            for cb in range(4):
                banks = [K.bank() for s in range(nsub)]
                for kgi in range(3):
                    nk = 16 if kgi < 2 else 12
                    slab, sbuf = next_slab()
                    for s in range(nsub):
                        ps, pb = banks[s]
                        for kc in range(nk):
                            fc = kgi * 16 + kc
                            P(lambda: nc.tensor.matmul(ps[0:np_, 0:512], lhsT=actb[:, fc, s * np_:(s + 1) * np_],
                                                       rhs=slab[:, kc, :], start=(fc == 0), stop=(fc == NFC - 1)),
                              rd=[sbuf, B_act[fc]], wr=[pb])
                for s in range(nsub):
                    ps, pb = banks[s]
                    evac_post(ps, pb, s, cb, np_, junk_b, B_junkb)
            post_norm_residual(nsub, np_, into_n=True)
            for s in range(nsub):
                dst = yp[b, ti * TT + s * 128: ti * TT + (s + 1) * 128, :] if prompt else ys[:, :]
                K.dma(K.act, c_y[s], dst, nv[0:np_, s, :], rd=[B_n[s]])

        junk_b = sb("junk_b", [128, 512], BF16)
        B_junkb = Buf("junkb")

        pin_tables()
        for b in range(2):
            for ti in range(NTILE):
                if dbg_tiles is not None and (b, ti) not in dbg_tiles:
                    continue
                run_tile("p", b, ti)
                pool_ok[0] = True
        if with_sample:
            run_tile("s", 0, 0)

        for c in K.out_chans:
            if c.cnt:
                nc.sync.wait_ge(c.sem, c.cnt)
    return nc


def _host_tables(rel_bias):
    rb = np.asarray(rel_bias, np.float32)[0]
    kl = np.arange(128)[:, None, None]
    blk = np.arange(5)[None, :, None]
    q = np.arange(128)[None, None, :]
    d = 512 - 128 * blk + q - kl
    idx = np.clip(d, -128, 128) + 128
    tab = rb[:, idx]
    tab = np.ascontiguousarray(np.transpose(tab, (1, 0, 2, 3))).copy()
    maskA = (q >= 64) & (kl < 64)
    maskB = (q < 64) & (kl >= 64)
    tab[:, :, 0, :][np.broadcast_to(maskA[:, 0, :][:, None, :], (128, 8, 128))] = MASKV
    tab[:, :, 4, :][np.broadcast_to(maskB[:, 0, :][:, None, :], (128, 8, 128))] = MASKV
    biasS = np.ascontiguousarray(tab[:, :, 0:4, 0:16]).reshape(128, 8 * 64)
    tab = np.ascontiguousarray(tab[:, :, [0, 3, 4, 1, 2], :])
    biasT = tab.reshape(128, 8 * 640)
    kk = np.arange(64)[:, None]
    qq = np.arange(64)[None, :]
    dn = (qq % 16) - (kk % 16)
    tn = rb[:, dn + 128]
    tn = np.ascontiguousarray(np.transpose(tn, (1, 0, 2))).copy()
    cross = (kk // 16) != (qq // 16)
    tn[np.broadcast_to(cross[:, None, :], (64, 8, 64))] = MASKV
    biasN = tn.reshape(64, 8 * 64)
    t = np.arange(64)
    seqsel = (t[:, None] // 16 == np.arange(4)[None, :]).astype(np.float32)
    same = (t[:, None] // 16 == t[None, :] // 16)
    Ub = (same & (t[:, None] <= t[None, :])).astype(np.float32)
    onesb = same.astype(np.float32)
    smask = np.concatenate([seqsel, Ub, onesb], axis=1).astype(np.float32)
    return biasT.astype(np.float32), biasN.astype(np.float32), smask, biasS.astype(np.float32)


_NC_CACHE = {}


def kernel(x_prompt, x_sample, cache_k, cache_v, state_gla, state_conv, g_mix_pre, w_in, w_gate_up,
           b_gate, rel_bias, g_gla, w_o, g_mix_post, g_ffn_pre, w_up, w_conv, b_conv, w_down, g_ffn_post):
    f = lambda a: np.ascontiguousarray(np.asarray(a, dtype=np.float32))
    x_prompt, x_sample = f(x_prompt), f(x_sample)
    cache_k, cache_v = f(cache_k)[0], f(cache_v)[0]
    state_gla, state_conv = f(state_gla)[0], f(state_conv)[0]
    biasT, biasN, smask, biasS = _host_tables(rel_bias)
    colmajor = lambda g: np.ascontiguousarray(f(g)[0].reshape(16, 128).T)
    rep = lambda v, n=128: np.ascontiguousarray(np.broadcast_to(f(v).reshape(1, -1), (n, f(v).size)))
    wc = f(w_conv)[0]
    bc = f(b_conv)[0]
    convw = np.stack([wc[0], wc[1], wc[2], bc], axis=-1)
    convw = np.ascontiguousarray(convw.reshape(NFC, 128, 4).transpose(1, 0, 2).reshape(128, NFC * 4))
    shared = {
        "w_in": f(w_in)[0], "w_o": f(w_o)[0], "w_up": f(w_up)[0], "w_down": f(w_down)[0],
        "gpre1": colmajor(g_mix_pre), "gpre2": colmajor(g_ffn_pre),
        "gpost1": rep(g_mix_post), "gpost2": rep(g_ffn_post),
        "brep": rep(b_gate), "ggla": rep(g_gla), "wg": f(w_gate_up)[0],
        "convw": convw, "biasT": biasT, "biasN": biasN, "smask": smask, "biasS": biasS,
        "cbias": np.ascontiguousarray(np.broadcast_to(f(rel_bias)[0][:, 256].reshape(1, 8), (128, 8))),
    }
    in_maps = []
    for c in range(NCORES):
        m = dict(shared)
        m["xp"] = x_prompt[2 * c:2 * c + 2]
        m["xs"] = np.ascontiguousarray(x_sample[4 * c:4 * c + 4].reshape(64, D))
        m["ck"] = np.ascontiguousarray(cache_k[4 * c:4 * c + 4].reshape(4, 512, 1024))
        m["cv"] = np.ascontiguousarray(cache_v[4 * c:4 * c + 4].reshape(4, 512, 1024))
        m["sg"] = state_gla[4 * c:4 * c + 4]
        m["sc"] = np.ascontiguousarray(state_conv[4 * c:4 * c + 4].reshape(8, DFF))
        in_maps.append(m)
    if "nc" not in _NC_CACHE:
        _NC_CACHE["nc"] = build_program()
    nc = _NC_CACHE["nc"]
    res = run_bass_kernel_spmd(nc, in_maps, core_ids=list(range(NCORES)))
    R = res.results
    cat = lambda k: np.concatenate([np.asarray(r[k]) for r in R], axis=0)
    y_prompt = cat("yp").reshape(16, SEQ, D)
    y_sample = cat("ys").reshape(32, 16, D)
    k_prompt = cat("kp").reshape(1, 16, 512, 8, 128)
    v_prompt = cat("vp").reshape(1, 16, 512, 8, 128)
    gla_prompt = cat("gp").reshape(1, 16, 4, 128, 256)
    conv_prompt = cat("cp").reshape(1, 16, 2, DFF)
    k_sample = cat("ksn").reshape(1, 32, 16, 8, 128)
    v_sample = cat("vsn").reshape(1, 32, 16, 8, 128)
    gla_sample = cat("gs").reshape(1, 32, 4, 128, 256)
    conv_sample = cat("cs").reshape(1, 32, 2, DFF)
    return (y_prompt.astype(np.float32), y_sample.astype(np.float32), k_prompt, v_prompt, gla_prompt,
            conv_prompt, k_sample, v_sample, gla_sample, conv_sample)
```

```python
import numpy as np
from contextlib import ExitStack
import concourse.bass as bass
import concourse.mybir as mybir
from concourse.bass_utils import run_bass_kernel_spmd

F32 = mybir.dt.float32
BF16 = mybir.dt.bfloat16
AF = mybir.ActivationFunctionType
ALU = mybir.AluOpType
AX = mybir.AxisListType

NCORES = 8
D = 2048
NKC = 16
SEQ = 2048
TT = 256
NTILE = SEQ // TT
DFF = 5632
NFC = 44
INW = 6160
EPS = 1e-6
MASKV = -30000.0
ATT_SCALE = 128.0 ** -0.5
GLA_QSCALE = 128.0 ** -0.5
NSLOT = 3
WITH_SAMPLE = True


class Buf:
    __slots__ = ("name", "wr", "rd", "excl")

    def __init__(self, name, excl=False):
        self.name = name
        self.wr = {}
        self.rd = {}
        self.excl = excl


class Eng:
    def __init__(self, name, h, sem):
        self.name = name
        self.h = h
        self.sem = sem
        self.cnt = 0
        self.seen = {}


class Chan:
    def __init__(self, sem):
        self.sem = sem
        self.cnt = 0


def inherit(new_bufs, old_bufs):
    allev = {}
    for b in old_bufs:
        for d in (b.wr, b.rd):
            for k, (sem, val) in d.items():
                if allev.get(k, (None, 0))[1] < val:
                    allev[k] = (sem, val)
    for b in new_bufs:
        b.wr = {}
        b.rd = dict(allev)


class KB:
    def __init__(self, nc, es):
        self.nc = nc
        self.es = es
        self.nsem = 0
        self.pe = Eng("pe", nc.tensor, self.sem())
        self.act = Eng("act", nc.scalar, self.sem())
        self.dve = Eng("dve", nc.vector, self.sem())
        self.pool = Eng("pool", nc.gpsimd, self.sem())
        self.sp = Eng("sp", nc.sync, None)
        self.out_chans = []
        self.ps = [es.enter_context(nc.psum_tensor(f"ps{i}", [128, 512], F32)) for i in range(8)]
        self.pb = [Buf(f"ps{i}", excl=True) for i in range(8)]
        self.rot = list(range(8))
        self.flip = 0

    def sem(self):
        self.nsem += 1
        return self.es.enter_context(self.nc.semaphore(f"sem{self.nsem}"))

    def chan(self, out=False):
        c = Chan(self.sem())
        if out:
            self.out_chans.append(c)
        return c

    def sb(self, name, shape, dt):
        return self.es.enter_context(self.nc.sbuf_tensor("sb_" + name, shape, dt))

    def bank(self):
        i = self.rot.pop(0)
        self.rot.append(i)
        return self.ps[i], self.pb[i]

    def pin_bank(self):
        i = self.rot.pop(0)
        return i, self.ps[i], self.pb[i]

    def unpin_bank(self, i):
        self.rot.append(i)

    def _waits(self, eng, rd, wr):
        need = {}

        def add(d, same_ok):
            for k, (sem, val) in d.items():
                if (sem is eng.sem) and not same_ok:
                    continue
                if need.get(k, (None, 0))[1] < val:
                    need[k] = (sem, val)

        strict = eng is not self.pe
        for b in rd:
            add(b.wr, True)
            if b.excl:
                add(b.rd, False)
        for b in wr:
            add(b.wr, strict)
            add(b.rd, strict)
        for k, (sem, val) in need.items():
            if eng.seen.get(k, 0) < val:
                eng.h.wait_ge(sem, val)
                eng.seen[k] = val

    def op(self, eng, fn, rd=(), wr=()):
        self._waits(eng, rd, wr)
        ins = fn()
        eng.cnt += 1
        ins.then_inc(eng.sem, 1)
        k = id(eng.sem)
        ev = (eng.sem, eng.cnt)
        for b in wr:
            b.wr = {k: ev}
            b.rd = {}
        for b in rd:
            b.rd[k] = ev

    def P(self, fn, rd=(), wr=()):
        self.op(self.pe, fn, rd, wr)

    def A(self, fn, rd=(), wr=()):
        self.op(self.act, fn, rd, wr)

    def V(self, fn, rd=(), wr=()):
        self.op(self.dve, fn, rd, wr)

    def G(self, fn, rd=(), wr=()):
        self.op(self.pool, fn, rd, wr)

    def AV(self, fa, fv, rd=(), wr=()):
        self.flip ^= 1
        if self.flip:
            self.op(self.act, fa, rd, wr)
        else:
            self.op(self.dve, fv, rd, wr)

    def dma(self, q, chan, out, in_, rd=(), wr=(), **kw):
        self._waits(q, rd, wr)
        ins = q.h.dma_start(out=out, in_=in_, **kw)
        chan.cnt += 16
        ins.then_inc(chan.sem, 16)
        k = id(chan.sem)
        ev = (chan.sem, chan.cnt)
        for b in wr:
            b.wr = {k: ev}
            b.rd = {}
        for b in rd:
            b.rd[k] = ev


def build_program(dbg_tiles=None, with_sample=WITH_SAMPLE, dbg_stop=None):
    nc = bass.Bass("TRN2", target_bir_lowering=False)

    def din(name, shape):
        return nc.dram_tensor(name, shape, F32, kind="ExternalInput").ap()

    def dout(name, shape):
        return nc.dram_tensor(name, shape, F32, kind="ExternalOutput").ap()

    xp = din("xp", [2, SEQ, D])
    xs = din("xs", [64, D])
    ck = din("ck", [4, 512, 1024])
    cv = din("cv", [4, 512, 1024])
    sg = din("sg", [4, 4, 128, 256])
    sc = din("sc", [8, DFF])
    w_in = din("w_in", [D, INW])
    w_o = din("w_o", [D, D])
    w_up = din("w_up", [D, 2 * DFF])
    w_down = din("w_down", [DFF, D])
    d_gpre1 = din("gpre1", [128, 16])
    d_gpre2 = din("gpre2", [128, 16])
    d_gpost1 = din("gpost1", [128, D])
    d_gpost2 = din("gpost2", [128, D])
    d_brep = din("brep", [128, 512])
    d_ggla = din("ggla", [128, 256])
    d_wg = din("wg", [16, 512])
    d_convw = din("convw", [128, NFC * 4])
    d_biasT = din("biasT", [128, 8 * 640])
    d_biasN = din("biasN", [64, 8 * 64])
    d_cbias = din("cbias", [128, 8])
    d_biasS = din("biasS", [128, 8 * 64])
    d_smask = din("smask", [64, 132])

    yp = dout("yp", [2, SEQ, D])
    ys = dout("ys", [64, D])
    kp = dout("kp", [2, 512, 1024])
    vp = dout("vp", [2, 512, 1024])
    gp = dout("gp", [2, 4, 128, 256])
    cp = dout("cp", [4, DFF])
    ksn = dout("ksn", [64, 1024])
    vsn = dout("vsn", [64, 1024])
    gs = dout("gs", [4, 4, 128, 256])
    cs = dout("cs", [8, DFF])

    win_b = nc.dram_tensor("win_b", [12, 128, 16, 512], BF16, kind="Internal").ap()
    wo_b = nc.dram_tensor("wo_b", [4, 128, 16, 512], BF16, kind="Internal").ap()
    wup_b = nc.dram_tensor("wup_b", [22, 128, 16, 512], BF16, kind="Internal").ap()
    wdn_b = nc.dram_tensor("wdn_b", [12, 128, 16, 512], BF16, kind="Internal").ap()

    es = ExitStack()
    with es:
        K = KB(nc, es)
        P, A, V, G, AV = K.P, K.A, K.V, K.G, K.AV
        sb = K.sb

        gpre1 = sb("gpre1", [128, 16], F32)
        gpre2 = sb("gpre2", [128, 16], F32)
        gpost = sb("gpost", [128, D], F32)
        brep = sb("brep", [128, 512], F32)
        ggla = sb("ggla", [128, 256], F32)
        convw = sb("convw", [128, NFC, 4], F32)
        biasT = sb("biasT", [128, 8, 640], F32)
        biasN = sb("biasN", [64, 8, 64], F32)
        smask = sb("smask", [64, 132], F32)
        cbias = sb("cbias", [128, 8], F32)
        biasS = sb("biasS", [128, 8, 4, 16], F32)
        smask_bf = sb("smask_bf", [64, 132], BF16)
        wg_bf = sb("wg_bf", [16, 512], BF16)
        wlo = sb("wlo", [128, 16, 16], BF16)
        identf = sb("identf", [128, 128], F32)
        Ubf = sb("Ubf", [128, 128], BF16)
        onesbf = sb("onesbf", [128, 128], BF16)
        oneT = sb("oneT", [128, 1], F32)
        epsT = sb("epsT", [128, 1], F32)
        mhalf = sb("mhalf", [128, 4], F32)
        B_const = Buf("const")
        c_const = K.chan()
        for (t, d) in [(gpre1, d_gpre1), (gpre2, d_gpre2), (brep, d_brep), (ggla, d_ggla), (smask, d_smask), (cbias, d_cbias)]:
            K.dma(K.sp, c_const, t[:, :], d[:, :])
        K.dma(K.sp, c_const, convw[:, :, :], d_convw.rearrange("p (j c) -> p j c", c=4))
        K.dma(K.sp, c_const, biasT[:, :, :], d_biasT.rearrange("p (h k) -> p h k", h=8))
        K.dma(K.sp, c_const, biasN[:, :, :], d_biasN.rearrange("p (h k) -> p h k", h=8))
        K.dma(K.sp, c_const, biasS[:, :, :, :], d_biasS.rearrange("p (h b t) -> p h b t", h=8, b=4))
        c_const_g = K.chan()
        K.dma(K.pool, c_const_g, wg_bf[:, :], d_wg[:, :])
        K.dma(K.pool, c_const_g, wlo[:, :, :], w_in[:, 6144:6160].rearrange("(kc p) n -> p kc n", p=128))
        B_const.wr = {id(c_const.sem): (c_const.sem, c_const.cnt), id(c_const_g.sem): (c_const_g.sem, c_const_g.cnt)}
        B_gpost = Buf("gpost")
        c_gpost = K.chan()

        B_gen = Buf("gen")
        G(lambda: nc.gpsimd.memset(identf[:, :], 1.0), wr=[B_gen])
        G(lambda: nc.gpsimd.affine_select(out=identf[:, :], in_=identf[:, :], pattern=[[1, 128]],
                                          compare_op=ALU.is_equal, fill=0.0, base=0, channel_multiplier=-1),
          rd=[B_gen], wr=[B_gen])
        G(lambda: nc.gpsimd.memset(Ubf[:, :], 1.0), wr=[B_gen])
        G(lambda: nc.gpsimd.affine_select(out=Ubf[:, :], in_=Ubf[:, :], pattern=[[1, 128]],
                                          compare_op=ALU.is_ge, fill=0.0, base=0, channel_multiplier=-1),
          rd=[B_gen], wr=[B_gen])
        G(lambda: nc.gpsimd.memset(onesbf[:, :], 1.0), wr=[B_gen])
        G(lambda: nc.gpsimd.memset(oneT[:, :], 1.0), wr=[B_gen])
        G(lambda: nc.gpsimd.memset(epsT[:, :], EPS), wr=[B_gen])
        G(lambda: nc.gpsimd.memset(mhalf[:, :], -0.5), wr=[B_gen])
        G(lambda: nc.gpsimd.tensor_copy(out=smask_bf[:, :], in_=smask[:, :]), rd=[B_const], wr=[B_gen])
        CONST = [B_const, B_gen]

        def convert(lst, chanW):
            for (src_ap, dst_ap) in lst:
                K.dma(K.pool, chanW, dst_ap, src_ap)
            b = Buf("wscr")
            b.wr = {id(chanW.sem): (chanW.sem, chanW.cnt)}
            return b

        def kcview(ap2d):
            return ap2d.rearrange("(kc p) n -> p kc n", p=128)

        B_win = [convert([(kcview(w_in[:, j * 512:(j + 1) * 512]), win_b[j]) for j in range(6 * g, 6 * g + 6)], K.chan())
                 for g in range(2)]
        B_wo = convert([(kcview(w_o[:, j * 512:(j + 1) * 512]), wo_b[j]) for j in range(4)], K.chan())
        B_wup = []
        for g, (j0, j1) in enumerate([(2 * k_, 2 * k_ + 2) for k_ in range(11)]):
            lst = []
            for j in range(j0, j1):
                lst.append((kcview(w_up[:, j * 256:(j + 1) * 256]), wup_b[j][:, :, 0:256]))
                lst.append((kcview(w_up[:, DFF + j * 256:DFF + (j + 1) * 256]), wup_b[j][:, :, 256:512]))
            B_wup.append((j1, convert(lst, K.chan())))
        B_wdn = []
        for g in range(4):
            lst = []
            for cb in range(g, g + 1):
                for kgi in range(3):
                    nk = 16 if kgi < 2 else 12
                    lst.append((kcview(w_down[kgi * 2048: kgi * 2048 + nk * 128, cb * 512:(cb + 1) * 512]),
                                wdn_b[cb * 3 + kgi][:, 0:nk, :]))
            B_wdn.append(convert(lst, K.chan()))

        def wup_buf(j):
            for (j1, b_) in B_wup:
                if j < j1:
                    return b_

        kTr = sb("kTr", [128, 8, 768], BF16)
        vr = sb("vr", [128, 6, 8, 129], BF16)
        B_kT = [[Buf(f"kT{h}_{r}") for r in range(3)] for h in range(8)]
        B_vr = [[Buf(f"vr{b}_{j}") for j in range(2)] for b in range(6)]
        V(lambda: nc.vector.memset(vr[:, :, :, 128:129], 1.0), wr=[b for bb in B_vr for b in bb])
        slabs = [sb(f"slab{i}", [128, 16, 512], BF16) for i in range(NSLOT)]
        B_slab = [Buf(f"slab{i}") for i in range(NSLOT)]
        c_slab = [K.chan() for i in range(NSLOT)]
        Sst = sb("Sst", [128, 4, 256], F32)
        Sbf = sb("Sbf", [128, 4, 256], BF16)
        B_S = [Buf(f"S{h}") for h in range(4)]
        B_Sbf = [Buf(f"Sbf{h}") for h in range(4)]
        c_S = K.chan(out=True)
        c_Sin = K.chan()
        carry = sb("carry", [128, NFC, 8], F32)
        B_carry = [Buf(f"carry{j}") for j in range(NFC)]
        qtTm = sb("qtTm", [128, 4, 4, 64], BF16)
        B_qtTm = Buf("qtTm")
        pTn = sb("pTn", [64, 4, 64], BF16)
        B_pTn = [Buf(f"pTn{i}") for i in range(4)]
        stats = sb("stats", [128, 64], F32)
        B_st = {}

        def st(name, c0, n):
            B_st[name] = Buf("st_" + name)
            return stats[:, c0:c0 + n]

        ss = st("ss", 0, 2)
        ms = st("ms", 2, 2)
        sd = st("sd", 4, 2)
        rstd = st("rstd", 6, 2)
        ssq = st("ssq", 8, 8)
        oss = st("oss", 16, 4)
        orstd = st("orstd", 20, 4)
        oms = st("oms", 24, 4)
        osd = st("osd", 28, 4)
        rinv = st("rinv", 32, 2)
        den = st("den", 34, 2)
        ecl = st("ecl", 36, 16)

        Xr = sb("Xr", [128, 4096], F32)
        Nr = sb("Nr", [128, 4096], F32)
        actT = sb("actT", [128, 16, TT], BF16)
        B_actT = [Buf(f"actT{kc}") for kc in range(16)]
        Wr = sb("Wr", [128, 9856], F32)

        xv = Xr[:, :].rearrange("p (s d) -> p s d", s=2)
        B_x = [Buf("x0"), Buf("x1")]
        qg = Xr[:, 0:1024].rearrange("p (s d) -> p s d", s=2)
        kg = Xr[:, 1024:2048].rearrange("p (s d) -> p s d", s=2)
        vg = Xr[:, 2048:3072].bitcast(BF16).rearrange("p (s d) -> p s d", s=2)
        qT = Xr[:, 3072:4096].bitcast(BF16).rearrange("p (h t) -> p h t", h=8)
        B_qg = [Buf("qg0"), Buf("qg1")]
        B_kg = [Buf("kg0"), Buf("kg1")]
        B_vg = [[Buf(f"vg{s}_{j}") for j in range(2)] for s in range(2)]
        B_qT = [Buf(f"qT{h}") for h in range(8)]
        XA = B_x
        XB = B_qg + B_kg + [b for bb in B_vg for b in bb] + B_qT
        k2m = [Xr[0:64, 512:1024].bitcast(BF16).rearrange("p (i d) -> p i d", i=2),
               Xr[0:64, 1536:2048].bitcast(BF16).rearrange("p (i d) -> p i d", i=2)]
        B_k2m = [Buf("k2m0"), Buf("k2m1")]
        pTs = Xr[:, 2560:3072].bitcast(BF16).rearrange("p (i b t) -> p i b t", i=4, b=4)
        B_pTs = [Buf(f"pTs{i}") for i in range(4)]
        nv = Nr[:, :].rearrange("p (s d) -> p s d", s=2)
        B_n = [Buf("n0"), Buf("n1")]
        kstb = [Nr[:, 0:512], Nr[:, 512:1024]]
        vstb = [Nr[:, 1024:1536], Nr[:, 1536:2048]]
        scb = [Nr[:, 2048:2688], Nr[:, 2688:3328]]
        pTb = [Nr[:, 3328:3648].bitcast(BF16), Nr[:, 3648:3968].bitcast(BF16)]
        B_kst = [Buf("kst0"), Buf("kst1")]
        B_vst = [Buf("vst0"), Buf("vst1")]
        B_sc = [Buf("sc0"), Buf("sc1")]
        B_pT = [Buf("pT0"), Buf("pT1")]
        c_kst = [K.chan(out=True), K.chan(out=True)]
        c_vst = [K.chan(out=True), K.chan(out=True)]
        NA = B_n
        NB = B_kst + B_vst + B_sc + B_pT
        NB_S = NB
        cvb = [Nr[0:8, 0:512], Nr[0:8, 512:1024]]
        B_cvb = [Buf("cvb0"), Buf("cvb1")]
        c_cvb = [K.chan(out=True), K.chan(out=True)]
        c_cvin = [K.chan(), K.chan()]
        NC_ = B_cvb
        mix_in = Wr[:, 0:4096].rearrange("p (s d) -> p s d", s=2)
        B_mi = [[Buf(f"mi{s}_{c}") for c in range(16)] for s in range(2)]
        o = 4096
        zt = Wr[:, o:o + 512]; o += 512
        sp_hi = Wr[:, o:o + 256].bitcast(BF16); o += 256
        sp_lo = Wr[:, o:o + 256].bitcast(BF16); o += 256
        E1 = Wr[:, o:o + 512]; o += 512
        Einv = Wr[:, o:o + 512]; o += 512
        ECL = Wr[:, o:o + 512]; o += 512
        k2_ = []; qtT_ = []; ktT_ = []
        for _par in range(2):
            k2_.append(Wr[:, o:o + 256].bitcast(BF16)); o += 256
            qtT_.append(Wr[:, o:o + 256].bitcast(BF16).rearrange("p (h t) -> p h t", h=4)); o += 256
            ktT_.append(Wr[:, o:o + 256].bitcast(BF16).rearrange("p (h t) -> p h t", h=4)); o += 256
        ATm = Wr[:, o:o + 256].bitcast(BF16).rearrange("p (b t) -> p b t", b=4); o += 256
        o_sb = Wr[:, o:o + 1024].rearrange("p (h v) -> p h v", h=4); o += 1024
        junk = Wr[:, o:o + 256].bitcast(BF16); o += 256
        aloT = Wr[:, o:o + 128].bitcast(BF16); o += 128
        assert o <= 9856, o
        B_zt, B_hi, B_lo, B_E1, B_Einv, B_ECL = [Buf(n) for n in ("zt", "hi", "lo", "E1", "Einv", "ECL")]
        B_ecl_ = [Buf("ecla"), Buf("eclb")]
        B_k2_ = [Buf("k2a"), Buf("k2b")]
        B_qtT_ = [Buf("qtTa"), Buf("qtTb")]
        B_ktT_ = [Buf("ktTa"), Buf("ktTb")]
        B_ATm = [Buf(f"ATm{h}") for h in range(4)]
        B_osb = [Buf(f"osb{h}") for h in range(4)]
        B_junk = Buf("junk")
        B_alo = Buf("aloT")
        WA = ([b for bb in B_mi for b in bb] + [B_zt, B_hi, B_lo, B_E1, B_Einv, B_ECL] + B_k2_ + B_qtT_ + B_ktT_
              + B_ATm + B_osb + [B_junk, B_alo])
        cst = Wr[:, 2048:4096].rearrange("p (b d) -> p b d", b=2)
        cstN = [Nr[:, 0:1024], Nr[:, 1024:2048]]
        CST = [cst[:, 0, :], cst[:, 1, :], cstN[0], cstN[1]]
        B_cst = [Buf(f"cst{i}") for i in range(4)]
        c_cst = [K.chan() for i in range(4)]
        Sbf4 = Wr[:, 2048:4096].bitcast(BF16).rearrange("p (i h v) -> p i h v", i=4, h=4)
        B_Sbf4 = [Buf(f"Sbf4_{i}") for i in range(4)]
        actb = Wr[:, 0:5632].bitcast(BF16).rearrange("p (j t) -> p j t", j=NFC)
        B_act = [Buf(f"act{j}") for j in range(NFC)]
        gsb = [Wr[:, 5632:5632 + 260], Wr[:, 5892:5892 + 260]]
        ccb = [Wr[:, 6152:6152 + 256], Wr[:, 6408:6408 + 256]]
        sgb = [Wr[:, 6664:6664 + 256], Wr[:, 6920:6920 + 256]]
        B_gs = [Buf("gs0"), Buf("gs1")]
        B_cc = [Buf("cc0"), Buf("cc1")]
        B_sg = [Buf("sg0"), Buf("sg1")]
        WB = B_act + B_gs + B_cc + B_sg

        c_x = [K.chan(), K.chan()]
        c_x2 = [K.chan(), K.chan()]
        c_y = [K.chan(out=True), K.chan(out=True)]

        plan = []
        state = {"cur": 0, "loaded": 0, "limit": 0}
        SLABS_PER_TILE = 50

        def tile_plan():
            l = []
            for j in range(12):
                l.append((win_b[j], 16, B_win[j // 6]))
            for j in range(4):
                l.append((wo_b[j], 16, B_wo))
            for sl in range(22):
                l.append((wup_b[sl], 16, wup_buf(sl)))
            for cb in range(4):
                for kgi in range(3):
                    l.append((wdn_b[cb * 3 + kgi], 16 if kgi < 2 else 12, B_wdn[cb]))
            return l

        NTILES_TOTAL = (2 * NTILE if dbg_tiles is None else len(dbg_tiles)) + (1 if with_sample else 0)
        for _ in range(NTILES_TOTAL):
            plan.extend(tile_plan())

        def pump(n):
            while state["loaded"] < min(n + NSLOT, len(plan), state["limit"]):
                m = state["loaded"]
                ap, nk, srcb = plan[m]
                K.dma(K.sp, c_slab[m % NSLOT], slabs[m % NSLOT][:, 0:nk, :], ap[:, 0:nk, :],
                      rd=[srcb], wr=[B_slab[m % NSLOT]])
                state["loaded"] += 1

        def next_slab():
            n = state["cur"]
            pump(n)
            state["cur"] += 1
            return slabs[n % NSLOT], B_slab[n % NSLOT]

        def next_slab_old():
            n = state["cur"]
            while state["loaded"] < min(n + NSLOT, len(plan)):
                m = state["loaded"]
                ap, nk, srcb = plan[m]
                K.dma(K.sp, c_slab[m % NSLOT], slabs[m % NSLOT][:, 0:nk, :], ap[:, 0:nk, :],
                      rd=[srcb], wr=[B_slab[m % NSLOT]])
                state["loaded"] += 1
            state["cur"] += 1
            return slabs[n % NSLOT], B_slab[n % NSLOT]

        pool_ok = [False]

        def pin_tables():
            pass

        B_ss = [Buf("ss0"), Buf("ss1")]
        B_ms = [Buf("ms0"), Buf("ms1")]
        B_sd = [Buf("sd0"), Buf("sd1")]
        B_rs = [Buf("rs0"), Buf("rs1")]
        B_ssq = [Buf("ssq0"), Buf("ssq1")]

        def rstd_s(s, np_, denom):
            V(lambda: nc.vector.tensor_scalar(out=ms[0:np_, s:s + 1], in0=ss[0:np_, s:s + 1], scalar1=1.0 / denom,
                                              scalar2=EPS, op0=ALU.mult, op1=ALU.add), rd=[B_ss[s]], wr=[B_ms[s]])
            if pool_ok[0]:
                G(lambda: nc.gpsimd.tensor_tensor(out=rstd[0:np_, s:s + 1], in0=ms[0:np_, s:s + 1],
                                                  in1=mhalf[0:np_, 0:1], op=ALU.pow),
                  rd=[B_ms[s]] + CONST, wr=[B_rs[s]])
            else:
                A(lambda: nc.scalar.activation(out=sd[0:np_, s:s + 1], in_=ms[0:np_, s:s + 1], func=AF.Sqrt),
                  rd=[B_ms[s]], wr=[B_sd[s]])
                V(lambda: nc.vector.reciprocal(out=rstd[0:np_, s:s + 1], in_=sd[0:np_, s:s + 1]),
                  rd=[B_sd[s]], wr=[B_rs[s]])

        def norm_transpose(nsub, np_, gcol, chunked=False):
            ntok = nsub * np_
            for s in range(nsub):
                if chunked:
                    A(lambda: nc.scalar.activation(out=actT[0:np_, 8 * s:8 * s + 8, :],
                                                   in_=xv[0:np_, s, :].rearrange("p (a b) -> p a b", a=8),
                                                   func=AF.Square, accum_out=ss[0:np_, s:s + 1]),
                      rd=[B_x[s]], wr=[B_ss[s]] + B_actT[8 * s:8 * s + 8])
                else:
                    A(lambda: nc.scalar.activation(out=nv[0:np_, s, :], in_=xv[0:np_, s, :], func=AF.Square,
                                                   accum_out=ss[0:np_, s:s + 1]),
                      rd=[B_x[s]], wr=[B_ss[s], B_n[s]])
            for s in range(nsub):
                rstd_s(s, np_, float(D))
            for s in range(nsub):
                V(lambda: nc.vector.tensor_scalar(out=nv[0:np_, s, :], in0=xv[0:np_, s, :], scalar1=rstd[0:np_, s:s + 1],
                                                  scalar2=None, op0=ALU.mult),
                  rd=[B_x[s], B_rs[s]], wr=[B_n[s]])
            for kc in range(NKC):
                ps, pb = K.bank()
                for s in range(nsub):
                    P(lambda: nc.tensor.transpose(out=ps[:, s * np_:(s + 1) * np_],
                                                  in_=nv[0:np_, s, kc * 128:(kc + 1) * 128],
                                                  identity=identf[0:np_, 0:np_]),
                      rd=[B_n[s]] + CONST, wr=[pb])
                AV(lambda: nc.scalar.mul(out=actT[:, kc, 0:ntok], in_=ps[:, 0:ntok], mul=gcol[:, kc:kc + 1]),
                   lambda: nc.vector.tensor_scalar(out=actT[:, kc, 0:ntok], in0=ps[:, 0:ntok],
                                                   scalar1=gcol[:, kc:kc + 1], scalar2=None, op0=ALU.mult),
                   rd=[pb] + CONST, wr=[B_actT[kc]])

        def fm_block(slab, sbuf, c0, ntok, ncol=128):
            ps, pb = K.bank()
            for kc in range(NKC):
                P(lambda: nc.tensor.matmul(ps[0:ncol, 0:ntok], lhsT=slab[:, kc, c0:c0 + ncol], rhs=actT[:, kc, 0:ntok],
                                           start=(kc == 0), stop=(kc == NKC - 1)),
                  rd=[sbuf, B_actT[kc]], wr=[pb])
            return ps, pb

        def tm_block(slab, sbuf, s, np_):
            ps, pb = K.bank()
            for kc in range(NKC):
                P(lambda: nc.tensor.matmul(ps[0:np_, 0:512], lhsT=actT[:, kc, s * np_:(s + 1) * np_],
                                           rhs=slab[:, kc, :], start=(kc == 0), stop=(kc == NKC - 1)),
                  rd=[sbuf, B_actT[kc]], wr=[pb])
            return ps, pb

        def copy_av(dst, src, rd, wr):
            AV(lambda: nc.scalar.copy(out=dst, in_=src),
               lambda: nc.vector.tensor_copy(out=dst, in_=src), rd=rd, wr=wr)

        def load_gpost(d_gp):
            K.dma(K.sp, c_gpost, gpost[:, :], d_gp[:, :], wr=[B_gpost])

        def evac_post(ps, pb, s, cb, np_, jk, jkb):
            A(lambda: nc.scalar.activation(out=jk[0:np_, 0:512], in_=ps[0:np_, :], func=AF.Square,
                                           accum_out=ssq[0:np_, s * 4 + cb:s * 4 + cb + 1]),
              rd=[pb], wr=[jkb, B_ssq[s]])
            V(lambda: nc.vector.tensor_tensor(out=nv[0:np_, s, cb * 512:(cb + 1) * 512], in0=ps[0:np_, :],
                                              in1=gpost[0:np_, cb * 512:(cb + 1) * 512], op=ALU.mult),
              rd=[pb, B_gpost], wr=[B_n[s]])

        def post_norm_residual(nsub, np_, into_n=False):
            for s in range(nsub):
                V(lambda: nc.vector.reduce_sum(out=ss[0:np_, s:s + 1], in_=ssq[0:np_, s * 4:(s + 1) * 4], axis=AX.X),
                  rd=[B_ssq[s]], wr=[B_ss[s]])
                rstd_s(s, np_, float(D))
            for s in range(nsub):
                dstv, dstb = (nv, B_n) if into_n else (xv, B_x)
                V(lambda: nc.vector.scalar_tensor_tensor(out=dstv[0:np_, s, :], in0=nv[0:np_, s, :],
                                                         scalar=rstd[0:np_, s:s + 1], in1=xv[0:np_, s, :],
                                                         op0=ALU.mult, op1=ALU.add),
                  rd=[B_n[s], B_rs[s], B_x[s]], wr=[dstb[s]])

        def attention_prompt(ti, extra=None):
            g0 = 2 * ti
            pending = []
            units_done = [0]

            def flush():
                (s, h, bsel, blks, pv) = pending.pop(0)
                pso, pbo = K.bank()
                for i, bk in enumerate(blks):
                    rp = bk % 6
                    P(lambda: nc.tensor.matmul(pso[:, 0:129], lhsT=pv[:, i * 128:(i + 1) * 128],
                                               rhs=vr[:, rp, h, 0:129], start=(i == 0), stop=(i == len(blks) - 1)),
                      rd=[B_pT[bsel], B_vr[rp][h // 4]], wr=[pbo])
                V(lambda: nc.vector.reciprocal(out=rinv[:, bsel:bsel + 1], in_=pso[:, 128:129]),
                  rd=[pbo], wr=[B_st["rinv"]])
                V(lambda: nc.vector.tensor_scalar(out=mix_in[:, s, h * 128:(h + 1) * 128], in0=pso[:, 0:128],
                                                  scalar1=rinv[:, bsel:bsel + 1], scalar2=None, op0=ALU.mult),
                  rd=[pbo, B_st["rinv"]], wr=[B_mi[s][h]])

            for s in range(2):
                g = g0 + s
                allb = [bk for bk in range(g - 4, g + 1) if bk >= 0]
                cst = [bk for bk in allb if 4 - (g - bk) in (1, 2)]
                var = [bk for bk in allb if 4 - (g - bk) not in (1, 2)]
                nvar, ncst = len(var), len(cst)
                blks = var + cst
                for h in range(8):
                    bsel = (s * 8 + h) % 2
                    psA, pbA = K.bank()
                    psB, pbB = K.bank()
                    for i, bk in enumerate(blks):
                        rp = bk % 6
                        if i < nvar:
                            dst, db = psB[:, i * 128:(i + 1) * 128], pbB
                        else:
                            dst, db = psA[:, (i - nvar) * 128:(i - nvar + 1) * 128], pbA
                        P(lambda: nc.tensor.matmul(dst, lhsT=kTr[:, h, rp * 128:(rp + 1) * 128],
                                                   rhs=qT[:, h, s * 128:(s + 1) * 128], start=True, stop=True),
                          rd=[B_kT[h][rp // 2], B_qT[h]], wr=[db])
                    scv = scb[bsel]
                    pv = pTb[bsel]
                    V(lambda: nc.vector.scalar_tensor_tensor(
                        out=scv[:, 0:nvar * 128], in0=psB[:, 0:nvar * 128], scalar=ATT_SCALE,
                        in1=biasT[:, h, (3 - nvar) * 128:384], op0=ALU.mult, op1=ALU.add),
                      rd=[pbB] + CONST, wr=[B_sc[bsel]])
                    if ncst:
                        A(lambda: nc.scalar.activation(out=pv[:, nvar * 128:(nvar + ncst) * 128],
                                                       in_=psA[:, 0:ncst * 128], func=AF.Exp,
                                                       bias=cbias[:, h:h + 1], scale=ATT_SCALE),
                          rd=[pbA] + CONST, wr=[B_pT[bsel]])
                    A(lambda: nc.scalar.activation(out=pv[:, 0:nvar * 128], in_=scv[:, 0:nvar * 128], func=AF.Exp),
                      rd=[B_sc[bsel]], wr=[B_pT[bsel]])
                    if pending:
                        flush()
                    pending.append((s, h, bsel, blks, pv))
                    units_done[0] += 1
                    if extra is not None:
                        extra(units_done[0])
            while pending:
                flush()

        def attention_sample():
            V(lambda: nc.vector.memset(pTs[:, :, :, :], 0.0), wr=B_pTs)
            V(lambda: nc.vector.memset(pTn[:, :, :], 0.0), wr=B_pTn)
            inherit(B_cst[2:4], B_kst + B_vst)
            ld = [0]
            pending = []

            def stage_load(src):
                cb_ = ld[0] % 4
                ld[0] += 1
                K.dma(K.sp, c_cst[cb_], CST[cb_], src, wr=[B_cst[cb_]])
                return cb_

            def flush():
                (i, h, bsel) = pending.pop(0)
                pso, pbo = K.bank()
                for bk in range(4):
                    P(lambda: nc.tensor.matmul(pso[0:64, 0:129], lhsT=pTs[:, i, bk, :], rhs=vr[:, bk, h, 0:129],
                                               start=(bk == 0), stop=False),
                      rd=[B_pTs[i], B_vr[bk][h // 4]], wr=[pbo])
                P(lambda: nc.tensor.matmul(pso[0:64, 0:129], lhsT=pTn[:, i, :], rhs=vr[0:64, 4, h, 0:129],
                                           start=False, stop=True),
                  rd=[B_pTn[i], B_vr[4][h // 4]], wr=[pbo])
                V(lambda: nc.vector.tensor_scalar(out=den[0:64, bsel:bsel + 1], in0=pso[0:64, 128:129],
                                                  scalar1=1e-30, scalar2=None, op0=ALU.max),
                  rd=[pbo], wr=[B_st["den"]])
                V(lambda: nc.vector.reciprocal(out=rinv[0:64, bsel:bsel + 1], in_=den[0:64, bsel:bsel + 1]),
                  rd=[B_st["den"]], wr=[B_st["rinv"]])
                if i == 0:
                    V(lambda: nc.vector.tensor_scalar(out=mix_in[0:64, 0, h * 128:(h + 1) * 128],
                                                      in0=pso[0:64, 0:128], scalar1=rinv[0:64, bsel:bsel + 1],
                                                      scalar2=None, op0=ALU.mult),
                      rd=[pbo, B_st["rinv"]], wr=[B_mi[0][h]])
                else:
                    V(lambda: nc.vector.scalar_tensor_tensor(out=mix_in[0:64, 0, h * 128:(h + 1) * 128],
                                                             in0=pso[0:64, 0:128], scalar=rinv[0:64, bsel:bsel + 1],
                                                             in1=mix_in[0:64, 0, h * 128:(h + 1) * 128],
                                                             op0=ALU.mult, op1=ALU.add),
                      rd=[pbo, B_st["rinv"], B_mi[0][h]], wr=[B_mi[0][h]])

            for i in range(4):
                for bk in range(4):
                    cb_ = stage_load(ck[i, bk * 128:(bk + 1) * 128, :])
                    for hq in range(2):
                        ps, pb = K.bank()
                        for hh in range(4):
                            h = hq * 4 + hh
                            P(lambda: nc.tensor.transpose(out=ps[:, hh * 128:(hh + 1) * 128],
                                                          in_=CST[cb_][:, h * 128:(h + 1) * 128],
                                                          identity=identf[:, :]),
                              rd=[B_cst[cb_]] + CONST, wr=[pb])
                        dst = kTr[:, hq * 4:hq * 4 + 4, bk * 128:(bk + 1) * 128]
                        src = ps[:, :].rearrange("p (h t) -> p h t", h=4)
                        copy_av(dst, src, [pb], [B_kT[hq * 4 + hh][bk // 2] for hh in range(4)])
                for bk in range(4):
                    cb_ = stage_load(cv[i, bk * 128:(bk + 1) * 128, :])
                    dst = vr[:, bk, :, 0:128]
                    src = CST[cb_].rearrange("p (h d) -> p h d", h=8)
                    copy_av(dst, src, [B_cst[cb_]], B_vr[bk])
                for h in range(8):
                    bsel = h % 2
                    psA, pbA = K.bank()
                    psB, pbB = K.bank()
                    for bk in range(4):
                        P(lambda: nc.tensor.matmul(psA[:, bk * 16:(bk + 1) * 16],
                                                   lhsT=kTr[:, h, bk * 128:(bk + 1) * 128],
                                                   rhs=qT[:, h, 16 * i:16 * i + 16], start=True, stop=True),
                          rd=[B_kT[h][bk // 2], B_qT[h]], wr=[pbA])
                    P(lambda: nc.tensor.matmul(psB[0:64, 0:16], lhsT=kTr[:, h, 512:576],
                                               rhs=qT[:, h, 16 * i:16 * i + 16], start=True, stop=True),
                      rd=[B_kT[h][2], B_qT[h]], wr=[pbB])
                    scv = scb[bsel]
                    V(lambda: nc.vector.scalar_tensor_tensor(
                        out=scv[:, 0:64].rearrange("p (b t) -> p b t", b=4),
                        in0=psA[:, 0:64].rearrange("p (b t) -> p b t", b=4), scalar=ATT_SCALE,
                        in1=biasS[:, h, :, :],
                        op0=ALU.mult, op1=ALU.add),
                      rd=[pbA] + CONST, wr=[B_sc[bsel]])
                    V(lambda: nc.vector.scalar_tensor_tensor(
                        out=scv[0:64, 64:80], in0=psB[0:64, 0:16], scalar=ATT_SCALE,
                        in1=biasN[:, h, 16 * i:16 * i + 16], op0=ALU.mult, op1=ALU.add),
                      rd=[pbB] + CONST, wr=[B_sc[bsel]])
                    if pending:
                        flush()
                    A(lambda: nc.scalar.activation(out=pTs[:, i, :, 16 * i:16 * i + 16],
                                                   in_=scv[:, 0:64].rearrange("p (b t) -> p b t", b=4), func=AF.Exp),
                      rd=[B_sc[bsel]], wr=[B_pTs[i]])
                    A(lambda: nc.scalar.activation(out=pTn[:, i, 16 * i:16 * i + 16], in_=scv[0:64, 64:80],
                                                   func=AF.Exp),
                      rd=[B_sc[bsel]], wr=[B_pTn[i]])
                    pending.append((i, h, bsel))
                while pending:
                    flush()
            inherit(B_kst + B_vst, B_cst[2:4])

        def gla_front(prompt, s, np_, par):
            k2, qtT, ktT = k2_[par], qtT_[par], ktT_[par]
            B_k2, B_qtT, B_ktT = B_k2_[par], B_qtT_[par], B_ktT_[par]
            eoff = par * 4 if prompt else 0
            nseq = 1 if prompt else 4
            if prompt:
                Umat = Ubf[:, :]
                Omat = onesbf[:, :]
                sel = onesbf[:, 0:1]
            else:
                Umat = smask_bf[0:64, 4:68]
                Omat = smask_bf[0:64, 68:132]
                sel = smask_bf[0:64, 0:4]
            ps, pb = K.bank()
            P(lambda: nc.tensor.matmul(ps[0:np_, 0:512], lhsT=aloT[0:16, s * np_:(s + 1) * np_], rhs=wg_bf[0:16, :],
                                       start=True, stop=True), rd=[B_alo] + CONST, wr=[pb])
            V(lambda: nc.vector.tensor_tensor(out=zt[0:np_, :], in0=ps[0:np_, 0:512], in1=brep[0:np_, :], op=ALU.add),
              rd=[pb] + CONST, wr=[B_zt])
            yield
            A(lambda: nc.scalar.activation(out=zt[0:np_, :], in_=zt[0:np_, :], func=AF.Exp, scale=-1.0),
              rd=[B_zt], wr=[B_zt])
            A(lambda: nc.scalar.activation(out=zt[0:np_, :], in_=zt[0:np_, :], func=AF.Ln, bias=oneT[0:np_, 0:1]),
              rd=[B_zt] + CONST, wr=[B_zt])
            yield
            V(lambda: nc.vector.tensor_copy(out=sp_hi[0:np_, :], in_=zt[0:np_, :]), rd=[B_zt], wr=[B_hi])
            V(lambda: nc.vector.tensor_tensor(out=sp_lo[0:np_, :], in0=zt[0:np_, :], in1=sp_hi[0:np_, :],
                                              op=ALU.subtract), rd=[B_zt, B_hi], wr=[B_lo])
            yield
            psc, pbc = K.bank()
            P(lambda: nc.tensor.matmul(psc[0:np_, 0:512], lhsT=Umat, rhs=sp_hi[0:np_, :], start=True, stop=False),
              rd=[B_hi] + CONST, wr=[pbc])
            P(lambda: nc.tensor.matmul(psc[0:np_, 0:512], lhsT=Umat, rhs=sp_lo[0:np_, :], start=False, stop=True),
              rd=[B_lo] + CONST, wr=[pbc])
            pst, pbt = K.bank()
            P(lambda: nc.tensor.matmul(pst[0:np_, 0:512], lhsT=Omat, rhs=sp_hi[0:np_, :], start=True, stop=False),
              rd=[B_hi] + CONST, wr=[pbt])
            P(lambda: nc.tensor.matmul(pst[0:np_, 0:512], lhsT=Omat, rhs=sp_lo[0:np_, :], start=False, stop=True),
              rd=[B_lo] + CONST, wr=[pbt])
            pse, pbe = K.bank()
            for h in range(4):
                P(lambda: nc.tensor.matmul(pse[:, h * nseq:(h + 1) * nseq], lhsT=sp_hi[0:np_, h * 128:(h + 1) * 128],
                                           rhs=sel, start=True, stop=False), rd=[B_hi] + CONST, wr=[pbe])
                P(lambda: nc.tensor.matmul(pse[:, h * nseq:(h + 1) * nseq], lhsT=sp_lo[0:np_, h * 128:(h + 1) * 128],
                                           rhs=sel, start=False, stop=True), rd=[B_lo] + CONST, wr=[pbe])
            yield
            A(lambda: nc.scalar.activation(out=E1[0:np_, :], in_=psc[0:np_, 0:512], func=AF.Exp, scale=-1.0 / 16.0),
              rd=[pbc], wr=[B_E1])
            A(lambda: nc.scalar.activation(out=Einv[0:np_, :], in_=psc[0:np_, 0:512], func=AF.Exp, scale=1.0 / 16.0),
              rd=[pbc], wr=[B_Einv])
            A(lambda: nc.scalar.activation(out=ECL[0:np_, :], in_=pst[0:np_, 0:512], func=AF.Exp, scale=-1.0 / 16.0),
              rd=[pbt], wr=[B_ECL])
            A(lambda: nc.scalar.activation(out=ecl[:, eoff:eoff + 4 * nseq], in_=pse[:, 0:4 * nseq], func=AF.Exp,
                                           scale=-1.0 / 16.0), rd=[pbe], wr=[B_ecl_[par]])
            yield
            V(lambda: nc.vector.tensor_tensor(out=E1[0:np_, :], in0=E1[0:np_, :], in1=qg[0:np_, s, :], op=ALU.mult),
              rd=[B_E1, B_qg[s]], wr=[B_E1])
            V(lambda: nc.vector.tensor_tensor(out=Einv[0:np_, :], in0=Einv[0:np_, :], in1=kg[0:np_, s, :], op=ALU.mult),
              rd=[B_Einv, B_kg[s]], wr=[B_Einv])
            V(lambda: nc.vector.tensor_tensor(out=k2[0:np_, :], in0=Einv[0:np_, :], in1=ECL[0:np_, :], op=ALU.mult),
              rd=[B_Einv, B_ECL], wr=[B_k2])
            yield
            psq, pbq = K.bank()
            psk, pbk = K.bank()
            for h in range(4):
                P(lambda: nc.tensor.transpose(out=psq[:, h * np_:(h + 1) * np_], in_=E1[0:np_, h * 128:(h + 1) * 128],
                                              identity=identf[0:np_, 0:np_]), rd=[B_E1] + CONST, wr=[pbq])
                P(lambda: nc.tensor.transpose(out=psk[:, h * np_:(h + 1) * np_], in_=Einv[0:np_, h * 128:(h + 1) * 128],
                                              identity=identf[0:np_, 0:np_]), rd=[B_Einv] + CONST, wr=[pbk])
            A(lambda: nc.scalar.copy(out=qtT[:, :, 0:np_], in_=psq[:, 0:4 * np_].rearrange("p (h t) -> p h t", h=4)),
              rd=[pbq], wr=[B_qtT])
            V(lambda: nc.vector.tensor_copy(out=ktT[:, :, 0:np_],
                                            in_=psk[:, 0:4 * np_].rearrange("p (h t) -> p h t", h=4)),
              rd=[pbk], wr=[B_ktT])
            yield
            if not prompt:
                V(lambda: nc.vector.memset(qtTm[:, :, :, :], 0.0), wr=[B_qtTm])
                for i in range(4):
                    V(lambda: nc.vector.tensor_copy(out=qtTm[:, i, :, 16 * i:16 * i + 16],
                                                    in_=qtT[:, :, 16 * i:16 * i + 16]),
                      rd=[B_qtT], wr=[B_qtTm])
                    V(lambda: nc.vector.tensor_scalar(out=k2m[i // 2][:, i % 2, :], in0=k2[0:64, :],
                                                      scalar1=smask[0:64, i:i + 1], scalar2=None, op0=ALU.mult),
                      rd=[B_k2] + CONST, wr=[B_k2m[i // 2]])
            yield

        def gla_back(prompt, s, np_, par, hook=None):
            nseq = 1 if prompt else 4
            Umat = Ubf[:, :] if prompt else smask_bf[0:64, 4:68]
            k2, qtT, ktT = k2_[par], qtT_[par], ktT_[par]
            B_k2, B_qtT, B_ktT = B_k2_[par], B_qtT_[par], B_ktT_[par]
            eoff = par * 4 if prompt else 0
            bankA = []
            for h in range(4):
                psa, pba = K.bank()
                bankA.append((psa, pba))
                P(lambda: nc.tensor.matmul(psa[0:np_, 0:np_], lhsT=ktT[:, h, 0:np_], rhs=qtT[:, h, 0:np_],
                                           start=True, stop=True), rd=[B_ktT, B_qtT], wr=[pba])
            for h in range(4):
                psa, pba = bankA[h]
                V(lambda: nc.vector.tensor_tensor(out=ATm[0:np_, h, 0:np_], in0=psa[0:np_, 0:np_], in1=Umat,
                                                  op=ALU.mult), rd=[pba] + CONST, wr=[B_ATm[h]])
            if hook is not None:
                hook()
            bankO = []
            for h in range(4):
                pso, pbo = K.bank()
                bankO.append((pso, pbo))
                P(lambda: nc.tensor.matmul(pso[0:np_, 0:256], lhsT=ATm[0:np_, h, 0:np_],
                                           rhs=vg[0:np_, s, h * 256:(h + 1) * 256], start=True, stop=False),
                  rd=[B_ATm[h], B_vg[s][h // 2]], wr=[pbo])
                if prompt:
                    P(lambda: nc.tensor.matmul(pso[0:np_, 0:256], lhsT=qtT[:, h, 0:np_], rhs=Sbf[:, h, :],
                                               start=False, stop=True), rd=[B_qtT, B_Sbf[h]], wr=[pbo])
                else:
                    for i in range(4):
                        P(lambda: nc.tensor.matmul(pso[0:np_, 0:256], lhsT=qtTm[:, i, h, :], rhs=Sbf4[:, i, h, :],
                                                   start=False, stop=(i == 3)), rd=[B_qtTm, B_Sbf4[i]], wr=[pbo])
            for h in range(4):
                pso, pbo = bankO[h]
                V(lambda: nc.vector.tensor_copy(out=o_sb[0:np_, h, :], in_=pso[0:np_, 0:256]), rd=[pbo], wr=[B_osb[h]])
                A(lambda: nc.scalar.activation(out=junk[0:np_, 0:256], in_=o_sb[0:np_, h, :], func=AF.Square,
                                               accum_out=oss[0:np_, h:h + 1]), rd=[B_osb[h]], wr=[B_junk, B_st["oss"]])
            if hook is not None:
                hook()
            if prompt:
                bankU = []
                for h in range(4):
                    psu, pbu = K.bank()
                    bankU.append((psu, pbu))
                    P(lambda: nc.tensor.matmul(psu[:, 0:256], lhsT=k2[0:np_, h * 128:(h + 1) * 128],
                                               rhs=vg[0:np_, s, h * 256:(h + 1) * 256], start=True, stop=True),
                      rd=[B_k2, B_vg[s][h // 2]], wr=[pbu])
                for h in range(4):
                    psu, pbu = bankU[h]
                    V(lambda: nc.vector.scalar_tensor_tensor(out=Sst[:, h, :], in0=Sst[:, h, :],
                                                             scalar=ecl[:, eoff + h:eoff + h + 1],
                                                             in1=psu[:, 0:256], op0=ALU.mult, op1=ALU.add),
                      rd=[B_S[h], B_ecl_[par], pbu], wr=[B_S[h]])
                    A(lambda: nc.scalar.copy(out=Sbf[:, h, :], in_=Sst[:, h, :]), rd=[B_S[h]], wr=[B_Sbf[h]])
            if hook is not None:
                hook()
                hook()
            V(lambda: nc.vector.tensor_scalar(out=oms[0:np_, :], in0=oss[0:np_, :], scalar1=1.0 / 256.0, scalar2=EPS,
                                              op0=ALU.mult, op1=ALU.add), rd=[B_st["oss"]], wr=[B_st["oms"]])
            if pool_ok[0]:
                G(lambda: nc.gpsimd.tensor_tensor(out=orstd[0:np_, :], in0=oms[0:np_, :], in1=mhalf[0:np_, 0:4],
                                                  op=ALU.pow), rd=[B_st["oms"]] + CONST, wr=[B_st["orstd"]])
            else:
                A(lambda: nc.scalar.activation(out=osd[0:np_, :], in_=oms[0:np_, :], func=AF.Sqrt),
                  rd=[B_st["oms"]], wr=[B_st["osd"]])
                V(lambda: nc.vector.reciprocal(out=orstd[0:np_, :], in_=osd[0:np_, :]), rd=[B_st["osd"]],
                  wr=[B_st["orstd"]])
            for h in range(4):
                mb = [B_mi[s][8 + 2 * h], B_mi[s][9 + 2 * h]]
                msl = mix_in[0:np_, s, 1024 + h * 256:1024 + (h + 1) * 256]
                tq = junk[0:np_, :].bitcast(F32)
                A(lambda: nc.scalar.activation(out=tq, in_=msl, func=AF.Tanh, scale=0.5), rd=mb, wr=[B_junk])
                V(lambda: nc.vector.scalar_tensor_tensor(out=msl, in0=tq, scalar=1.0, in1=msl,
                                                         op0=ALU.add, op1=ALU.mult),
                  rd=[B_junk] + mb, wr=mb)
                V(lambda: nc.vector.scalar_tensor_tensor(out=o_sb[0:np_, h, :], in0=o_sb[0:np_, h, :],
                                                         scalar=orstd[0:np_, h:h + 1], in1=ggla[0:np_, :],
                                                         op0=ALU.mult, op1=ALU.mult),
                  rd=[B_osb[h], B_st["orstd"]] + CONST, wr=[B_osb[h]])
                V(lambda: nc.vector.scalar_tensor_tensor(out=msl, in0=msl, scalar=0.5, in1=o_sb[0:np_, h, :],
                                                         op0=ALU.mult, op1=ALU.mult),
                  rd=[B_osb[h]] + mb, wr=mb)

        def gla_sample_state_update():
            for i in range(4):
                K.dma(K.sp, c_Sin, Sst[:, :, :], sg[i].rearrange("h d v -> d h v"), wr=B_S)
                for h in range(4):
                    psu, pbu = K.bank()
                    P(lambda: nc.tensor.matmul(psu[:, 0:256], lhsT=k2m[i // 2][:, i % 2, h * 128:(h + 1) * 128],
                                               rhs=vg[0:64, 0, h * 256:(h + 1) * 256], start=True, stop=True),
                      rd=[B_k2m[i // 2], B_vg[0][h // 2]], wr=[pbu])
                    V(lambda: nc.vector.scalar_tensor_tensor(out=Sst[:, h, :], in0=Sst[:, h, :],
                                                             scalar=ecl[:, h * 4 + i:h * 4 + i + 1],
                                                             in1=psu[:, 0:256], op0=ALU.mult, op1=ALU.add),
                      rd=[B_S[h], B_ecl_[0], pbu], wr=[B_S[h]])
                K.dma(K.act, c_S, gs[i].rearrange("h d v -> d h v"), Sst[:, :, :], rd=B_S)

        def run_tile(kind, b, ti):
            prompt = (kind == "p")
            nsub, np_ = (2, 128) if prompt else (1, 64)
            ntok = nsub * np_
            rslot = ti % 3
            kcol0 = rslot * 256
            emit_kv = (prompt and ti >= NTILE - 2) or (not prompt)

            def dbg_dump(ph):
                for s in range(nsub):
                    dst = yp[b, ti * TT + s * 128: ti * TT + (s + 1) * 128, :] if prompt else ys[:, :]
                    if ph < 4:
                        K.dma(K.act, c_y[s], dst, mix_in[0:np_, s, :], rd=WA)
                    else:
                        K.dma(K.act, c_y[s], dst, xv[0:np_, s, :], rd=[B_x[s]])

            for s in range(nsub):
                src = xp[b, ti * TT + s * 128: ti * TT + (s + 1) * 128, :] if prompt else xs[:, :]
                K.dma(K.sp, c_x[s], xv[0:np_, s, :], src, wr=[B_x[s]])
            state["limit"] += SLABS_PER_TILE
            pump(state["cur"])
            norm_transpose(nsub, np_, gpre1, chunked=True)
            inherit(XB + B_k2m + B_pTs, XA)
            inherit(NB, NA)
            inherit(WA + B_cst + B_Sbf4, WB)

            if dbg_stop == 0:
                return dbg_dump(0)
            kv_i = 0
            for j in range(4):
                slab, sbuf = next_slab()
                for hb in range(4):
                    h = (j % 2) * 4 + hb
                    ps, pb = fm_block(slab, sbuf, hb * 128, ntok)
                    if j < 2:
                        dst, dbuf = qT[:, h, 0:ntok], B_qT[h]
                    elif prompt:
                        dst, dbuf = kTr[:, h, kcol0:kcol0 + ntok], B_kT[h][rslot]
                    else:
                        dst, dbuf = kTr[:, h, 512:512 + ntok], B_kT[h][2]
                    copy_av(dst, ps[:, 0:ntok], [pb], [dbuf])
                if j >= 2 and emit_kv:
                    for s in range(nsub):
                        ps, pb = tm_block(slab, sbuf, s, np_)
                        bs = kv_i % 2
                        kv_i += 1
                        copy_av(kstb[bs][0:np_, :], ps[0:np_, :], [pb], [B_kst[bs]])
                        c0 = (j - 2) * 512
                        if prompt:
                            r0 = (ti - (NTILE - 2)) * TT + s * 128
                            dst = kp[b, r0:r0 + 128, c0:c0 + 512]
                        else:
                            dst = ksn[:, c0:c0 + 512]
                        K.dma(K.act, c_kst[bs], dst, kstb[bs][0:np_, :], rd=[B_kst[bs]])
            for j in range(2):
                slab, sbuf = next_slab()
                for s in range(nsub):
                    ps, pb = tm_block(slab, sbuf, s, np_)
                    blk = (2 * ti + s) % 6 if prompt else 4
                    dst = vr[0:np_, blk, 4 * j:4 * j + 4, 0:128]
                    src = ps[0:np_, :].rearrange("p (h d) -> p h d", h=4)
                    copy_av(dst, src, [pb], [B_vr[blk][j]])
                    if emit_kv:
                        bs = kv_i % 2
                        kv_i += 1
                        copy_av(vstb[bs][0:np_, :], ps[0:np_, :], [pb], [B_vst[bs]])
                        c0 = j * 512
                        if prompt:
                            r0 = (ti - (NTILE - 2)) * TT + s * 128
                            dst2 = vp[b, r0:r0 + 128, c0:c0 + 512]
                        else:
                            dst2 = vsn[:, c0:c0 + 512]
                        K.dma(K.act, c_vst[bs], dst2, vstb[bs][0:np_, :], rd=[B_vst[bs]])
            for j in range(2):
                slab, sbuf = next_slab()
                for s in range(nsub):
                    ps, pb = tm_block(slab, sbuf, s, np_)
                    if j == 0:
                        A(lambda: nc.scalar.mul(out=qg[0:np_, s, :], in_=ps[0:np_, :], mul=GLA_QSCALE),
                          rd=[pb], wr=[B_qg[s]])
                    else:
                        V(lambda: nc.vector.tensor_copy(out=kg[0:np_, s, :], in_=ps[0:np_, :]), rd=[pb], wr=[B_kg[s]])
            for j in range(2):
                slab, sbuf = next_slab()
                for s in range(nsub):
                    ps, pb = tm_block(slab, sbuf, s, np_)
                    copy_av(vg[0:np_, s, j * 512:(j + 1) * 512], ps[0:np_, :], [pb], [B_vg[s][j]])
            ps, pb = K.bank()
            for kc in range(NKC):
                P(lambda: nc.tensor.matmul(ps[0:16, 0:ntok], lhsT=wlo[:, kc, :], rhs=actT[:, kc, 0:ntok],
                                           start=(kc == 0), stop=(kc == NKC - 1)),
                  rd=[B_actT[kc]] + CONST, wr=[pb])
            V(lambda: nc.vector.tensor_copy(out=aloT[0:16, 0:ntok], in_=ps[0:16, 0:ntok]), rd=[pb], wr=[B_alo])
            load_gpost(d_gpost1)

            if dbg_stop == 1:
                return dbg_dump(1)
            rg = {}
            RGS = [(0, 0), (0, 1), (1, 0), (1, 1)]

            def rg_piece(blk, piece, npieces):
                j, s = RGS[blk]
                if piece == 0:
                    if s == 0:
                        rg["slab"] = next_slab()
                    rg["bank"] = K.pin_bank()
                slab, sbuf = rg["slab"]
                bi, ps, pb = rg["bank"]
                per = NKC // npieces
                for kc in range(piece * per, (piece + 1) * per):
                    P(lambda: nc.tensor.matmul(ps[0:np_, 0:512], lhsT=actT[:, kc, s * np_:(s + 1) * np_],
                                               rhs=slab[:, kc, :], start=(kc == 0), stop=(kc == NKC - 1)),
                      rd=[sbuf, B_actT[kc]], wr=[pb])
                if piece == npieces - 1:
                    copy_av(mix_in[0:np_, s, 1024 + j * 512:1024 + (j + 1) * 512], ps[0:np_, :], [pb],
                            B_mi[s][8 + 4 * j:12 + 4 * j])
                    K.unpin_bank(bi)

            def mix_transposes(kcs):
                for kc in kcs:
                    ps, pb = K.bank()
                    for s in range(nsub):
                        P(lambda: nc.tensor.transpose(out=ps[:, s * np_:(s + 1) * np_],
                                                      in_=mix_in[0:np_, s, kc * 128:(kc + 1) * 128],
                                                      identity=identf[0:np_, 0:np_]),
                          rd=[B_mi[s][kc]] + CONST, wr=[pb])
                    copy_av(actT[:, kc, 0:ntok], ps[:, 0:ntok], [pb], [B_actT[kc]])

            if prompt:
                if ti == 0:
                    V(lambda: nc.vector.memset(Sst[:, :, :], 0.0), wr=B_S)
                    V(lambda: nc.vector.memset(Sbf[:, :, :], 0.0), wr=B_Sbf)
                g0 = gla_front(True, 0, np_, 0)

                def extra(u):
                    rg_piece((u - 1) // 4, (u - 1) % 4, 4)
                    next(g0, None)

                attention_prompt(ti, extra=extra)
                if dbg_stop == 2:
                    return dbg_dump(2)
                for _ in g0:
                    pass
                mix_transposes(range(0, 8))
                g1 = gla_front(True, 1, np_, 1)

                def hook():
                    next(g1, None)
                    next(g1, None)

                gla_back(True, 0, np_, 0, hook=hook)
                for _ in g1:
                    pass
                gla_back(True, 1, np_, 1)
                if ti == NTILE - 1:
                    K.dma(K.act, c_S, gp[b].rearrange("h d v -> d h v"), Sst[:, :, :], rd=B_S)
            else:
                rg_piece(0, 0, 1)
                rg_piece(2, 0, 1)
                attention_sample()
                inherit(B_Sbf4, B_cst)
                for i in range(4):
                    K.dma(K.sp, c_Sin, Sst[:, :, :], sg[i].rearrange("h d v -> d h v"), wr=B_S)
                    V(lambda: nc.vector.tensor_copy(out=Sbf4[:, i, :, :], in_=Sst[:, :, :]), rd=B_S, wr=[B_Sbf4[i]])
                for _ in gla_front(False, 0, np_, 0):
                    pass
                gla_back(False, 0, np_, 0)
                gla_sample_state_update()
            if dbg_stop == 3.5:
                return dbg_dump(3.5)
            mix_transposes(range(8, NKC) if prompt else range(NKC))
            inherit(XA, XB + B_k2m + B_pTs)
            inherit(NA, NB)
            for s in range(nsub):
                src = xp[b, ti * TT + s * 128: ti * TT + (s + 1) * 128, :] if prompt else xs[:, :]
                K.dma(K.act, c_x2[s], xv[0:np_, s, :], src, wr=[B_x[s]])
            for cb in range(4):
                slab, sbuf = next_slab()
                for s in range(nsub):
                    ps, pb = tm_block(slab, sbuf, s, np_)
                    evac_post(ps, pb, s, cb, np_, junk, B_junk)
            post_norm_residual(nsub, np_)

            if dbg_stop == 4:
                return dbg_dump(4)
            norm_transpose(nsub, np_, gpre2)
            inherit(WB, WA + B_cst + B_Sbf4)
            inherit(NC_, NA)

            if dbg_stop == 5:
                return dbg_dump(5)
            load_gpost(d_gpost2)
            nseq, L = (1, 256) if prompt else (4, 16)
            if prompt and ti == 0:
                V(lambda: nc.vector.memset(carry[:, :, :], 0.0), wr=B_carry)
            if not prompt:
                for c in range(11):
                    bs = c % 2
                    K.dma(K.sp, c_cvin[bs], cvb[bs], sc[:, c * 512:(c + 1) * 512], wr=[B_cvb[bs]])
                    ps, pb = K.bank()
                    for q in range(4):
                        P(lambda: nc.tensor.transpose(out=ps[:, q * 8:(q + 1) * 8], in_=cvb[bs][:, q * 128:(q + 1) * 128],
                                                      identity=identf[0:8, 0:8]), rd=[B_cvb[bs]] + CONST, wr=[pb])
                    V(lambda: nc.vector.tensor_copy(out=carry[:, 4 * c:4 * c + 4, :],
                                                    in_=ps[:, 0:32].rearrange("p (q r) -> p q r", q=4)),
                      rd=[pb], wr=B_carry[4 * c:4 * c + 4])
            for sl in range(22):
                gslab, gsbuf = next_slab()
                for fb in range(2):
                    j = sl * 2 + fb
                    bs = j % 2
                    psg, pbg = fm_block(gslab, gsbuf, fb * 128, ntok)
                    psv, pbv = fm_block(gslab, gsbuf, 256 + fb * 128, ntok)
                    gsv = gsb[bs][:, 0:nseq * (L + 2)].rearrange("p (i t) -> p i t", i=nseq)
                    ccv = ccb[bs][:, 0:ntok].rearrange("p (i t) -> p i t", i=nseq)
                    sgv = sgb[bs][:, 0:ntok].rearrange("p (i t) -> p i t", i=nseq)
                    V(lambda: nc.vector.tensor_copy(out=gsv[:, :, 0:2],
                                                    in_=carry[:, j, 0:2 * nseq].rearrange("p (i r) -> p i r", i=nseq)),
                      rd=[B_carry[j]], wr=[B_gs[bs]])
                    A(lambda: nc.scalar.copy(out=gsv[:, :, 2:L + 2],
                                             in_=psg[:, 0:ntok].rearrange("p (i t) -> p i t", i=nseq)),
                      rd=[pbg], wr=[B_gs[bs]])
                    A(lambda: nc.scalar.activation(out=ccv, in_=psg[:, 0:ntok].rearrange("p (i t) -> p i t", i=nseq),
                                                   func=AF.Identity, scale=convw[:, j, 2:3], bias=convw[:, j, 3:4]),
                      rd=[pbg] + CONST, wr=[B_cc[bs]])
                    V(lambda: nc.vector.tensor_copy(out=carry[:, j, 0:2 * nseq].rearrange("p (i r) -> p i r", i=nseq),
                                                    in_=gsv[:, :, L:L + 2]),
                      rd=[B_gs[bs]], wr=[B_carry[j]])
                    V(lambda: nc.vector.scalar_tensor_tensor(out=ccv, in0=gsv[:, :, 1:L + 1], scalar=convw[:, j, 1:2],
                                                             in1=ccv, op0=ALU.mult, op1=ALU.add),
                      rd=[B_gs[bs], B_cc[bs]] + CONST, wr=[B_cc[bs]])
                    V(lambda: nc.vector.scalar_tensor_tensor(out=ccv, in0=gsv[:, :, 0:L], scalar=convw[:, j, 0:1],
                                                             in1=ccv, op0=ALU.mult, op1=ALU.add),
                      rd=[B_gs[bs], B_cc[bs]] + CONST, wr=[B_cc[bs]])
                    A(lambda: nc.scalar.activation(out=sgv, in_=ccv, func=AF.Silu), rd=[B_cc[bs]], wr=[B_sg[bs]])
                    V(lambda: nc.vector.tensor_tensor(out=actb[:, j, 0:ntok], in0=psv[:, 0:ntok], in1=sgb[bs][:, 0:ntok],
                                                      op=ALU.mult), rd=[pbv, B_sg[bs]], wr=[B_act[j]])
            pin_tables()
            if (prompt and ti == NTILE - 1) or not prompt:
                nr = 2 * nseq
                for c in range(11):
                    bs = c % 2
                    ps, pb = K.bank()
                    for q in range(4):
                        P(lambda: nc.tensor.transpose(out=ps[0:nr, q * 128:(q + 1) * 128], in_=carry[:, 4 * c + q, 0:nr],
                                                      identity=identf[:, :]), rd=[B_carry[4 * c + q]] + CONST, wr=[pb])
                    V(lambda: nc.vector.tensor_copy(out=cvb[bs][0:nr, :], in_=ps[0:nr, 0:512]), rd=[pb], wr=[B_cvb[bs]])
                    dst = cp[2 * b:2 * b + 2, c * 512:(c + 1) * 512] if prompt else cs[:, c * 512:(c + 1) * 512]
                    K.dma(K.act, c_cvb[bs], dst, cvb[bs][0:nr, :], rd=[B_cvb[bs]])
            inherit(NA, NC_)

            if dbg_stop == 6:
                return dbg_dump(6)
            for cb in range(4):
                banks = [K.bank() for s in range(nsub)]
                for kgi in range(3):
                    nk = 16 if kgi < 2 else 12
                    slab, sbuf = next_slab()
                    for s in range(nsub):
                        ps, pb = banks[s]
                        for kc in range(nk):
                            fc = kgi * 16 + kc
                            P(lambda: nc.tensor.matmul(ps[0:np_, 0:512], lhsT=actb[:, fc, s * np_:(s + 1) * np_],
                                                       rhs=slab[:, kc, :], start=(fc == 0), stop=(fc == NFC - 1)),
                              rd=[sbuf, B_act[fc]], wr=[pb])
                for s in range(nsub):
                    ps, pb = banks[s]
                    evac_post(ps, pb, s, cb, np_, junk_b, B_junkb)
            post_norm_residual(nsub, np_, into_n=True)
            for s in range(nsub):
                dst = yp[b, ti * TT + s * 128: ti * TT + (s + 1) * 128, :] if prompt else ys[:, :]
                K.dma(K.act, c_y[s], dst, nv[0:np_, s, :], rd=[B_n[s]])

        junk_b = sb("junk_b", [128, 512], BF16)
        B_junkb = Buf("junkb")

        pin_tables()
        if with_sample:
            run_tile("s", 0, 0)
            pool_ok[0] = True
        for b in range(2):
            for ti in range(NTILE):
                if dbg_tiles is not None and (b, ti) not in dbg_tiles:
                    continue
                run_tile("p", b, ti)
                pool_ok[0] = True

        for c in K.out_chans:
            if c.cnt:
                nc.sync.wait_ge(c.sem, c.cnt)
    return nc


def _host_tables(rel_bias):
    rb = np.asarray(rel_bias, np.float32)[0]
    kl = np.arange(128)[:, None, None]
    blk = np.arange(5)[None, :, None]
    q = np.arange(128)[None, None, :]
    d = 512 - 128 * blk + q - kl
    idx = np.clip(d, -128, 128) + 128
    tab = rb[:, idx]
    tab = np.ascontiguousarray(np.transpose(tab, (1, 0, 2, 3))).copy()
    maskA = (q >= 64) & (kl < 64)
    maskB = (q < 64) & (kl >= 64)
    tab[:, :, 0, :][np.broadcast_to(maskA[:, 0, :][:, None, :], (128, 8, 128))] = MASKV
    tab[:, :, 4, :][np.broadcast_to(maskB[:, 0, :][:, None, :], (128, 8, 128))] = MASKV
    biasS = np.ascontiguousarray(tab[:, :, 0:4, 0:16]).reshape(128, 8 * 64)
    tab = np.ascontiguousarray(tab[:, :, [0, 3, 4, 1, 2], :])
    biasT = tab.reshape(128, 8 * 640)
    kk = np.arange(64)[:, None]
    qq = np.arange(64)[None, :]
    dn = (qq % 16) - (kk % 16)
    tn = rb[:, dn + 128]
    tn = np.ascontiguousarray(np.transpose(tn, (1, 0, 2))).copy()
    cross = (kk // 16) != (qq // 16)
    tn[np.broadcast_to(cross[:, None, :], (64, 8, 64))] = MASKV
    biasN = tn.reshape(64, 8 * 64)
    t = np.arange(64)
    seqsel = (t[:, None] // 16 == np.arange(4)[None, :]).astype(np.float32)
    same = (t[:, None] // 16 == t[None, :] // 16)
    Ub = (same & (t[:, None] <= t[None, :])).astype(np.float32)
    onesb = same.astype(np.float32)
    smask = np.concatenate([seqsel, Ub, onesb], axis=1).astype(np.float32)
    return biasT.astype(np.float32), biasN.astype(np.float32), smask, biasS.astype(np.float32)


_NC_CACHE = {}


def kernel(x_prompt, x_sample, cache_k, cache_v, state_gla, state_conv, g_mix_pre, w_in, w_gate_up,
           b_gate, rel_bias, g_gla, w_o, g_mix_post, g_ffn_pre, w_up, w_conv, b_conv, w_down, g_ffn_post):
    f = lambda a: np.ascontiguousarray(np.asarray(a, dtype=np.float32))
    x_prompt, x_sample = f(x_prompt), f(x_sample)
    cache_k, cache_v = f(cache_k)[0], f(cache_v)[0]
    state_gla, state_conv = f(state_gla)[0], f(state_conv)[0]
    biasT, biasN, smask, biasS = _host_tables(rel_bias)
    colmajor = lambda g: np.ascontiguousarray(f(g)[0].reshape(16, 128).T)
    rep = lambda v, n=128: np.ascontiguousarray(np.broadcast_to(f(v).reshape(1, -1), (n, f(v).size)))
    wc = f(w_conv)[0]
    bc = f(b_conv)[0]
    convw = np.stack([wc[0], wc[1], wc[2], bc], axis=-1)
    convw = np.ascontiguousarray(convw.reshape(NFC, 128, 4).transpose(1, 0, 2).reshape(128, NFC * 4))
    shared = {
        "w_in": f(w_in)[0], "w_o": f(w_o)[0], "w_up": f(w_up)[0], "w_down": f(w_down)[0],
        "gpre1": colmajor(g_mix_pre), "gpre2": colmajor(g_ffn_pre),
        "gpost1": rep(g_mix_post), "gpost2": rep(g_ffn_post),
        "brep": rep(b_gate), "ggla": rep(g_gla), "wg": f(w_gate_up)[0],
        "convw": convw, "biasT": biasT, "biasN": biasN, "smask": smask, "biasS": biasS,
        "cbias": np.ascontiguousarray(np.broadcast_to(f(rel_bias)[0][:, 256].reshape(1, 8), (128, 8))),
    }
    in_maps = []
    for c in range(NCORES):
        m = dict(shared)
        m["xp"] = x_prompt[2 * c:2 * c + 2]
        m["xs"] = np.ascontiguousarray(x_sample[4 * c:4 * c + 4].reshape(64, D))
        m["ck"] = np.ascontiguousarray(cache_k[4 * c:4 * c + 4].reshape(4, 512, 1024))
        m["cv"] = np.ascontiguousarray(cache_v[4 * c:4 * c + 4].reshape(4, 512, 1024))
        m["sg"] = state_gla[4 * c:4 * c + 4]
        m["sc"] = np.ascontiguousarray(state_conv[4 * c:4 * c + 4].reshape(8, DFF))
        in_maps.append(m)
    if "nc" not in _NC_CACHE:
        _NC_CACHE["nc"] = build_program()
    nc = _NC_CACHE["nc"]
    res = run_bass_kernel_spmd(nc, in_maps, core_ids=list(range(NCORES)))
    R = res.results
    cat = lambda k: np.concatenate([np.asarray(r[k]) for r in R], axis=0)
    y_prompt = cat("yp").reshape(16, SEQ, D)
    y_sample = cat("ys").reshape(32, 16, D)
    k_prompt = cat("kp").reshape(1, 16, 512, 8, 128)
    v_prompt = cat("vp").reshape(1, 16, 512, 8, 128)
    gla_prompt = cat("gp").reshape(1, 16, 4, 128, 256)
    conv_prompt = cat("cp").reshape(1, 16, 2, DFF)
    k_sample = cat("ksn").reshape(1, 32, 16, 8, 128)
    v_sample = cat("vsn").reshape(1, 32, 16, 8, 128)
    gla_sample = cat("gs").reshape(1, 32, 4, 128, 256)
    conv_sample = cat("cs").reshape(1, 32, 2, DFF)
    return (y_prompt.astype(np.float32), y_sample.astype(np.float32), k_prompt, v_prompt, gla_prompt,
            conv_prompt, k_sample, v_sample, gla_sample, conv_sample)
```

```python
import numpy as np
from contextlib import ExitStack
import concourse.bass as bass
import concourse.mybir as mybir
from concourse.bass_utils import run_bass_kernel_spmd

F32 = mybir.dt.float32
BF16 = mybir.dt.bfloat16
AF = mybir.ActivationFunctionType
ALU = mybir.AluOpType
AX = mybir.AxisListType

NCORES = 8
D = 2048
NKC = 16
SEQ = 2048
TT = 256
NTILE = SEQ // TT
DFF = 5632
NFC = 44
INW = 6160
EPS = 1e-6
MASKV = -30000.0
ATT_SCALE = 128.0 ** -0.5
GLA_QSCALE = 128.0 ** -0.5
NSLOT = 3
WITH_SAMPLE = True


class Buf:
    __slots__ = ("name", "wr", "rd", "excl")

    def __init__(self, name, excl=False):
        self.name = name
        self.wr = {}
        self.rd = {}
        self.excl = excl


class Eng:
    def __init__(self, name, h, sem):
        self.name = name
        self.h = h
        self.sem = sem
        self.cnt = 0
        self.seen = {}


class Chan:
    def __init__(self, sem):
        self.sem = sem
        self.cnt = 0


def inherit(new_bufs, old_bufs):
    allev = {}
    for b in old_bufs:
        for d in (b.wr, b.rd):
            for k, (sem, val) in d.items():
                if allev.get(k, (None, 0))[1] < val:
                    allev[k] = (sem, val)
    for b in new_bufs:
        b.wr = {}
        b.rd = dict(allev)


class KB:
    def __init__(self, nc, es):
        self.nc = nc
        self.es = es
        self.nsem = 0
        self.pe = Eng("pe", nc.tensor, self.sem())
        self.act = Eng("act", nc.scalar, self.sem())
        self.dve = Eng("dve", nc.vector, self.sem())
        self.pool = Eng("pool", nc.gpsimd, self.sem())
        self.sp = Eng("sp", nc.sync, None)
        self.out_chans = []
        self.ps = [es.enter_context(nc.psum_tensor(f"ps{i}", [128, 512], F32)) for i in range(8)]
        self.pb = [Buf(f"ps{i}", excl=True) for i in range(8)]
        self.rot = list(range(8))
        self.flip = 0

    def sem(self):
        self.nsem += 1
        return self.es.enter_context(self.nc.semaphore(f"sem{self.nsem}"))

    def chan(self, out=False):
        c = Chan(self.sem())
        if out:
            self.out_chans.append(c)
        return c

    def sb(self, name, shape, dt):
        return self.es.enter_context(self.nc.sbuf_tensor("sb_" + name, shape, dt))

    def bank(self):
        i = self.rot.pop(0)
        self.rot.append(i)
        return self.ps[i], self.pb[i]

    def pin_bank(self):
        i = self.rot.pop(0)
        return i, self.ps[i], self.pb[i]

    def unpin_bank(self, i):
        self.rot.append(i)

    def _waits(self, eng, rd, wr):
        need = {}

        def add(d, same_ok):
            for k, (sem, val) in d.items():
                if (sem is eng.sem) and not same_ok:
                    continue
                if need.get(k, (None, 0))[1] < val:
                    need[k] = (sem, val)

        strict = eng is not self.pe
        for b in rd:
            add(b.wr, True)
            if b.excl:
                add(b.rd, False)
        for b in wr:
            add(b.wr, strict)
            add(b.rd, strict)
        for k, (sem, val) in need.items():
            if eng.seen.get(k, 0) < val:
                eng.h.wait_ge(sem, val)
                eng.seen[k] = val

    def op(self, eng, fn, rd=(), wr=()):
        self._waits(eng, rd, wr)
        ins = fn()
        eng.cnt += 1
        ins.then_inc(eng.sem, 1)
        k = id(eng.sem)
        ev = (eng.sem, eng.cnt)
        for b in wr:
            b.wr = {k: ev}
            b.rd = {}
        for b in rd:
            b.rd[k] = ev

    def P(self, fn, rd=(), wr=()):
        self.op(self.pe, fn, rd, wr)

    def A(self, fn, rd=(), wr=()):
        self.op(self.act, fn, rd, wr)

    def V(self, fn, rd=(), wr=()):
        self.op(self.dve, fn, rd, wr)

    def G(self, fn, rd=(), wr=()):
        self.op(self.pool, fn, rd, wr)

    def AV(self, fa, fv, rd=(), wr=()):
        self.flip ^= 1
        if self.flip:
            self.op(self.act, fa, rd, wr)
        else:
            self.op(self.dve, fv, rd, wr)

    def dma(self, q, chan, out, in_, rd=(), wr=(), **kw):
        self._waits(q, rd, wr)
        ins = q.h.dma_start(out=out, in_=in_, **kw)
        chan.cnt += 16
        ins.then_inc(chan.sem, 16)
        k = id(chan.sem)
        ev = (chan.sem, chan.cnt)
        for b in wr:
            b.wr = {k: ev}
            b.rd = {}
        for b in rd:
            b.rd[k] = ev


def build_program(dbg_tiles=None, with_sample=WITH_SAMPLE, dbg_stop=None):
    nc = bass.Bass("TRN2", target_bir_lowering=False)

    def din(name, shape):
        return nc.dram_tensor(name, shape, F32, kind="ExternalInput").ap()

    def dout(name, shape):
        return nc.dram_tensor(name, shape, F32, kind="ExternalOutput").ap()

    xp = din("xp", [2, SEQ, D])
    xs = din("xs", [64, D])
    ck = din("ck", [4, 512, 1024])
    cv = din("cv", [4, 512, 1024])
    sg = din("sg", [4, 4, 128, 256])
    sc = din("sc", [8, DFF])
    w_in = din("w_in", [D, INW])
    w_o = din("w_o", [D, D])
    w_up = din("w_up", [D, 2 * DFF])
    w_down = din("w_down", [DFF, D])
    d_gpre1 = din("gpre1", [128, 16])
    d_gpre2 = din("gpre2", [128, 16])
    d_gpost1 = din("gpost1", [128, D])
    d_gpost2 = din("gpost2", [128, D])
    d_brep = din("brep", [128, 512])
    d_ggla = din("ggla", [128, 256])
    d_wg = din("wg", [16, 512])
    d_convw = din("convw", [128, NFC * 4])
    d_biasT = din("biasT", [128, 8 * 640])
    d_biasN = din("biasN", [64, 8 * 64])
    d_cbias = din("cbias", [128, 8])
    d_biasS = din("biasS", [128, 8 * 64])
    d_smask = din("smask", [64, 132])

    yp = dout("yp", [2, SEQ, D])
    ys = dout("ys", [64, D])
    kp = dout("kp", [2, 512, 1024])
    vp = dout("vp", [2, 512, 1024])
    gp = dout("gp", [2, 4, 128, 256])
    cp = dout("cp", [4, DFF])
    ksn = dout("ksn", [64, 1024])
    vsn = dout("vsn", [64, 1024])
    gs = dout("gs", [4, 4, 128, 256])
    cs = dout("cs", [8, DFF])

    win_b = nc.dram_tensor("win_b", [12, 128, 16, 512], BF16, kind="Internal").ap()
    wo_b = nc.dram_tensor("wo_b", [4, 128, 16, 512], BF16, kind="Internal").ap()
    wup_b = nc.dram_tensor("wup_b", [22, 128, 16, 512], BF16, kind="Internal").ap()
    wdn_b = nc.dram_tensor("wdn_b", [12, 128, 16, 512], BF16, kind="Internal").ap()

    es = ExitStack()
    with es:
        K = KB(nc, es)
        P, A, V, G, AV = K.P, K.A, K.V, K.G, K.AV
        sb = K.sb

        gpre1 = sb("gpre1", [128, 16], F32)
        gpre2 = sb("gpre2", [128, 16], F32)
        gpost = sb("gpost", [128, D], F32)
        brep = sb("brep", [128, 512], F32)
        ggla = sb("ggla", [128, 256], F32)
        convw = sb("convw", [128, NFC, 4], F32)
        biasT = sb("biasT", [128, 8, 640], F32)
        biasN = sb("biasN", [64, 8, 64], F32)
        smask = sb("smask", [64, 132], F32)
        cbias = sb("cbias", [128, 8], F32)
        biasS = sb("biasS", [128, 8, 4, 16], F32)
        smask_bf = sb("smask_bf", [64, 132], BF16)
        wg_bf = sb("wg_bf", [16, 512], BF16)
        wlo = sb("wlo", [128, 16, 16], BF16)
        identf = sb("identf", [128, 128], F32)
        Ubf = sb("Ubf", [128, 128], BF16)
        onesbf = sb("onesbf", [128, 128], BF16)
        oneT = sb("oneT", [128, 1], F32)
        epsT = sb("epsT", [128, 1], F32)
        mhalf = sb("mhalf", [128, 4], F32)
        B_const = Buf("const")
        c_const = K.chan()
        for (t, d) in [(gpre1, d_gpre1), (gpre2, d_gpre2), (brep, d_brep), (ggla, d_ggla), (smask, d_smask), (cbias, d_cbias)]:
            K.dma(K.sp, c_const, t[:, :], d[:, :])
        K.dma(K.sp, c_const, convw[:, :, :], d_convw.rearrange("p (j c) -> p j c", c=4))
        K.dma(K.sp, c_const, biasT[:, :, :], d_biasT.rearrange("p (h k) -> p h k", h=8))
        K.dma(K.sp, c_const, biasN[:, :, :], d_biasN.rearrange("p (h k) -> p h k", h=8))
        K.dma(K.sp, c_const, biasS[:, :, :, :], d_biasS.rearrange("p (h b t) -> p h b t", h=8, b=4))
        c_const_g = K.chan()
        K.dma(K.pool, c_const_g, wg_bf[:, :], d_wg[:, :])
        K.dma(K.pool, c_const_g, wlo[:, :, :], w_in[:, 6144:6160].rearrange("(kc p) n -> p kc n", p=128))
        B_const.wr = {id(c_const.sem): (c_const.sem, c_const.cnt), id(c_const_g.sem): (c_const_g.sem, c_const_g.cnt)}
        B_gpost = Buf("gpost")
        c_gpost = K.chan()

        B_gen = Buf("gen")
        G(lambda: nc.gpsimd.memset(identf[:, :], 1.0), wr=[B_gen])
        G(lambda: nc.gpsimd.affine_select(out=identf[:, :], in_=identf[:, :], pattern=[[1, 128]],
                                          compare_op=ALU.is_equal, fill=0.0, base=0, channel_multiplier=-1),
          rd=[B_gen], wr=[B_gen])
        G(lambda: nc.gpsimd.memset(Ubf[:, :], 1.0), wr=[B_gen])
        G(lambda: nc.gpsimd.affine_select(out=Ubf[:, :], in_=Ubf[:, :], pattern=[[1, 128]],
                                          compare_op=ALU.is_ge, fill=0.0, base=0, channel_multiplier=-1),
          rd=[B_gen], wr=[B_gen])
        G(lambda: nc.gpsimd.memset(onesbf[:, :], 1.0), wr=[B_gen])
        G(lambda: nc.gpsimd.memset(oneT[:, :], 1.0), wr=[B_gen])
        G(lambda: nc.gpsimd.memset(epsT[:, :], EPS), wr=[B_gen])
        G(lambda: nc.gpsimd.memset(mhalf[:, :], -0.5), wr=[B_gen])
        G(lambda: nc.gpsimd.tensor_copy(out=smask_bf[:, :], in_=smask[:, :]), rd=[B_const], wr=[B_gen])
        CONST = [B_const, B_gen]

        def convert(lst, chanW):
            for (src_ap, dst_ap) in lst:
                K.dma(K.pool, chanW, dst_ap, src_ap)
            b = Buf("wscr")
            b.wr = {id(chanW.sem): (chanW.sem, chanW.cnt)}
            return b

        def kcview(ap2d):
            return ap2d.rearrange("(kc p) n -> p kc n", p=128)

        B_win = [convert([(kcview(w_in[:, j * 512:(j + 1) * 512]), win_b[j]) for j in range(2 * g, 2 * g + 2)], K.chan())
                 for g in range(6)]
        B_wo = convert([(kcview(w_o[:, j * 512:(j + 1) * 512]), wo_b[j]) for j in range(4)], K.chan())
        B_wup = []
        for g, (j0, j1) in enumerate([(2 * k_, 2 * k_ + 2) for k_ in range(11)]):
            lst = []
            for j in range(j0, j1):
                lst.append((kcview(w_up[:, j * 256:(j + 1) * 256]), wup_b[j][:, :, 0:256]))
                lst.append((kcview(w_up[:, DFF + j * 256:DFF + (j + 1) * 256]), wup_b[j][:, :, 256:512]))
            B_wup.append((j1, convert(lst, K.chan())))
        B_wdn = []
        for g in range(4):
            lst = []
            for cb in range(g, g + 1):
                for kgi in range(3):
                    nk = 16 if kgi < 2 else 12
                    lst.append((kcview(w_down[kgi * 2048: kgi * 2048 + nk * 128, cb * 512:(cb + 1) * 512]),
                                wdn_b[cb * 3 + kgi][:, 0:nk, :]))
            B_wdn.append(convert(lst, K.chan()))

        def wup_buf(j):
            for (j1, b_) in B_wup:
                if j < j1:
                    return b_

        kTr = sb("kTr", [128, 8, 768], BF16)
        vr = sb("vr", [128, 6, 8, 129], BF16)
        B_kT = [[Buf(f"kT{h}_{r}") for r in range(3)] for h in range(8)]
        B_vr = [[Buf(f"vr{b}_{j}") for j in range(2)] for b in range(6)]
        V(lambda: nc.vector.memset(vr[:, :, :, 128:129], 1.0), wr=[b for bb in B_vr for b in bb])
        slabs = [sb(f"slab{i}", [128, 16, 512], BF16) for i in range(NSLOT)]
        B_slab = [Buf(f"slab{i}") for i in range(NSLOT)]
        c_slab = [K.chan() for i in range(NSLOT)]
        Sst = sb("Sst", [128, 4, 256], F32)
        Sbf = sb("Sbf", [128, 4, 256], BF16)
        B_S = [Buf(f"S{h}") for h in range(4)]
        B_Sbf = [Buf(f"Sbf{h}") for h in range(4)]
        c_S = K.chan(out=True)
        c_Sin = K.chan()
        carry = sb("carry", [128, NFC, 8], F32)
        B_carry = [Buf(f"carry{j}") for j in range(NFC)]
        qtTm = sb("qtTm", [128, 4, 4, 64], BF16)
        B_qtTm = Buf("qtTm")
        pTn = sb("pTn", [64, 4, 64], BF16)
        B_pTn = [Buf(f"pTn{i}") for i in range(4)]
        stats = sb("stats", [128, 64], F32)
        B_st = {}

        def st(name, c0, n):
            B_st[name] = Buf("st_" + name)
            return stats[:, c0:c0 + n]

        ss = st("ss", 0, 2)
        ms = st("ms", 2, 2)
        sd = st("sd", 4, 2)
        rstd = st("rstd", 6, 2)
        ssq = st("ssq", 8, 8)
        oss = st("oss", 16, 4)
        orstd = st("orstd", 20, 4)
        oms = st("oms", 24, 4)
        osd = st("osd", 28, 4)
        rinv = st("rinv", 32, 2)
        den = st("den", 34, 2)
        ecl = st("ecl", 36, 16)

        Xr = sb("Xr", [128, 4096], F32)
        Nr = sb("Nr", [128, 4096], F32)
        actT = sb("actT", [128, 16, TT], BF16)
        B_actT = [Buf(f"actT{kc}") for kc in range(16)]
        Wr = sb("Wr", [128, 9856], F32)

        xv = Xr[:, :].rearrange("p (s d) -> p s d", s=2)
        B_x = [Buf("x0"), Buf("x1")]
        qg = Xr[:, 0:1024].rearrange("p (s d) -> p s d", s=2)
        kg = Xr[:, 1024:2048].rearrange("p (s d) -> p s d", s=2)
        vg = Xr[:, 2048:3072].bitcast(BF16).rearrange("p (s d) -> p s d", s=2)
        qT = Xr[:, 3072:4096].bitcast(BF16).rearrange("p (h t) -> p h t", h=8)
        B_qg = [Buf("qg0"), Buf("qg1")]
        B_kg = [Buf("kg0"), Buf("kg1")]
        B_vg = [[Buf(f"vg{s}_{j}") for j in range(2)] for s in range(2)]
        B_qT = [Buf(f"qT{h}") for h in range(8)]
        XA = B_x
        XB = B_qg + B_kg + [b for bb in B_vg for b in bb] + B_qT
        k2m = [Xr[0:64, 512:1024].bitcast(BF16).rearrange("p (i d) -> p i d", i=2),
               Xr[0:64, 1536:2048].bitcast(BF16).rearrange("p (i d) -> p i d", i=2)]
        B_k2m = [Buf("k2m0"), Buf("k2m1")]
        pTs = Xr[:, 2560:3072].bitcast(BF16).rearrange("p (i b t) -> p i b t", i=4, b=4)
        B_pTs = [Buf(f"pTs{i}") for i in range(4)]
        nv = Nr[:, :].rearrange("p (s d) -> p s d", s=2)
        B_n = [Buf("n0"), Buf("n1")]
        kstb = [Nr[:, 0:512], Nr[:, 512:1024]]
        vstb = [Nr[:, 1024:1536], Nr[:, 1536:2048]]
        scb = [Nr[:, 2048:2688], Nr[:, 2688:3328]]
        pTb = [Nr[:, 3328:3648].bitcast(BF16), Nr[:, 3648:3968].bitcast(BF16)]
        B_kst = [Buf("kst0"), Buf("kst1")]
        B_vst = [Buf("vst0"), Buf("vst1")]
        B_sc = [Buf("sc0"), Buf("sc1")]
        B_pT = [Buf("pT0"), Buf("pT1")]
        c_kst = [K.chan(out=True), K.chan(out=True)]
        c_vst = [K.chan(out=True), K.chan(out=True)]
        NA = B_n
        NB = B_kst + B_vst + B_sc + B_pT
        NB_S = NB
        cvb = [Nr[0:8, 0:512], Nr[0:8, 512:1024]]
        B_cvb = [Buf("cvb0"), Buf("cvb1")]
        c_cvb = [K.chan(out=True), K.chan(out=True)]
        c_cvin = [K.chan(), K.chan()]
        NC_ = B_cvb
        mix_in = Wr[:, 0:4096].rearrange("p (s d) -> p s d", s=2)
        B_mi = [[Buf(f"mi{s}_{c}") for c in range(16)] for s in range(2)]
        o = 4096
        zt = Wr[:, o:o + 512]; o += 512
        sp_hi = Wr[:, o:o + 256].bitcast(BF16); o += 256
        sp_lo = Wr[:, o:o + 256].bitcast(BF16); o += 256
        E1 = Wr[:, o:o + 512]; o += 512
        Einv = Wr[:, o:o + 512]; o += 512
        ECL = Wr[:, o:o + 512]; o += 512
        k2_ = []; qtT_ = []; ktT_ = []
        for _par in range(2):
            k2_.append(Wr[:, o:o + 256].bitcast(BF16)); o += 256
            qtT_.append(Wr[:, o:o + 256].bitcast(BF16).rearrange("p (h t) -> p h t", h=4)); o += 256
            ktT_.append(Wr[:, o:o + 256].bitcast(BF16).rearrange("p (h t) -> p h t", h=4)); o += 256
        ATm = Wr[:, o:o + 256].bitcast(BF16).rearrange("p (b t) -> p b t", b=4); o += 256
        o_sb = Wr[:, o:o + 1024].rearrange("p (h v) -> p h v", h=4); o += 1024
        junk = Wr[:, o:o + 256].bitcast(BF16); o += 256
        aloT = Wr[:, o:o + 128].bitcast(BF16); o += 128
        assert o <= 9856, o
        B_zt, B_hi, B_lo, B_E1, B_Einv, B_ECL = [Buf(n) for n in ("zt", "hi", "lo", "E1", "Einv", "ECL")]
        B_ecl_ = [Buf("ecla"), Buf("eclb")]
        B_k2_ = [Buf("k2a"), Buf("k2b")]
        B_qtT_ = [Buf("qtTa"), Buf("qtTb")]
        B_ktT_ = [Buf("ktTa"), Buf("ktTb")]
        B_ATm = [Buf(f"ATm{h}") for h in range(4)]
        B_osb = [Buf(f"osb{h}") for h in range(4)]
        B_junk = Buf("junk")
        B_alo = Buf("aloT")
        WA = ([b for bb in B_mi for b in bb] + [B_zt, B_hi, B_lo, B_E1, B_Einv, B_ECL] + B_k2_ + B_qtT_ + B_ktT_
              + B_ATm + B_osb + [B_junk, B_alo])
        cst = Wr[:, 2048:4096].rearrange("p (b d) -> p b d", b=2)
        cstN = [Nr[:, 0:1024], Nr[:, 1024:2048]]
        CST = [cst[:, 0, :], cst[:, 1, :], cstN[0], cstN[1]]
        B_cst = [Buf(f"cst{i}") for i in range(4)]
        c_cst = [K.chan() for i in range(4)]
        Sbf4 = Wr[:, 2048:4096].bitcast(BF16).rearrange("p (i h v) -> p i h v", i=4, h=4)
        B_Sbf4 = [Buf(f"Sbf4_{i}") for i in range(4)]
        actb = Wr[:, 0:5632].bitcast(BF16).rearrange("p (j t) -> p j t", j=NFC)
        B_act = [Buf(f"act{j}") for j in range(NFC)]
        gsb = [Wr[:, 5632:5632 + 260], Wr[:, 5892:5892 + 260]]
        ccb = [Wr[:, 6152:6152 + 256], Wr[:, 6408:6408 + 256]]
        sgb = [Wr[:, 6664:6664 + 256], Wr[:, 6920:6920 + 256]]
        B_gs = [Buf("gs0"), Buf("gs1")]
        B_cc = [Buf("cc0"), Buf("cc1")]
        B_sg = [Buf("sg0"), Buf("sg1")]
        WB = B_act + B_gs + B_cc + B_sg

        c_x = [K.chan(), K.chan()]
        c_x2 = [K.chan(), K.chan()]
        c_y = [K.chan(out=True), K.chan(out=True)]

        plan = []
        state = {"cur": 0, "loaded": 0, "limit": 0}
        SLABS_PER_TILE = 50

        def tile_plan():
            l = []
            for j in range(12):
                l.append((win_b[j], 16, B_win[j // 2]))
            for j in range(4):
                l.append((wo_b[j], 16, B_wo))
            for sl in range(22):
                l.append((wup_b[sl], 16, wup_buf(sl)))
            for cb in range(4):
                for kgi in range(3):
                    l.append((wdn_b[cb * 3 + kgi], 16 if kgi < 2 else 12, B_wdn[cb]))
            return l

        NTILES_TOTAL = (2 * NTILE if dbg_tiles is None else len(dbg_tiles)) + (1 if with_sample else 0)
        for _ in range(NTILES_TOTAL):
            plan.extend(tile_plan())

        def pump(n):
            while state["loaded"] < min(n + NSLOT, len(plan), state["limit"]):
                m = state["loaded"]
                ap, nk, srcb = plan[m]
                K.dma(K.sp, c_slab[m % NSLOT], slabs[m % NSLOT][:, 0:nk, :], ap[:, 0:nk, :],
                      rd=[srcb], wr=[B_slab[m % NSLOT]])
                state["loaded"] += 1

        def next_slab():
            n = state["cur"]
            pump(n)
            state["cur"] += 1
            return slabs[n % NSLOT], B_slab[n % NSLOT]

        def next_slab_old():
            n = state["cur"]
            while state["loaded"] < min(n + NSLOT, len(plan)):
                m = state["loaded"]
                ap, nk, srcb = plan[m]
                K.dma(K.sp, c_slab[m % NSLOT], slabs[m % NSLOT][:, 0:nk, :], ap[:, 0:nk, :],
                      rd=[srcb], wr=[B_slab[m % NSLOT]])
                state["loaded"] += 1
            state["cur"] += 1
            return slabs[n % NSLOT], B_slab[n % NSLOT]

        pool_ok = [False]

        def pin_tables():
            pass

        B_ss = [Buf("ss0"), Buf("ss1")]
        B_ms = [Buf("ms0"), Buf("ms1")]
        B_sd = [Buf("sd0"), Buf("sd1")]
        B_rs = [Buf("rs0"), Buf("rs1")]
        B_ssq = [Buf("ssq0"), Buf("ssq1")]

        def rstd_s(s, np_, denom):
            V(lambda: nc.vector.tensor_scalar(out=ms[0:np_, s:s + 1], in0=ss[0:np_, s:s + 1], scalar1=1.0 / denom,
                                              scalar2=EPS, op0=ALU.mult, op1=ALU.add), rd=[B_ss[s]], wr=[B_ms[s]])
            if pool_ok[0]:
                G(lambda: nc.gpsimd.tensor_tensor(out=rstd[0:np_, s:s + 1], in0=ms[0:np_, s:s + 1],
                                                  in1=mhalf[0:np_, 0:1], op=ALU.pow),
                  rd=[B_ms[s]] + CONST, wr=[B_rs[s]])
            else:
                A(lambda: nc.scalar.activation(out=sd[0:np_, s:s + 1], in_=ms[0:np_, s:s + 1], func=AF.Sqrt),
                  rd=[B_ms[s]], wr=[B_sd[s]])
                V(lambda: nc.vector.reciprocal(out=rstd[0:np_, s:s + 1], in_=sd[0:np_, s:s + 1]),
                  rd=[B_sd[s]], wr=[B_rs[s]])

        def norm_transpose(nsub, np_, gcol, chunked=False):
            ntok = nsub * np_
            for s in range(nsub):
                if chunked:
                    A(lambda: nc.scalar.activation(out=actT[0:np_, 8 * s:8 * s + 8, :],
                                                   in_=xv[0:np_, s, :].rearrange("p (a b) -> p a b", a=8),
                                                   func=AF.Square, accum_out=ss[0:np_, s:s + 1]),
                      rd=[B_x[s]], wr=[B_ss[s]] + B_actT[8 * s:8 * s + 8])
                else:
                    A(lambda: nc.scalar.activation(out=nv[0:np_, s, :], in_=xv[0:np_, s, :], func=AF.Square,
                                                   accum_out=ss[0:np_, s:s + 1]),
                      rd=[B_x[s]], wr=[B_ss[s], B_n[s]])
            for s in range(nsub):
                rstd_s(s, np_, float(D))
            for s in range(nsub):
                V(lambda: nc.vector.tensor_scalar(out=nv[0:np_, s, :], in0=xv[0:np_, s, :], scalar1=rstd[0:np_, s:s + 1],
                                                  scalar2=None, op0=ALU.mult),
                  rd=[B_x[s], B_rs[s]], wr=[B_n[s]])
            for kc in range(NKC):
                ps, pb = K.bank()
                for s in range(nsub):
                    P(lambda: nc.tensor.transpose(out=ps[:, s * np_:(s + 1) * np_],
                                                  in_=nv[0:np_, s, kc * 128:(kc + 1) * 128],
                                                  identity=identf[0:np_, 0:np_]),
                      rd=[B_n[s]] + CONST, wr=[pb])
                AV(lambda: nc.scalar.mul(out=actT[:, kc, 0:ntok], in_=ps[:, 0:ntok], mul=gcol[:, kc:kc + 1]),
                   lambda: nc.vector.tensor_scalar(out=actT[:, kc, 0:ntok], in0=ps[:, 0:ntok],
                                                   scalar1=gcol[:, kc:kc + 1], scalar2=None, op0=ALU.mult),
                   rd=[pb] + CONST, wr=[B_actT[kc]])

        def fm_block(slab, sbuf, c0, ntok, ncol=128):
            ps, pb = K.bank()
            for kc in range(NKC):
                P(lambda: nc.tensor.matmul(ps[0:ncol, 0:ntok], lhsT=slab[:, kc, c0:c0 + ncol], rhs=actT[:, kc, 0:ntok],
                                           start=(kc == 0), stop=(kc == NKC - 1)),
                  rd=[sbuf, B_actT[kc]], wr=[pb])
            return ps, pb

        def tm_block(slab, sbuf, s, np_):
            ps, pb = K.bank()
            for kc in range(NKC):
                P(lambda: nc.tensor.matmul(ps[0:np_, 0:512], lhsT=actT[:, kc, s * np_:(s + 1) * np_],
                                           rhs=slab[:, kc, :], start=(kc == 0), stop=(kc == NKC - 1)),
                  rd=[sbuf, B_actT[kc]], wr=[pb])
            return ps, pb

        def copy_av(dst, src, rd, wr):
            AV(lambda: nc.scalar.copy(out=dst, in_=src),
               lambda: nc.vector.tensor_copy(out=dst, in_=src), rd=rd, wr=wr)

        def load_gpost(d_gp):
            K.dma(K.sp, c_gpost, gpost[:, :], d_gp[:, :], wr=[B_gpost])

        def evac_post(ps, pb, s, cb, np_, jk, jkb):
            A(lambda: nc.scalar.activation(out=jk[0:np_, 0:512], in_=ps[0:np_, :], func=AF.Square,
                                           accum_out=ssq[0:np_, s * 4 + cb:s * 4 + cb + 1]),
              rd=[pb], wr=[jkb, B_ssq[s]])
            V(lambda: nc.vector.tensor_tensor(out=nv[0:np_, s, cb * 512:(cb + 1) * 512], in0=ps[0:np_, :],
                                              in1=gpost[0:np_, cb * 512:(cb + 1) * 512], op=ALU.mult),
              rd=[pb, B_gpost], wr=[B_n[s]])

        def post_norm_residual(nsub, np_, into_n=False):
            for s in range(nsub):
                V(lambda: nc.vector.reduce_sum(out=ss[0:np_, s:s + 1], in_=ssq[0:np_, s * 4:(s + 1) * 4], axis=AX.X),
                  rd=[B_ssq[s]], wr=[B_ss[s]])
                rstd_s(s, np_, float(D))
            for s in range(nsub):
                dstv, dstb = (nv, B_n) if into_n else (xv, B_x)
                V(lambda: nc.vector.scalar_tensor_tensor(out=dstv[0:np_, s, :], in0=nv[0:np_, s, :],
                                                         scalar=rstd[0:np_, s:s + 1], in1=xv[0:np_, s, :],
                                                         op0=ALU.mult, op1=ALU.add),
                  rd=[B_n[s], B_rs[s], B_x[s]], wr=[dstb[s]])

        def attention_prompt(ti, extra=None):
            g0 = 2 * ti
            pending = []
            units_done = [0]

            def flush():
                (s, h, bsel, blks, pv) = pending.pop(0)
                pso, pbo = K.bank()
                for i, bk in enumerate(blks):
                    rp = bk % 6
                    P(lambda: nc.tensor.matmul(pso[:, 0:129], lhsT=pv[:, i * 128:(i + 1) * 128],
                                               rhs=vr[:, rp, h, 0:129], start=(i == 0), stop=(i == len(blks) - 1)),
                      rd=[B_pT[bsel], B_vr[rp][h // 4]], wr=[pbo])
                V(lambda: nc.vector.reciprocal(out=rinv[:, bsel:bsel + 1], in_=pso[:, 128:129]),
                  rd=[pbo], wr=[B_st["rinv"]])
                V(lambda: nc.vector.tensor_scalar(out=mix_in[:, s, h * 128:(h + 1) * 128], in0=pso[:, 0:128],
                                                  scalar1=rinv[:, bsel:bsel + 1], scalar2=None, op0=ALU.mult),
                  rd=[pbo, B_st["rinv"]], wr=[B_mi[s][h]])

            for s in range(2):
                g = g0 + s
                allb = [bk for bk in range(g - 4, g + 1) if bk >= 0]
                cst = [bk for bk in allb if 4 - (g - bk) in (1, 2)]
                var = [bk for bk in allb if 4 - (g - bk) not in (1, 2)]
                nvar, ncst = len(var), len(cst)
                blks = var + cst
                for h in range(8):
                    bsel = (s * 8 + h) % 2
                    psA, pbA = K.bank()
                    psB, pbB = K.bank()
                    for i, bk in enumerate(blks):
                        rp = bk % 6
                        if i < nvar:
                            dst, db = psB[:, i * 128:(i + 1) * 128], pbB
                        else:
                            dst, db = psA[:, (i - nvar) * 128:(i - nvar + 1) * 128], pbA
                        P(lambda: nc.tensor.matmul(dst, lhsT=kTr[:, h, rp * 128:(rp + 1) * 128],
                                                   rhs=qT[:, h, s * 128:(s + 1) * 128], start=True, stop=True),
                          rd=[B_kT[h][rp // 2], B_qT[h]], wr=[db])
                    scv = scb[bsel]
                    pv = pTb[bsel]
                    V(lambda: nc.vector.scalar_tensor_tensor(
                        out=scv[:, 0:nvar * 128], in0=psB[:, 0:nvar * 128], scalar=ATT_SCALE,
                        in1=biasT[:, h, (3 - nvar) * 128:384], op0=ALU.mult, op1=ALU.add),
                      rd=[pbB] + CONST, wr=[B_sc[bsel]])
                    if ncst:
                        A(lambda: nc.scalar.activation(out=pv[:, nvar * 128:(nvar + ncst) * 128],
                                                       in_=psA[:, 0:ncst * 128], func=AF.Exp,
                                                       bias=cbias[:, h:h + 1], scale=ATT_SCALE),
                          rd=[pbA] + CONST, wr=[B_pT[bsel]])
                    A(lambda: nc.scalar.activation(out=pv[:, 0:nvar * 128], in_=scv[:, 0:nvar * 128], func=AF.Exp),
                      rd=[B_sc[bsel]], wr=[B_pT[bsel]])
                    if pending:
                        flush()
                    pending.append((s, h, bsel, blks, pv))
                    units_done[0] += 1
                    if extra is not None:
                        extra(units_done[0])
            while pending:
                flush()

        def attention_sample():
            V(lambda: nc.vector.memset(pTs[:, :, :, :], 0.0), wr=B_pTs)
            V(lambda: nc.vector.memset(pTn[:, :, :], 0.0), wr=B_pTn)
            inherit(B_cst[2:4], B_kst + B_vst)
            ld = [0]
            pending = []

            def stage_load(src):
                cb_ = ld[0] % 4
                ld[0] += 1
                K.dma(K.sp, c_cst[cb_], CST[cb_], src, wr=[B_cst[cb_]])
                return cb_

            def flush():
                (i, h, bsel) = pending.pop(0)
                pso, pbo = K.bank()
                for bk in range(4):
                    P(lambda: nc.tensor.matmul(pso[0:64, 0:129], lhsT=pTs[:, i, bk, :], rhs=vr[:, bk, h, 0:129],
                                               start=(bk == 0), stop=False),
                      rd=[B_pTs[i], B_vr[bk][h // 4]], wr=[pbo])
                P(lambda: nc.tensor.matmul(pso[0:64, 0:129], lhsT=pTn[:, i, :], rhs=vr[0:64, 4, h, 0:129],
                                           start=False, stop=True),
                  rd=[B_pTn[i], B_vr[4][h // 4]], wr=[pbo])
                V(lambda: nc.vector.tensor_scalar(out=den[0:64, bsel:bsel + 1], in0=pso[0:64, 128:129],
                                                  scalar1=1e-30, scalar2=None, op0=ALU.max),
                  rd=[pbo], wr=[B_st["den"]])
                V(lambda: nc.vector.reciprocal(out=rinv[0:64, bsel:bsel + 1], in_=den[0:64, bsel:bsel + 1]),
                  rd=[B_st["den"]], wr=[B_st["rinv"]])
                if i == 0:
                    V(lambda: nc.vector.tensor_scalar(out=mix_in[0:64, 0, h * 128:(h + 1) * 128],
                                                      in0=pso[0:64, 0:128], scalar1=rinv[0:64, bsel:bsel + 1],
                                                      scalar2=None, op0=ALU.mult),
                      rd=[pbo, B_st["rinv"]], wr=[B_mi[0][h]])
                else:
                    V(lambda: nc.vector.scalar_tensor_tensor(out=mix_in[0:64, 0, h * 128:(h + 1) * 128],
                                                             in0=pso[0:64, 0:128], scalar=rinv[0:64, bsel:bsel + 1],
                                                             in1=mix_in[0:64, 0, h * 128:(h + 1) * 128],
                                                             op0=ALU.mult, op1=ALU.add),
                      rd=[pbo, B_st["rinv"], B_mi[0][h]], wr=[B_mi[0][h]])

            for i in range(4):
                for bk in range(4):
                    cb_ = stage_load(ck[i, bk * 128:(bk + 1) * 128, :])
                    for hq in range(2):
                        ps, pb = K.bank()
                        for hh in range(4):
                            h = hq * 4 + hh
                            P(lambda: nc.tensor.transpose(out=ps[:, hh * 128:(hh + 1) * 128],
                                                          in_=CST[cb_][:, h * 128:(h + 1) * 128],
                                                          identity=identf[:, :]),
                              rd=[B_cst[cb_]] + CONST, wr=[pb])
                        dst = kTr[:, hq * 4:hq * 4 + 4, bk * 128:(bk + 1) * 128]
                        src = ps[:, :].rearrange("p (h t) -> p h t", h=4)
                        copy_av(dst, src, [pb], [B_kT[hq * 4 + hh][bk // 2] for hh in range(4)])
                for bk in range(4):
                    cb_ = stage_load(cv[i, bk * 128:(bk + 1) * 128, :])
                    dst = vr[:, bk, :, 0:128]
                    src = CST[cb_].rearrange("p (h d) -> p h d", h=8)
                    copy_av(dst, src, [B_cst[cb_]], B_vr[bk])
                for h in range(8):
                    bsel = h % 2
                    psA, pbA = K.bank()
                    psB, pbB = K.bank()
                    for bk in range(4):
                        P(lambda: nc.tensor.matmul(psA[:, bk * 16:(bk + 1) * 16],
                                                   lhsT=kTr[:, h, bk * 128:(bk + 1) * 128],
                                                   rhs=qT[:, h, 16 * i:16 * i + 16], start=True, stop=True),
                          rd=[B_kT[h][bk // 2], B_qT[h]], wr=[pbA])
                    P(lambda: nc.tensor.matmul(psB[0:64, 0:16], lhsT=kTr[:, h, 512:576],
                                               rhs=qT[:, h, 16 * i:16 * i + 16], start=True, stop=True),
                      rd=[B_kT[h][2], B_qT[h]], wr=[pbB])
                    scv = scb[bsel]
                    V(lambda: nc.vector.scalar_tensor_tensor(
                        out=scv[:, 0:64].rearrange("p (b t) -> p b t", b=4),
                        in0=psA[:, 0:64].rearrange("p (b t) -> p b t", b=4), scalar=ATT_SCALE,
                        in1=biasS[:, h, :, :],
                        op0=ALU.mult, op1=ALU.add),
                      rd=[pbA] + CONST, wr=[B_sc[bsel]])
                    V(lambda: nc.vector.scalar_tensor_tensor(
                        out=scv[0:64, 64:80], in0=psB[0:64, 0:16], scalar=ATT_SCALE,
                        in1=biasN[:, h, 16 * i:16 * i + 16], op0=ALU.mult, op1=ALU.add),
                      rd=[pbB] + CONST, wr=[B_sc[bsel]])
                    if pending:
                        flush()
                    A(lambda: nc.scalar.activation(out=pTs[:, i, :, 16 * i:16 * i + 16],
                                                   in_=scv[:, 0:64].rearrange("p (b t) -> p b t", b=4), func=AF.Exp),
                      rd=[B_sc[bsel]], wr=[B_pTs[i]])
                    A(lambda: nc.scalar.activation(out=pTn[:, i, 16 * i:16 * i + 16], in_=scv[0:64, 64:80],
                                                   func=AF.Exp),
                      rd=[B_sc[bsel]], wr=[B_pTn[i]])
                    pending.append((i, h, bsel))
                while pending:
                    flush()
            inherit(B_kst + B_vst, B_cst[2:4])

        def gla_front(prompt, s, np_, par):
            k2, qtT, ktT = k2_[par], qtT_[par], ktT_[par]
            B_k2, B_qtT, B_ktT = B_k2_[par], B_qtT_[par], B_ktT_[par]
            eoff = par * 4 if prompt else 0
            nseq = 1 if prompt else 4
            if prompt:
                Umat = Ubf[:, :]
                Omat = onesbf[:, :]
                sel = onesbf[:, 0:1]
            else:
                Umat = smask_bf[0:64, 4:68]
                Omat = smask_bf[0:64, 68:132]
                sel = smask_bf[0:64, 0:4]
            ps, pb = K.bank()
            P(lambda: nc.tensor.matmul(ps[0:np_, 0:512], lhsT=aloT[0:16, s * np_:(s + 1) * np_], rhs=wg_bf[0:16, :],
                                       start=True, stop=True), rd=[B_alo] + CONST, wr=[pb])
            V(lambda: nc.vector.tensor_tensor(out=zt[0:np_, :], in0=ps[0:np_, 0:512], in1=brep[0:np_, :], op=ALU.add),
              rd=[pb] + CONST, wr=[B_zt])
            yield
            A(lambda: nc.scalar.activation(out=zt[0:np_, :], in_=zt[0:np_, :], func=AF.Exp, scale=-1.0),
              rd=[B_zt], wr=[B_zt])
            A(lambda: nc.scalar.activation(out=zt[0:np_, :], in_=zt[0:np_, :], func=AF.Ln, bias=oneT[0:np_, 0:1]),
              rd=[B_zt] + CONST, wr=[B_zt])
            yield
            V(lambda: nc.vector.tensor_copy(out=sp_hi[0:np_, :], in_=zt[0:np_, :]), rd=[B_zt], wr=[B_hi])
            V(lambda: nc.vector.tensor_tensor(out=sp_lo[0:np_, :], in0=zt[0:np_, :], in1=sp_hi[0:np_, :],
                                              op=ALU.subtract), rd=[B_zt, B_hi], wr=[B_lo])
            yield
            psc, pbc = K.bank()
            P(lambda: nc.tensor.matmul(psc[0:np_, 0:512], lhsT=Umat, rhs=sp_hi[0:np_, :], start=True, stop=False),
              rd=[B_hi] + CONST, wr=[pbc])
            P(lambda: nc.tensor.matmul(psc[0:np_, 0:512], lhsT=Umat, rhs=sp_lo[0:np_, :], start=False, stop=True),
              rd=[B_lo] + CONST, wr=[pbc])
            pst, pbt = K.bank()
            P(lambda: nc.tensor.matmul(pst[0:np_, 0:512], lhsT=Omat, rhs=sp_hi[0:np_, :], start=True, stop=False),
              rd=[B_hi] + CONST, wr=[pbt])
            P(lambda: nc.tensor.matmul(pst[0:np_, 0:512], lhsT=Omat, rhs=sp_lo[0:np_, :], start=False, stop=True),
              rd=[B_lo] + CONST, wr=[pbt])
            pse, pbe = K.bank()
            for h in range(4):
                P(lambda: nc.tensor.matmul(pse[:, h * nseq:(h + 1) * nseq], lhsT=sp_hi[0:np_, h * 128:(h + 1) * 128],
                                           rhs=sel, start=True, stop=False), rd=[B_hi] + CONST, wr=[pbe])
                P(lambda: nc.tensor.matmul(pse[:, h * nseq:(h + 1) * nseq], lhsT=sp_lo[0:np_, h * 128:(h + 1) * 128],
                                           rhs=sel, start=False, stop=True), rd=[B_lo] + CONST, wr=[pbe])
            yield
            A(lambda: nc.scalar.activation(out=E1[0:np_, :], in_=psc[0:np_, 0:512], func=AF.Exp, scale=-1.0 / 16.0),
              rd=[pbc], wr=[B_E1])
            A(lambda: nc.scalar.activation(out=Einv[0:np_, :], in_=psc[0:np_, 0:512], func=AF.Exp, scale=1.0 / 16.0),
              rd=[pbc], wr=[B_Einv])
            A(lambda: nc.scalar.activation(out=ECL[0:np_, :], in_=pst[0:np_, 0:512], func=AF.Exp, scale=-1.0 / 16.0),
              rd=[pbt], wr=[B_ECL])
            A(lambda: nc.scalar.activation(out=ecl[:, eoff:eoff + 4 * nseq], in_=pse[:, 0:4 * nseq], func=AF.Exp,
                                           scale=-1.0 / 16.0), rd=[pbe], wr=[B_ecl_[par]])
            yield
            V(lambda: nc.vector.tensor_tensor(out=E1[0:np_, :], in0=E1[0:np_, :], in1=qg[0:np_, s, :], op=ALU.mult),
              rd=[B_E1, B_qg[s]], wr=[B_E1])
            V(lambda: nc.vector.tensor_tensor(out=Einv[0:np_, :], in0=Einv[0:np_, :], in1=kg[0:np_, s, :], op=ALU.mult),
              rd=[B_Einv, B_kg[s]], wr=[B_Einv])
            V(lambda: nc.vector.tensor_tensor(out=k2[0:np_, :], in0=Einv[0:np_, :], in1=ECL[0:np_, :], op=ALU.mult),
              rd=[B_Einv, B_ECL], wr=[B_k2])
            yield
            psq, pbq = K.bank()
            psk, pbk = K.bank()
            for h in range(4):
                P(lambda: nc.tensor.transpose(out=psq[:, h * np_:(h + 1) * np_], in_=E1[0:np_, h * 128:(h + 1) * 128],
                                              identity=identf[0:np_, 0:np_]), rd=[B_E1] + CONST, wr=[pbq])
                P(lambda: nc.tensor.transpose(out=psk[:, h * np_:(h + 1) * np_], in_=Einv[0:np_, h * 128:(h + 1) * 128],
                                              identity=identf[0:np_, 0:np_]), rd=[B_Einv] + CONST, wr=[pbk])
            A(lambda: nc.scalar.copy(out=qtT[:, :, 0:np_], in_=psq[:, 0:4 * np_].rearrange("p (h t) -> p h t", h=4)),
              rd=[pbq], wr=[B_qtT])
            V(lambda: nc.vector.tensor_copy(out=ktT[:, :, 0:np_],
                                            in_=psk[:, 0:4 * np_].rearrange("p (h t) -> p h t", h=4)),
              rd=[pbk], wr=[B_ktT])
            yield
            if not prompt:
                V(lambda: nc.vector.memset(qtTm[:, :, :, :], 0.0), wr=[B_qtTm])
                for i in range(4):
                    V(lambda: nc.vector.tensor_copy(out=qtTm[:, i, :, 16 * i:16 * i + 16],
                                                    in_=qtT[:, :, 16 * i:16 * i + 16]),
                      rd=[B_qtT], wr=[B_qtTm])
                    V(lambda: nc.vector.tensor_scalar(out=k2m[i // 2][:, i % 2, :], in0=k2[0:64, :],
                                                      scalar1=smask[0:64, i:i + 1], scalar2=None, op0=ALU.mult),
                      rd=[B_k2] + CONST, wr=[B_k2m[i // 2]])
            yield

        def gla_back(prompt, s, np_, par, hook=None):
            nseq = 1 if prompt else 4
            Umat = Ubf[:, :] if prompt else smask_bf[0:64, 4:68]
            k2, qtT, ktT = k2_[par], qtT_[par], ktT_[par]
            B_k2, B_qtT, B_ktT = B_k2_[par], B_qtT_[par], B_ktT_[par]
            eoff = par * 4 if prompt else 0
            bankA = []
            for h in range(4):
                psa, pba = K.bank()
                bankA.append((psa, pba))
                P(lambda: nc.tensor.matmul(psa[0:np_, 0:np_], lhsT=ktT[:, h, 0:np_], rhs=qtT[:, h, 0:np_],
                                           start=True, stop=True), rd=[B_ktT, B_qtT], wr=[pba])
            for h in range(4):
                psa, pba = bankA[h]
                V(lambda: nc.vector.tensor_tensor(out=ATm[0:np_, h, 0:np_], in0=psa[0:np_, 0:np_], in1=Umat,
                                                  op=ALU.mult), rd=[pba] + CONST, wr=[B_ATm[h]])
            if hook is not None:
                hook()
            bankO = []
            for h in range(4):
                pso, pbo = K.bank()
                bankO.append((pso, pbo))
                P(lambda: nc.tensor.matmul(pso[0:np_, 0:256], lhsT=ATm[0:np_, h, 0:np_],
                                           rhs=vg[0:np_, s, h * 256:(h + 1) * 256], start=True, stop=False),
                  rd=[B_ATm[h], B_vg[s][h // 2]], wr=[pbo])
                if prompt:
                    P(lambda: nc.tensor.matmul(pso[0:np_, 0:256], lhsT=qtT[:, h, 0:np_], rhs=Sbf[:, h, :],
                                               start=False, stop=True), rd=[B_qtT, B_Sbf[h]], wr=[pbo])
                else:
                    for i in range(4):
                        P(lambda: nc.tensor.matmul(pso[0:np_, 0:256], lhsT=qtTm[:, i, h, :], rhs=Sbf4[:, i, h, :],
                                                   start=False, stop=(i == 3)), rd=[B_qtTm, B_Sbf4[i]], wr=[pbo])
            for h in range(4):
                pso, pbo = bankO[h]
                V(lambda: nc.vector.tensor_copy(out=o_sb[0:np_, h, :], in_=pso[0:np_, 0:256]), rd=[pbo], wr=[B_osb[h]])
                A(lambda: nc.scalar.activation(out=junk[0:np_, 0:256], in_=o_sb[0:np_, h, :], func=AF.Square,
                                               accum_out=oss[0:np_, h:h + 1]), rd=[B_osb[h]], wr=[B_junk, B_st["oss"]])
            if hook is not None:
                hook()
            if prompt:
                bankU = []
                for h in range(4):
                    psu, pbu = K.bank()
                    bankU.append((psu, pbu))
                    P(lambda: nc.tensor.matmul(psu[:, 0:256], lhsT=k2[0:np_, h * 128:(h + 1) * 128],
                                               rhs=vg[0:np_, s, h * 256:(h + 1) * 256], start=True, stop=True),
                      rd=[B_k2, B_vg[s][h // 2]], wr=[pbu])
                for h in range(4):
                    psu, pbu = bankU[h]
                    V(lambda: nc.vector.scalar_tensor_tensor(out=Sst[:, h, :], in0=Sst[:, h, :],
                                                             scalar=ecl[:, eoff + h:eoff + h + 1],
                                                             in1=psu[:, 0:256], op0=ALU.mult, op1=ALU.add),
                      rd=[B_S[h], B_ecl_[par], pbu], wr=[B_S[h]])
                    A(lambda: nc.scalar.copy(out=Sbf[:, h, :], in_=Sst[:, h, :]), rd=[B_S[h]], wr=[B_Sbf[h]])
            if hook is not None:
                hook()
                hook()
            V(lambda: nc.vector.tensor_scalar(out=oms[0:np_, :], in0=oss[0:np_, :], scalar1=1.0 / 256.0, scalar2=EPS,
                                              op0=ALU.mult, op1=ALU.add), rd=[B_st["oss"]], wr=[B_st["oms"]])
            if pool_ok[0]:
                G(lambda: nc.gpsimd.tensor_tensor(out=orstd[0:np_, :], in0=oms[0:np_, :], in1=mhalf[0:np_, 0:4],
                                                  op=ALU.pow), rd=[B_st["oms"]] + CONST, wr=[B_st["orstd"]])
            else:
                A(lambda: nc.scalar.activation(out=osd[0:np_, :], in_=oms[0:np_, :], func=AF.Sqrt),
                  rd=[B_st["oms"]], wr=[B_st["osd"]])
                V(lambda: nc.vector.reciprocal(out=orstd[0:np_, :], in_=osd[0:np_, :]), rd=[B_st["osd"]],
                  wr=[B_st["orstd"]])
            for h in range(4):
                mb = [B_mi[s][8 + 2 * h], B_mi[s][9 + 2 * h]]
                msl = mix_in[0:np_, s, 1024 + h * 256:1024 + (h + 1) * 256]
                tq = junk[0:np_, :].bitcast(F32)
                A(lambda: nc.scalar.activation(out=tq, in_=msl, func=AF.Tanh, scale=0.5), rd=mb, wr=[B_junk])
                V(lambda: nc.vector.scalar_tensor_tensor(out=msl, in0=tq, scalar=1.0, in1=msl,
                                                         op0=ALU.add, op1=ALU.mult),
                  rd=[B_junk] + mb, wr=mb)
                V(lambda: nc.vector.scalar_tensor_tensor(out=o_sb[0:np_, h, :], in0=o_sb[0:np_, h, :],
                                                         scalar=orstd[0:np_, h:h + 1], in1=ggla[0:np_, :],
                                                         op0=ALU.mult, op1=ALU.mult),
                  rd=[B_osb[h], B_st["orstd"]] + CONST, wr=[B_osb[h]])
                V(lambda: nc.vector.scalar_tensor_tensor(out=msl, in0=msl, scalar=0.5, in1=o_sb[0:np_, h, :],
                                                         op0=ALU.mult, op1=ALU.mult),
                  rd=[B_osb[h]] + mb, wr=mb)

        def gla_sample_state_update():
            for i in range(4):
                K.dma(K.sp, c_Sin, Sst[:, :, :], sg[i].rearrange("h d v -> d h v"), wr=B_S)
                for h in range(4):
                    psu, pbu = K.bank()
                    P(lambda: nc.tensor.matmul(psu[:, 0:256], lhsT=k2m[i // 2][:, i % 2, h * 128:(h + 1) * 128],
                                               rhs=vg[0:64, 0, h * 256:(h + 1) * 256], start=True, stop=True),
                      rd=[B_k2m[i // 2], B_vg[0][h // 2]], wr=[pbu])
                    V(lambda: nc.vector.scalar_tensor_tensor(out=Sst[:, h, :], in0=Sst[:, h, :],
                                                             scalar=ecl[:, h * 4 + i:h * 4 + i + 1],
                                                             in1=psu[:, 0:256], op0=ALU.mult, op1=ALU.add),
                      rd=[B_S[h], B_ecl_[0], pbu], wr=[B_S[h]])
                K.dma(K.act, c_S, gs[i].rearrange("h d v -> d h v"), Sst[:, :, :], rd=B_S)

        def run_tile(kind, b, ti):
            prompt = (kind == "p")
            nsub, np_ = (2, 128) if prompt else (1, 64)
            ntok = nsub * np_
            rslot = ti % 3
            kcol0 = rslot * 256
            emit_kv = (prompt and ti >= NTILE - 2) or (not prompt)

            def dbg_dump(ph):
                for s in range(nsub):
                    dst = yp[b, ti * TT + s * 128: ti * TT + (s + 1) * 128, :] if prompt else ys[:, :]
                    if ph < 4:
                        K.dma(K.act, c_y[s], dst, mix_in[0:np_, s, :], rd=WA)
                    else:
                        K.dma(K.act, c_y[s], dst, xv[0:np_, s, :], rd=[B_x[s]])

            for s in range(nsub):
                src = xp[b, ti * TT + s * 128: ti * TT + (s + 1) * 128, :] if prompt else xs[:, :]
                K.dma(K.sp, c_x[s], xv[0:np_, s, :], src, wr=[B_x[s]])
            state["limit"] += SLABS_PER_TILE
            pump(state["cur"])
            norm_transpose(nsub, np_, gpre1, chunked=True)
            inherit(XB + B_k2m + B_pTs, XA)
            inherit(NB, NA)
            inherit(WA + B_cst + B_Sbf4, WB)

            if dbg_stop == 0:
                return dbg_dump(0)
            kv_i = 0
            for j in range(4):
                slab, sbuf = next_slab()
                for hb in range(4):
                    h = (j % 2) * 4 + hb
                    ps, pb = fm_block(slab, sbuf, hb * 128, ntok)
                    if j < 2:
                        dst, dbuf = qT[:, h, 0:ntok], B_qT[h]
                    elif prompt:
                        dst, dbuf = kTr[:, h, kcol0:kcol0 + ntok], B_kT[h][rslot]
                    else:
                        dst, dbuf = kTr[:, h, 512:512 + ntok], B_kT[h][2]
                    copy_av(dst, ps[:, 0:ntok], [pb], [dbuf])
                if j >= 2 and emit_kv:
                    for s in range(nsub):
                        ps, pb = tm_block(slab, sbuf, s, np_)
                        bs = kv_i % 2
                        kv_i += 1
                        copy_av(kstb[bs][0:np_, :], ps[0:np_, :], [pb], [B_kst[bs]])
                        c0 = (j - 2) * 512
                        if prompt:
                            r0 = (ti - (NTILE - 2)) * TT + s * 128
                            dst = kp[b, r0:r0 + 128, c0:c0 + 512]
                        else:
                            dst = ksn[:, c0:c0 + 512]
                        K.dma(K.act, c_kst[bs], dst, kstb[bs][0:np_, :], rd=[B_kst[bs]])
            for j in range(2):
                slab, sbuf = next_slab()
                for s in range(nsub):
                    ps, pb = tm_block(slab, sbuf, s, np_)
                    blk = (2 * ti + s) % 6 if prompt else 4
                    dst = vr[0:np_, blk, 4 * j:4 * j + 4, 0:128]
                    src = ps[0:np_, :].rearrange("p (h d) -> p h d", h=4)
                    copy_av(dst, src, [pb], [B_vr[blk][j]])
                    if emit_kv:
                        bs = kv_i % 2
                        kv_i += 1
                        copy_av(vstb[bs][0:np_, :], ps[0:np_, :], [pb], [B_vst[bs]])
                        c0 = j * 512
                        if prompt:
                            r0 = (ti - (NTILE - 2)) * TT + s * 128
                            dst2 = vp[b, r0:r0 + 128, c0:c0 + 512]
                        else:
                            dst2 = vsn[:, c0:c0 + 512]
                        K.dma(K.act, c_vst[bs], dst2, vstb[bs][0:np_, :], rd=[B_vst[bs]])
            for j in range(2):
                slab, sbuf = next_slab()
                for s in range(nsub):
                    ps, pb = tm_block(slab, sbuf, s, np_)
                    if j == 0:
                        A(lambda: nc.scalar.mul(out=qg[0:np_, s, :], in_=ps[0:np_, :], mul=GLA_QSCALE),
                          rd=[pb], wr=[B_qg[s]])
                    else:
                        V(lambda: nc.vector.tensor_copy(out=kg[0:np_, s, :], in_=ps[0:np_, :]), rd=[pb], wr=[B_kg[s]])
            for j in range(2):
                slab, sbuf = next_slab()
                for s in range(nsub):
                    ps, pb = tm_block(slab, sbuf, s, np_)
                    copy_av(vg[0:np_, s, j * 512:(j + 1) * 512], ps[0:np_, :], [pb], [B_vg[s][j]])
            ps, pb = K.bank()
            for kc in range(NKC):
                P(lambda: nc.tensor.matmul(ps[0:16, 0:ntok], lhsT=wlo[:, kc, :], rhs=actT[:, kc, 0:ntok],
                                           start=(kc == 0), stop=(kc == NKC - 1)),
                  rd=[B_actT[kc]] + CONST, wr=[pb])
            V(lambda: nc.vector.tensor_copy(out=aloT[0:16, 0:ntok], in_=ps[0:16, 0:ntok]), rd=[pb], wr=[B_alo])
            load_gpost(d_gpost1)

            if dbg_stop == 1:
                return dbg_dump(1)
            rg = {}
            RGS = [(0, 0), (0, 1), (1, 0), (1, 1)]

            def rg_piece(blk, piece, npieces):
                j, s = RGS[blk]
                if piece == 0:
                    if s == 0:
                        rg["slab"] = next_slab()
                    rg["bank"] = K.pin_bank()
                slab, sbuf = rg["slab"]
                bi, ps, pb = rg["bank"]
                per = NKC // npieces
                for kc in range(piece * per, (piece + 1) * per):
                    P(lambda: nc.tensor.matmul(ps[0:np_, 0:512], lhsT=actT[:, kc, s * np_:(s + 1) * np_],
                                               rhs=slab[:, kc, :], start=(kc == 0), stop=(kc == NKC - 1)),
                      rd=[sbuf, B_actT[kc]], wr=[pb])
                if piece == npieces - 1:
                    copy_av(mix_in[0:np_, s, 1024 + j * 512:1024 + (j + 1) * 512], ps[0:np_, :], [pb],
                            B_mi[s][8 + 4 * j:12 + 4 * j])
                    K.unpin_bank(bi)

            def mix_transposes(kcs):
                for kc in kcs:
                    ps, pb = K.bank()
                    for s in range(nsub):
                        P(lambda: nc.tensor.transpose(out=ps[:, s * np_:(s + 1) * np_],
                                                      in_=mix_in[0:np_, s, kc * 128:(kc + 1) * 128],
                                                      identity=identf[0:np_, 0:np_]),
                          rd=[B_mi[s][kc]] + CONST, wr=[pb])
                    copy_av(actT[:, kc, 0:ntok], ps[:, 0:ntok], [pb], [B_actT[kc]])

            if prompt:
                if ti == 0:
                    V(lambda: nc.vector.memset(Sst[:, :, :], 0.0), wr=B_S)
                    V(lambda: nc.vector.memset(Sbf[:, :, :], 0.0), wr=B_Sbf)
                g0 = gla_front(True, 0, np_, 0)

                def extra(u):
                    rg_piece((u - 1) // 4, (u - 1) % 4, 4)
                    next(g0, None)

                attention_prompt(ti, extra=extra)
                if dbg_stop == 2:
                    return dbg_dump(2)
                for _ in g0:
                    pass
                mix_transposes(range(0, 8))
                g1 = gla_front(True, 1, np_, 1)

                def hook():
                    next(g1, None)
                    next(g1, None)

                gla_back(True, 0, np_, 0, hook=hook)
                for _ in g1:
                    pass
                gla_back(True, 1, np_, 1)
                if ti == NTILE - 1:
                    K.dma(K.act, c_S, gp[b].rearrange("h d v -> d h v"), Sst[:, :, :], rd=B_S)
            else:
                rg_piece(0, 0, 1)
                rg_piece(2, 0, 1)
                attention_sample()
                inherit(B_Sbf4, B_cst)
                for i in range(4):
                    K.dma(K.sp, c_Sin, Sst[:, :, :], sg[i].rearrange("h d v -> d h v"), wr=B_S)
                    V(lambda: nc.vector.tensor_copy(out=Sbf4[:, i, :, :], in_=Sst[:, :, :]), rd=B_S, wr=[B_Sbf4[i]])
                for _ in gla_front(False, 0, np_, 0):
                    pass
                gla_back(False, 0, np_, 0)
                gla_sample_state_update()
            if dbg_stop == 3.5:
                return dbg_dump(3.5)
            mix_transposes(range(8, NKC) if prompt else range(NKC))
            inherit(XA, XB + B_k2m + B_pTs)
            inherit(NA, NB)
            for s in range(nsub):
                src = xp[b, ti * TT + s * 128: ti * TT + (s + 1) * 128, :] if prompt else xs[:, :]
                K.dma(K.act, c_x2[s], xv[0:np_, s, :], src, wr=[B_x[s]])
            for cb in range(4):
                slab, sbuf = next_slab()
                for s in range(nsub):
                    ps, pb = tm_block(slab, sbuf, s, np_)
                    evac_post(ps, pb, s, cb, np_, junk, B_junk)
            post_norm_residual(nsub, np_)

            if dbg_stop == 4:
                return dbg_dump(4)
            norm_transpose(nsub, np_, gpre2)
            inherit(WB, WA + B_cst + B_Sbf4)
            inherit(NC_, NA)

            if dbg_stop == 5:
                return dbg_dump(5)
            load_gpost(d_gpost2)
            nseq, L = (1, 256) if prompt else (4, 16)
            if prompt and ti == 0:
                V(lambda: nc.vector.memset(carry[:, :, :], 0.0), wr=B_carry)
            if not prompt:
                for c in range(11):
                    bs = c % 2
                    K.dma(K.sp, c_cvin[bs], cvb[bs], sc[:, c * 512:(c + 1) * 512], wr=[B_cvb[bs]])
                    ps, pb = K.bank()
                    for q in range(4):
                        P(lambda: nc.tensor.transpose(out=ps[:, q * 8:(q + 1) * 8], in_=cvb[bs][:, q * 128:(q + 1) * 128],
                                                      identity=identf[0:8, 0:8]), rd=[B_cvb[bs]] + CONST, wr=[pb])
                    V(lambda: nc.vector.tensor_copy(out=carry[:, 4 * c:4 * c + 4, :],
                                                    in_=ps[:, 0:32].rearrange("p (q r) -> p q r", q=4)),
                      rd=[pb], wr=B_carry[4 * c:4 * c + 4])
            for sl in range(22):
                gslab, gsbuf = next_slab()
                for fb in range(2):
                    j = sl * 2 + fb
                    bs = j % 2
                    psg, pbg = fm_block(gslab, gsbuf, fb * 128, ntok)
                    psv, pbv = fm_block(gslab, gsbuf, 256 + fb * 128, ntok)
                    gsv = gsb[bs][:, 0:nseq * (L + 2)].rearrange("p (i t) -> p i t", i=nseq)
                    ccv = ccb[bs][:, 0:ntok].rearrange("p (i t) -> p i t", i=nseq)
                    sgv = sgb[bs][:, 0:ntok].rearrange("p (i t) -> p i t", i=nseq)
                    V(lambda: nc.vector.tensor_copy(out=gsv[:, :, 0:2],
                                                    in_=carry[:, j, 0:2 * nseq].rearrange("p (i r) -> p i r", i=nseq)),
                      rd=[B_carry[j]], wr=[B_gs[bs]])
                    A(lambda: nc.scalar.copy(out=gsv[:, :, 2:L + 2],
                                             in_=psg[:, 0:ntok].rearrange("p (i t) -> p i t", i=nseq)),
                      rd=[pbg], wr=[B_gs[bs]])
                    A(lambda: nc.scalar.activation(out=ccv, in_=psg[:, 0:ntok].rearrange("p (i t) -> p i t", i=nseq),
                                                   func=AF.Identity, scale=convw[:, j, 2:3], bias=convw[:, j, 3:4]),
                      rd=[pbg] + CONST, wr=[B_cc[bs]])
                    V(lambda: nc.vector.tensor_copy(out=carry[:, j, 0:2 * nseq].rearrange("p (i r) -> p i r", i=nseq),
                                                    in_=gsv[:, :, L:L + 2]),
                      rd=[B_gs[bs]], wr=[B_carry[j]])
                    V(lambda: nc.vector.scalar_tensor_tensor(out=ccv, in0=gsv[:, :, 1:L + 1], scalar=convw[:, j, 1:2],
                                                             in1=ccv, op0=ALU.mult, op1=ALU.add),
                      rd=[B_gs[bs], B_cc[bs]] + CONST, wr=[B_cc[bs]])
                    V(lambda: nc.vector.scalar_tensor_tensor(out=ccv, in0=gsv[:, :, 0:L], scalar=convw[:, j, 0:1],
                                                             in1=ccv, op0=ALU.mult, op1=ALU.add),
                      rd=[B_gs[bs], B_cc[bs]] + CONST, wr=[B_cc[bs]])
                    A(lambda: nc.scalar.activation(out=sgv, in_=ccv, func=AF.Silu), rd=[B_cc[bs]], wr=[B_sg[bs]])
                    V(lambda: nc.vector.tensor_tensor(out=actb[:, j, 0:ntok], in0=psv[:, 0:ntok], in1=sgb[bs][:, 0:ntok],
                                                      op=ALU.mult), rd=[pbv, B_sg[bs]], wr=[B_act[j]])
            pin_tables()
            if (prompt and ti == NTILE - 1) or not prompt:
                nr = 2 * nseq
                for c in range(11):
                    bs = c % 2
                    ps, pb = K.bank()
                    for q in range(4):
                        P(lambda: nc.tensor.transpose(out=ps[0:nr, q * 128:(q + 1) * 128], in_=carry[:, 4 * c + q, 0:nr],
                                                      identity=identf[:, :]), rd=[B_carry[4 * c + q]] + CONST, wr=[pb])
                    V(lambda: nc.vector.tensor_copy(out=cvb[bs][0:nr, :], in_=ps[0:nr, 0:512]), rd=[pb], wr=[B_cvb[bs]])
                    dst = cp[2 * b:2 * b + 2, c * 512:(c + 1) * 512] if prompt else cs[:, c * 512:(c + 1) * 512]
                    K.dma(K.act, c_cvb[bs], dst, cvb[bs][0:nr, :], rd=[B_cvb[bs]])
            inherit(NA, NC_)

            if dbg_stop == 6:
                return dbg_dump(6)
            for cb in range(4):
                banks = [K.bank() for s in range(nsub)]
                for kgi in range(3):
                    nk = 16 if kgi < 2 else 12
                    slab, sbuf = next_slab()
                    for s in range(nsub):
                        ps, pb = banks[s]
                        for kc in range(nk):
                            fc = kgi * 16 + kc
                            P(lambda: nc.tensor.matmul(ps[0:np_, 0:512], lhsT=actb[:, fc, s * np_:(s + 1) * np_],
                                                       rhs=slab[:, kc, :], start=(fc == 0), stop=(fc == NFC - 1)),
                              rd=[sbuf, B_act[fc]], wr=[pb])
                for s in range(nsub):
                    ps, pb = banks[s]
                    evac_post(ps, pb, s, cb, np_, junk_b, B_junkb)
            post_norm_residual(nsub, np_, into_n=True)
            for s in range(nsub):
                dst = yp[b, ti * TT + s * 128: ti * TT + (s + 1) * 128, :] if prompt else ys[:, :]
                K.dma(K.act, c_y[s], dst, nv[0:np_, s, :], rd=[B_n[s]])

        junk_b = sb("junk_b", [128, 512], BF16)
        B_junkb = Buf("junkb")

        pin_tables()
        if with_sample:
            run_tile("s", 0, 0)
            pool_ok[0] = True
        for b in range(2):
            for ti in range(NTILE):
                if dbg_tiles is not None and (b, ti) not in dbg_tiles:
                    continue
                run_tile("p", b, ti)
                pool_ok[0] = True

        for c in K.out_chans:
            if c.cnt:
                nc.sync.wait_ge(c.sem, c.cnt)
    return nc


def _host_tables(rel_bias):
    rb = np.asarray(rel_bias, np.float32)[0]
    kl = np.arange(128)[:, None, None]
    blk = np.arange(5)[None, :, None]
    q = np.arange(128)[None, None, :]
    d = 512 - 128 * blk + q - kl
    idx = np.clip(d, -128, 128) + 128
    tab = rb[:, idx]
    tab = np.ascontiguousarray(np.transpose(tab, (1, 0, 2, 3))).copy()
    maskA = (q >= 64) & (kl < 64)
    maskB = (q < 64) & (kl >= 64)
    tab[:, :, 0, :][np.broadcast_to(maskA[:, 0, :][:, None, :], (128, 8, 128))] = MASKV
    tab[:, :, 4, :][np.broadcast_to(maskB[:, 0, :][:, None, :], (128, 8, 128))] = MASKV
    biasS = np.ascontiguousarray(tab[:, :, 0:4, 0:16]).reshape(128, 8 * 64)
    tab = np.ascontiguousarray(tab[:, :, [0, 3, 4, 1, 2], :])
    biasT = tab.reshape(128, 8 * 640)
    kk = np.arange(64)[:, None]
    qq = np.arange(64)[None, :]
    dn = (qq % 16) - (kk % 16)
    tn = rb[:, dn + 128]
    tn = np.ascontiguousarray(np.transpose(tn, (1, 0, 2))).copy()
    cross = (kk // 16) != (qq // 16)
    tn[np.broadcast_to(cross[:, None, :], (64, 8, 64))] = MASKV
    biasN = tn.reshape(64, 8 * 64)
    t = np.arange(64)
    seqsel = (t[:, None] // 16 == np.arange(4)[None, :]).astype(np.float32)
    same = (t[:, None] // 16 == t[None, :] // 16)
    Ub = (same & (t[:, None] <= t[None, :])).astype(np.float32)
    onesb = same.astype(np.float32)
    smask = np.concatenate([seqsel, Ub, onesb], axis=1).astype(np.float32)
    return biasT.astype(np.float32), biasN.astype(np.float32), smask, biasS.astype(np.float32)


_NC_CACHE = {}


def kernel(x_prompt, x_sample, cache_k, cache_v, state_gla, state_conv, g_mix_pre, w_in, w_gate_up,
           b_gate, rel_bias, g_gla, w_o, g_mix_post, g_ffn_pre, w_up, w_conv, b_conv, w_down, g_ffn_post):
    f = lambda a: np.ascontiguousarray(np.asarray(a, dtype=np.float32))
    x_prompt, x_sample = f(x_prompt), f(x_sample)
    cache_k, cache_v = f(cache_k)[0], f(cache_v)[0]
    state_gla, state_conv = f(state_gla)[0], f(state_conv)[0]
    biasT, biasN, smask, biasS = _host_tables(rel_bias)
    colmajor = lambda g: np.ascontiguousarray(f(g)[0].reshape(16, 128).T)
    rep = lambda v, n=128: np.ascontiguousarray(np.broadcast_to(f(v).reshape(1, -1), (n, f(v).size)))
    wc = f(w_conv)[0]
    bc = f(b_conv)[0]
    convw = np.stack([wc[0], wc[1], wc[2], bc], axis=-1)
    convw = np.ascontiguousarray(convw.reshape(NFC, 128, 4).transpose(1, 0, 2).reshape(128, NFC * 4))
    shared = {
        "w_in": f(w_in)[0], "w_o": f(w_o)[0], "w_up": f(w_up)[0], "w_down": f(w_down)[0],
        "gpre1": colmajor(g_mix_pre), "gpre2": colmajor(g_ffn_pre),
        "gpost1": rep(g_mix_post), "gpost2": rep(g_ffn_post),
        "brep": rep(b_gate), "ggla": rep(g_gla), "wg": f(w_gate_up)[0],
        "convw": convw, "biasT": biasT, "biasN": biasN, "smask": smask, "biasS": biasS,
        "cbias": np.ascontiguousarray(np.broadcast_to(f(rel_bias)[0][:, 256].reshape(1, 8), (128, 8))),
    }
    in_maps = []
    for c in range(NCORES):
        m = dict(shared)
        m["xp"] = x_prompt[2 * c:2 * c + 2]
        m["xs"] = np.ascontiguousarray(x_sample[4 * c:4 * c + 4].reshape(64, D))
        m["ck"] = np.ascontiguousarray(cache_k[4 * c:4 * c + 4].reshape(4, 512, 1024))
        m["cv"] = np.ascontiguousarray(cache_v[4 * c:4 * c + 4].reshape(4, 512, 1024))
        m["sg"] = state_gla[4 * c:4 * c + 4]
        m["sc"] = np.ascontiguousarray(state_conv[4 * c:4 * c + 4].reshape(8, DFF))
        in_maps.append(m)
    if "nc" not in _NC_CACHE:
        _NC_CACHE["nc"] = build_program()
    nc = _NC_CACHE["nc"]
    res = run_bass_kernel_spmd(nc, in_maps, core_ids=list(range(NCORES)))
    R = res.results
    cat = lambda k: np.concatenate([np.asarray(r[k]) for r in R], axis=0)
    y_prompt = cat("yp").reshape(16, SEQ, D)
    y_sample = cat("ys").reshape(32, 16, D)
    k_prompt = cat("kp").reshape(1, 16, 512, 8, 128)
    v_prompt = cat("vp").reshape(1, 16, 512, 8, 128)
    gla_prompt = cat("gp").reshape(1, 16, 4, 128, 256)
    conv_prompt = cat("cp").reshape(1, 16, 2, DFF)
    k_sample = cat("ksn").reshape(1, 32, 16, 8, 128)
    v_sample = cat("vsn").reshape(1, 32, 16, 8, 128)
    gla_sample = cat("gs").reshape(1, 32, 4, 128, 256)
    conv_sample = cat("cs").reshape(1, 32, 2, DFF)
    return (y_prompt.astype(np.float32), y_sample.astype(np.float32), k_prompt, v_prompt, gla_prompt,
            conv_prompt, k_sample, v_sample, gla_sample, conv_sample)
```
